# Optimizing a Trainium2 kernel written in Bass

```python
import math
import jax, jax.numpy as jnp
from jax import lax
import numpy as np

D_MODEL = 1024
BATCH = 8
SEQ = 2048
DEPTH = 1
DEC_BATCH = 128
DEC_SEQ = 1
PAST_LEN = 16384
PAGE_SIZE = 128

D_MIX = 2 * D_MODEL
W_A = D_MIX // 2
H_A = 8
DH_A = W_A // H_A
W_B = D_MIX - W_A
H_B = 4
DH_B = W_B // H_B
CHUNK = 128
CONV_W = 4
D_PLE = 256
N_IN = 3 * W_A + 3 * W_B
EPS = 1e-6

kernel_name = 'hymba_gmlp_mlstm_decode_step'


def rmsnorm(x, g):
    xf = x.astype(jnp.float32)
    y = xf * lax.rsqrt(jnp.mean(xf * xf, axis=-1, keepdims=True) + EPS)
    return (y * g.astype(jnp.float32)).astype(x.dtype)


def head_norm(x, g, b=None):
    xf = x.astype(jnp.float32)
    mu = jnp.mean(xf, axis=-1, keepdims=True)
    var = jnp.mean(jnp.square(xf - mu), axis=-1, keepdims=True)
    y = (xf - mu) * lax.rsqrt(var + EPS) * g.astype(jnp.float32)
    if b is not None:
        y = y + b.astype(jnp.float32)
    return y


def causal_conv(x, buf, w, b):
    T = x.shape[1]
    xpad = jnp.concatenate([buf.astype(x.dtype), x], axis=1)
    y = xpad[:, 0:T] * w[0]
    for j in range(1, CONV_W):
        y = y + xpad[:, j:j + T] * w[j]
    return y + b, xpad[:, -(CONV_W - 1):]


def chunk_spatial(v, w_s, b_s):
    B, T = v.shape[0], v.shape[1]
    L = min(CHUNK, T)
    nc = T // L
    w = w_s[:, :L, :L] * jnp.tril(jnp.ones((L, L), w_s.dtype))
    vc = v.reshape(B, nc, L, H_A, DH_A)
    out = jnp.einsum('hts,bcshd->bcthd', w, vc) + b_s[:, :L].T[None, None, :, :, None]
    return out.reshape(B, T, H_A, DH_A)


def mlstm_chunk(carry, inp):
    C, n, m = carry
    q, k, v, li, lf = inp
    L = q.shape[1]
    b = jnp.cumsum(lf, axis=1)
    causal = jnp.tril(jnp.ones((L, L), bool))
    dmat = b[:, :, None, :] - b[:, None, :, :] + li[:, None, :, :]
    dmat = jnp.where(causal[None, :, :, None], dmat, -jnp.inf)
    inter = b + m[:, None, :]
    m_t = jnp.maximum(inter, jnp.max(dmat, axis=2))
    s = jnp.exp(dmat - m_t[:, :, None, :]) * jnp.einsum('bthd,bshd->btsh', q, k)
    g = jnp.exp(inter - m_t)
    num = jnp.einsum('btsh,bshe->bthe', s, v) + g[..., None] * jnp.einsum('bhed,bthd->bthe', C, q)
    den = jnp.sum(s, axis=2) + g * jnp.einsum('bhd,bthd->bth', n, q)
    h = num / jnp.maximum(jnp.abs(den), jnp.exp(-m_t))[..., None]
    m_new = m_t[:, -1]
    w_end = jnp.exp(b[:, -1:, :] - b + li - m_new[:, None, :])
    g_end = jnp.exp(b[:, -1] + m - m_new)
    C_new = g_end[..., None, None] * C + jnp.einsum('bsh,bshe,bshd->bhed', w_end, v, k)
    n_new = g_end[..., None] * n + jnp.einsum('bsh,bshd->bhd', w_end, k)
    return (C_new, n_new, m_new), h


def mlstm_seq(q, k, v, li, lf, C, n, m):
    B, T = q.shape[0], q.shape[1]
    L = min(CHUNK, T)
    nc = T // L

    def chunks(a):
        return jnp.moveaxis(a.reshape((B, nc, L) + a.shape[2:]), 1, 0)

    (C, n, m), h = lax.scan(mlstm_chunk, (C, n, m), (chunks(q), chunks(k), chunks(v), chunks(li), chunks(lf)))
    h = jnp.moveaxis(h, 0, 1).reshape((B, T) + h.shape[3:])
    return h, C, n, m


def mixer_layer(x, p, norm_g, w_in, ln_v_g, ln_v_b, w_sp, b_sp, conv_w, conv_b, w_q, w_k, w_v,
                w_if, b_if, gn_g, skip, w_out, w_pg, b_pg, w_pp, ple_g, C0, n0, m0, conv0):
    B, T = x.shape[0], x.shape[1]
    h = rmsnorm(x, norm_g)
    proj = h @ w_in
    u, v, z_a, x_b, o_b, z_b = jnp.split(
        proj, [W_A, 2 * W_A, 3 * W_A, 3 * W_A + W_B, 3 * W_A + 2 * W_B], axis=-1)
    u = jax.nn.gelu(u, approximate=False)
    v = jax.nn.gelu(v, approximate=False)
    v_n = head_norm(v.reshape(B, T, H_A, DH_A), ln_v_g, ln_v_b).astype(x.dtype)
    sp = chunk_spatial(v_n, w_sp, b_sp).reshape(B, T, W_A)
    y_a = u * sp * jax.nn.silu(z_a)
    xc, conv_tail = causal_conv(x_b, conv0, conv_w, conv_b)
    xc = jax.nn.silu(xc)
    xch = xc.reshape(B, T, H_B, DH_B)
    xbh = x_b.reshape(B, T, H_B, DH_B)
    q = jnp.einsum('bthd,hde->bthe', xch, w_q)
    k = jnp.einsum('bthd,hde->bthe', xch, w_k)
    vv = jnp.einsum('bthd,hde->bthe', xbh, w_v)
    gin = jnp.concatenate([q.reshape(B, T, W_B), k.reshape(B, T, W_B), vv.reshape(B, T, W_B)], axis=-1)
    gates = (gin @ w_if + b_if).astype(jnp.float32)
    li = gates[..., :H_B]
    lf = jax.nn.log_sigmoid(gates[..., H_B:])
    hb, C, n, m = mlstm_seq(q.astype(jnp.float32), (k * (DH_B ** -0.5)).astype(jnp.float32),
                            vv.astype(jnp.float32), li, lf, C0.astype(jnp.float32),
                            n0.astype(jnp.float32), m0.astype(jnp.float32))
    hb = hb * jax.nn.sigmoid(o_b.reshape(B, T, H_B, DH_B).astype(jnp.float32))
    hb = head_norm(hb, gn_g).reshape(B, T, W_B).astype(x.dtype)
    y_b = (hb + skip * xc) * jax.nn.silu(z_b)
    x = x + jnp.concatenate([y_a, y_b], axis=-1) @ w_out
    e = rmsnorm(p @ w_pp, ple_g)
    x = x + jax.nn.sigmoid(x @ w_pg + b_pg) * e
    return x, C.astype(x.dtype), n.astype(x.dtype), m.astype(x.dtype), conv_tail, v_n.reshape(B, T, W_A)


def setup_inputs(seed: int = 0) -> dict:
    key = jax.random.key(seed)
    ks = jax.random.split(key, 32)

    def nrm(k, shape, s):
        return s * jax.random.normal(k, shape, jnp.float32)

    b_if = jnp.concatenate([
        nrm(ks[20], (DEPTH, H_B), 0.1),
        jnp.linspace(3.0, 6.0, H_B, dtype=jnp.float32)[None, :] + nrm(ks[21], (DEPTH, H_B), 0.01)], axis=-1)
    return {
        'x_prompt': nrm(ks[0], (BATCH, SEQ, D_MODEL), 1.0),
        'x_sample': nrm(ks[1], (DEC_BATCH, DEC_SEQ, D_MODEL), 1.0),
        'state_mlstm_C': nrm(ks[2], (DEPTH, DEC_BATCH, H_B, DH_B, DH_B), 0.1),
        'state_mlstm_n': nrm(ks[3], (DEPTH, DEC_BATCH, H_B, DH_B), 0.1),
        'state_mlstm_m': nrm(ks[4], (DEPTH, DEC_BATCH, H_B), 0.5),
        'state_conv': nrm(ks[5], (DEPTH, DEC_BATCH, CONV_W - 1, W_B), 1.0),
        'p_prompt': nrm(ks[6], (DEPTH, BATCH, SEQ, D_PLE), 1.0),
        'p_sample': nrm(ks[7], (DEPTH, DEC_BATCH, DEC_SEQ, D_PLE), 1.0),
        'norm_in_g': 1.0 + nrm(ks[8], (DEPTH, D_MODEL), 0.02),
        'w_in': nrm(ks[9], (DEPTH, D_MODEL, N_IN), D_MODEL ** -0.5),
        'ln_v_g': 1.0 + nrm(ks[10], (DEPTH, H_A, DH_A), 0.02),
        'ln_v_b': nrm(ks[11], (DEPTH, H_A, DH_A), 0.02),
        'w_spatial': nrm(ks[12], (DEPTH, H_A, CHUNK, CHUNK), CHUNK ** -0.5),
        'b_spatial': 1.0 + nrm(ks[13], (DEPTH, H_A, CHUNK), 0.1),
        'conv_w': nrm(ks[14], (DEPTH, CONV_W, W_B), CONV_W ** -0.5),
        'conv_b': nrm(ks[15], (DEPTH, W_B), 0.02),
        'w_q': nrm(ks[16], (DEPTH, H_B, DH_B, DH_B), DH_B ** -0.5),
        'w_k': nrm(ks[17], (DEPTH, H_B, DH_B, DH_B), DH_B ** -0.5),
        'w_v': nrm(ks[18], (DEPTH, H_B, DH_B, DH_B), DH_B ** -0.5),
        'w_if': nrm(ks[19], (DEPTH, 3 * W_B, 2 * H_B), (3 * W_B) ** -0.5),
        'b_if': b_if,
        'gn_g': 1.0 + nrm(ks[22], (DEPTH, H_B, DH_B), 0.02),
        'skip': 1.0 + nrm(ks[23], (DEPTH, W_B), 0.02),
        'w_out': nrm(ks[24], (DEPTH, D_MIX, D_MODEL), D_MIX ** -0.5),
        'w_ple_gate': nrm(ks[25], (DEPTH, D_MODEL, D_MODEL), D_MODEL ** -0.5),
        'b_ple_gate': nrm(ks[26], (DEPTH, D_MODEL), 0.01),
        'w_ple_proj': nrm(ks[27], (DEPTH, D_PLE, D_MODEL), D_PLE ** -0.5),
        'ple_norm_g': 1.0 + nrm(ks[28], (DEPTH, D_MODEL), 0.02),
        'final_norm_g': 1.0 + nrm(ks[29], (D_MODEL,), 0.02),
    }


def reference(x_prompt, x_sample, state_mlstm_C, state_mlstm_n, state_mlstm_m, state_conv,
              p_prompt, p_sample, norm_in_g, w_in, ln_v_g, ln_v_b, w_spatial, b_spatial,
              conv_w, conv_b, w_q, w_k, w_v, w_if, b_if, gn_g, skip, w_out, w_ple_gate,
              b_ple_gate, w_ple_proj, ple_norm_g, final_norm_g):
    xp, xs = x_prompt, x_sample
    bp = x_prompt.shape[0]
    Cp_l, np_l, mp_l, cp_l = [], [], [], []
    Cs_l, ns_l, ms_l, cs_l, vs_l = [], [], [], [], []
    for i in range(DEPTH):
        lw = (norm_in_g[i], w_in[i], ln_v_g[i], ln_v_b[i], w_spatial[i], b_spatial[i], conv_w[i],
              conv_b[i], w_q[i], w_k[i], w_v[i], w_if[i], b_if[i], gn_g[i], skip[i], w_out[i],
              w_ple_gate[i], b_ple_gate[i], w_ple_proj[i], ple_norm_g[i])
        C0 = jnp.zeros((bp, H_B, DH_B, DH_B), jnp.float32)
        n0 = jnp.zeros((bp, H_B, DH_B), jnp.float32)
        m0 = jnp.zeros((bp, H_B), jnp.float32)
        cb0 = jnp.zeros((bp, CONV_W - 1, W_B), xp.dtype)
        xp, Cp, np_, mp, cp, _ = mixer_layer(xp, p_prompt[i], *lw, C0, n0, m0, cb0)
        xs, Cs, ns, ms, cs, vs = mixer_layer(xs, p_sample[i], *lw, state_mlstm_C[i], state_mlstm_n[i],
                                             state_mlstm_m[i], state_conv[i])
        Cp_l.append(Cp); np_l.append(np_); mp_l.append(mp); cp_l.append(cp)
        Cs_l.append(Cs); ns_l.append(ns); ms_l.append(ms); cs_l.append(cs); vs_l.append(vs)
    y_prompt = rmsnorm(xp, final_norm_g)
    y_sample = rmsnorm(xs, final_norm_g)
    return (y_prompt, y_sample,
            jnp.stack(Cp_l), jnp.stack(np_l), jnp.stack(mp_l), jnp.stack(cp_l),
            jnp.stack(Cs_l), jnp.stack(ns_l), jnp.stack(ms_l), jnp.stack(cs_l), jnp.stack(vs_l))
```

```python
import numpy as np
from contextlib import ExitStack
import concourse.bass as bass
import concourse.mybir as mybir
from concourse.bass_utils import run_bass_kernel_spmd

F32 = mybir.dt.float32
BF16 = mybir.dt.bfloat16
AF = mybir.ActivationFunctionType
ALU = mybir.AluOpType
AX = mybir.AxisListType

T = 2048
TH = 1024
NS = 16
TC = TH + NS
XW = TC + 3
SLOT = 8352
NSLOT = 7
LN16 = 2.772588722239781
EPS = 1e-6
DEBUG = {}


class Op:
    __slots__ = ("eng", "fn", "deps", "ticket", "needed", "dkey", "dval", "seq", "open_group")
    _seq = [0]

    def __init__(self, eng, fn, deps):
        Op._seq[0] += 1
        self.seq = Op._seq[0]
        self.eng = eng
        self.fn = fn
        self.deps = deps
        self.ticket = None
        self.needed = False
        self.dkey = None
        self.dval = None


class _Rec:
    def __getattr__(self, name):
        def f(*a, **k):
            self.__dict__["call"] = (name, a, k)
            return None
        return f


class Prog:
    ENGS = ("pe", "act", "dve", "pool", "sp")

    def __init__(self, nc):
        self.nc = nc
        self.ops = {e: [] for e in self.ENGS}
        self.dcount = {}

    def add(self, eng, fn, deps=()):
        dl = [d for d in deps if d is not None]
        rec = _Rec()
        fn(rec)
        name, a, k = rec.call
        fn = (lambda e, name=name, a=a, k=k: getattr(e, name)(*a, **k))
        op = Op(eng, fn, dl)
        op.open_group = (name == "matmul" and k.get("stop") is False)
        for d in dl:
            d.needed = True
        self.ops[eng].append(op)
        return op

    def dma(self, queue, out, in_, key, deps=(), **kw):
        def fn(e):
            return e.dma_start(out=out, in_=in_, **kw)

        op = self.add(queue, fn, deps)
        self.dcount[key] = self.dcount.get(key, 0) + 16
        op.dkey = key
        op.dval = self.dcount[key]
        return op

    def emit(self):
        nc = self.nc
        with ExitStack() as es:
            esem = {e: es.enter_context(nc.semaphore("s_" + e)) for e in self.ENGS}
            dsem = {k: es.enter_context(nc.semaphore("d_%s" % (k,))) for k in self.dcount}
            for e in self.ENGS:
                c = 0
                for op in self.ops[e]:
                    if op.dkey is None and op.needed:
                        c += 1
                        op.ticket = c
            block = es.enter_context(nc.Block())

            def run(ename, eng):
                waited = {}
                for op in self.ops[ename]:
                    for d in op.deps:
                        if d.dkey is not None:
                            s, v = dsem[d.dkey], d.dval
                        else:
                            s, v = esem[d.eng], d.ticket
                        if waited.get(s.name, 0) < v:
                            eng.wait_ge(s, v)
                            waited[s.name] = v
                    ins = op.fn(eng)
                    if op.dkey is not None:
                        ins.then_inc(dsem[op.dkey], 16)
                    elif op.needed:
                        ins.then_inc(esem[ename], 1)

            @block.tensor
            def _(e):
                run("pe", e)

            @block.scalar
            def _(e):
                run("act", e)

            @block.vector
            def _(e):
                run("dve", e)

            @block.gpsimd
            def _(e):
                run("pool", e)

            @block.sync
            def _(e):
                run("sp", e)


def _key(op):
    return ("d", op.dkey) if op.dkey is not None else ("e", op.eng)


class Buf:
    registry = []

    def __init__(self, ap, tname=None, lo=0, hi=0, psum=False):
        self.psum = psum
        self.ap = ap
        self.wr = {}
        self.rd = {}
        self.old = {}
        if tname is not None:
            for (tn, l, h, b) in Buf.registry:
                if tn == tname and l < hi and lo < h:
                    for d in (b.old, b.wr, b.rd):
                        for k, v in d.items():
                            self._put(self.old, k, v)
            Buf.registry.append((tname, lo, hi, self))

    @staticmethod
    def _put(d, k, op):
        cur = d.get(k)
        if cur is None:
            d[k] = op
        elif op.dkey is not None:
            if op.dval > cur.dval:
                d[k] = op
        elif op.seq > cur.seq:
            d[k] = op

    def __getitem__(self, idx):
        return self.ap[idx]

    def rdeps(self, eng):
        out = [op for k, op in self.wr.items() if not (eng == "pe" and k == ("e", "pe"))]
        if self.psum:
            out += [op for k, op in self.rd.items() if k != ("e", eng)]
        return out

    def wdeps(self, eng, partial=False):
        skip = ("e", "pe") if eng == "pe" else None
        out = [op for k, op in self.old.items() if k != skip]
        if not partial:
            out += [op for k, op in self.rd.items() if k != skip]
            out += [op for k, op in self.wr.items() if k != skip]
        return out

    def read(self, op):
        self._put(self.rd, _key(op), op)

    def wrote(self, op, partial=False):
        if not partial:
            self.wr = {}
            self.rd = {}
            self.old = {}
        self._put(self.wr, _key(op), op)


def build(limit=None):
    Buf.registry = []
    Op._seq[0] = 0
    nc = bass.Bass("TRN2", target_bir_lowering=False)
    P = Prog(nc)
    es = ExitStack()

    def din(name, shape):
        return nc.dram_tensor(name, list(shape), F32, kind="ExternalInput").ap()

    def dout(name, shape):
        return nc.dram_tensor(name, list(shape), F32, kind="ExternalOutput").ap()

    xp = din("xp", [T, 1024]); xs = din("xs", [NS, 1024])
    pp_ = din("pp", [T, 256]); psm = din("psm", [NS, 256])
    C_in = din("C_in", [NS, 4, 256, 256]); n_in = din("n_in", [NS, 1024]); m_in = din("m_in", [NS, 4])
    cv_in = din("cv_in", [NS, 3, 1024])
    norm_g = din("norm_g", [1024]); w_in = din("w_in", [1024, 6144])
    lnv_g = din("lnv_g", [1024]); lnv_b = din("lnv_b", [1024])
    w_sp = din("w_sp", [8, 128, 128]); b_sp = din("b_sp", [8, 128])
    conv_w = din("conv_w", [4, 1024]); conv_b = din("conv_b", [1024])
    w_q = din("w_q", [4, 256, 256]); w_k = din("w_k", [4, 256, 256]); w_v = din("w_v", [4, 256, 256])
    w_if = din("w_if", [3072, 8]); b_if = din("b_if", [8])
    gn_g = din("gn_g", [1024]); skip = din("skip", [1024])
    w_out = din("w_out", [2048, 1024]); w_pg = din("w_pg", [1024, 1024]); b_pg = din("b_pg", [1024])
    w_pp = din("w_pp", [256, 1024]); ple_g = din("ple_g", [1024]); fin_g = din("fin_g", [1024])

    y_p = dout("y_p", [T, 1024]); y_s = dout("y_s", [NS, 1024])
    Cp = dout("Cp", [4, 256, 256]); np_o = dout("np_o", [4, 256]); mp_o = dout("mp_o", [1, 4])
    cvp = dout("cvp", [3, 1024])
    Cs = dout("Cs", [NS, 4, 256, 256]); ns_o = dout("ns_o", [NS, 1024]); ms_o = dout("ms_o", [NS, 4])
    cvs = dout("cvs", [NS, 3, 1024]); vs_o = dout("vs_o", [NS, 1024])

    hscr = nc.dram_tensor("hscr", [128, 8 * TC], BF16, kind="Internal").ap()
    hscrb = Buf(hscr, "hscr", 0, 1)
    out_dmas = []
    cnt = [0]

    def sbt(shape, dt, name=None):
        cnt[0] += 1
        return es.enter_context(nc.sbuf_tensor(name or ("t%d" % cnt[0]), list(shape), dt))

    def newbuf(shape, dt):
        t = sbt(shape, dt)
        ap = t[:] if len(shape) == 2 else t[tuple(slice(None) for _ in shape)]
        b = Buf(ap, "nb%d" % cnt[0], 0, 1)
        b.tname = "nb%d" % cnt[0]
        return b

    def rebuf(b):
        nb_ = Buf(b.ap, b.tname, 0, 1)
        nb_.tname = b.tname
        return nb_

    def E(eng, fn, reads=(), writes=(), pwrites=(), extra=(), war=()):
        deps = list(extra)
        for b in war:
            deps += [op for k, op in b.rd.items()]
        for b in reads:
            deps += b.rdeps(eng)
        for b in writes:
            deps += b.wdeps(eng)
        for b in pwrites:
            deps += b.wdeps(eng, partial=True)
        op = P.add(eng, fn, deps)
        for b in reads:
            b.read(op)
        for b in writes:
            b.wrote(op)
        for b in pwrites:
            b.wrote(op, partial=True)
        return op

    dkc = [0]

    def DMA(queue, out, in_, reads=(), writes=(), pwrites=(), key=None, **kw):
        deps = []
        for b in reads:
            deps += b.rdeps(queue)
        for b in writes:
            deps += b.wdeps(queue)
        for b in pwrites:
            deps += b.wdeps(queue, partial=True)
        if key is None:
            dkc[0] += 1
            key = "k%d" % dkc[0]
        op = P.dma(queue, out, in_, key, deps, **kw)
        for b in reads:
            b.read(op)
        for b in writes:
            b.wrote(op)
        for b in pwrites:
            b.wrote(op, partial=True)
        return op

    def rkey(prefix, i, n):
        return "%s%d" % (prefix, i % n)

    arena = sbt([128, NSLOT * SLOT], BF16, "arena")
    FS = 4096
    fscr = sbt([128, FS], F32, "fscr")
    cst = sbt([128, 3, 1024], F32, "cst")

    def ar(slot, off, n, dt=BF16):
        lo = slot * SLOT + off
        ap = arena[:, lo:lo + n]
        if dt == F32:
            ap = ap.bitcast(F32)
        return Buf(ap, "arena", lo * 2, (lo + n) * 2)

    def fs(off, n, dt=F32):
        ap = fscr[:, off:off + n]
        if dt == BF16:
            ap = ap.bitcast(BF16)
        return Buf(ap, "fscr", off * 4, (off + n) * 4)

    def cs_(i):
        return Buf(cst[:, i, :], "cst", i * 4096, (i + 1) * 4096)

    banks = [es.enter_context(nc.psum_tensor("bank%d" % i, [128, 512], F32)) for i in range(8)]

    def pb(bank, off, n, dt=F32):
        ap = banks[bank][:, off:off + n]
        if dt == BF16:
            ap = ap.bitcast(BF16)
        return Buf(ap, "bank%d" % bank, 0, 2048, psum=True)

    ident = newbuf([128, 128], BF16); identf = newbuf([128, 128], F32)
    maskT = newbuf([128, 128], BF16); tri = newbuf([128, 128], F32)
    onesf = newbuf([128, 128], F32); onesb = newbuf([128, 128], BF16)
    epsc = newbuf([128, 1], F32); mhalf = newbuf([128, 1], F32); onec = newbuf([128, 1], F32)
    nln16 = newbuf([128, 1], F32)
    monec = newbuf([128, 1], F32)
    c1024 = newbuf([128, 1], F32); c128 = newbuf([128, 1], F32); c256 = newbuf([128, 1], F32)

    def memset(b, val, eng="pool"):
        return E(eng, lambda e: e.memset(b.ap, val), writes=[b])

    memset(identf, 1.0)
    E("pool", lambda e: e.affine_select(out=identf.ap, in_=identf.ap, pattern=[[-1, 128]], compare_op=ALU.is_equal,
                                        fill=0.0, base=0, channel_multiplier=1), reads=[identf], writes=[identf])
    E("pool", lambda e: e.tensor_copy(out=ident.ap, in_=identf.ap), reads=[identf], writes=[ident])
    memset(tri, 1.0)
    E("pool", lambda e: e.affine_select(out=tri.ap, in_=tri.ap, pattern=[[1, 128]], compare_op=ALU.is_ge,
                                        fill=0.0, base=0, channel_multiplier=-1), reads=[tri], writes=[tri])
    E("pool", lambda e: e.tensor_copy(out=maskT.ap, in_=tri.ap), reads=[tri], writes=[maskT])
    memset(onesf, 1.0); memset(onesb, 1.0)
    memset(epsc, EPS); memset(mhalf, -0.5); memset(onec, 1.0); memset(nln16, -LN16); memset(monec, -1.0)
    memset(c1024, 1.0 / 1024); memset(c128, 1.0 / 128); memset(c256, 1.0 / 256)

    def rstd_pool(out_ap, in_ap, cinv, n, tp, rbufs, wbuf, tmpb):
        def bc(c):
            return c.ap[:tp, 0:1] if n == 1 else c.ap[:tp, 0:1].broadcast_to([tp, n])
        E("pool", lambda e: e.tensor_tensor(out=tmpb.ap[:tp, 0:n], in0=in_ap, in1=bc(cinv), op=ALU.mult),
          reads=list(rbufs) + [cinv], writes=[tmpb])
        E("pool", lambda e: e.tensor_tensor(out=tmpb.ap[:tp, 0:n], in0=tmpb.ap[:tp, 0:n], in1=bc(epsc), op=ALU.add),
          reads=[tmpb, epsc], writes=[tmpb])
        return E("pool", lambda e: e.tensor_tensor(out=out_ap, in0=tmpb.ap[:tp, 0:n], in1=bc(mhalf), op=ALU.pow),
                 reads=[tmpb, mhalf], pwrites=[wbuf])

    wq = newbuf([128, 4, 2, 256], BF16); wk = newbuf([128, 4, 2, 256], BF16); wv = newbuf([128, 4, 2, 256], BF16)
    wif = newbuf([128, 24, 8], BF16)
    bifb = newbuf([128, 8], F32)
    gngT = newbuf([128, 8], F32); skipT = newbuf([128, 8], F32); cbT = newbuf([128, 8], F32)
    lgT = newbuf([128, 8], F32); lbT = newbuf([128, 8], F32)
    cwT = newbuf([128, 4, 8], F32)
    bsp0 = newbuf([128, 8], F32)
    W00 = newbuf([16, 8], F32)
    WT = newbuf([128, 8, 128], BF16)
    Wdiag = newbuf([16, 8, 16], BF16)

    def deferred_setup():
        for wb_, src in ((wq, w_q), (wk, w_k), (wv, w_v)):
            DMA("pool", wb_.ap, src.rearrange("h (kk p) e -> p h kk e", p=128), writes=[wb_])
        DMA("pool", wif.ap, w_if.rearrange("(k p) g -> p k g", p=128), writes=[wif])
        DMA("sp", bifb.ap, b_if.partition_broadcast(128), writes=[bifb])
        for b_, src in ((gngT, gn_g), (skipT, skip), (cbT, conv_b), (lgT, lnv_g), (lbT, lnv_b)):
            DMA("sp", b_.ap, src.rearrange("(c p) -> p c", p=128), writes=[b_], allow_slow_non_contiguous=True)
        DMA("sp", cwT.ap, conv_w.rearrange("j (c p) -> p j c", p=128), writes=[cwT], allow_slow_non_contiguous=True)
        DMA("sp", bsp0.ap, b_sp[:, 0].partition_broadcast(128), writes=[bsp0], allow_slow_non_contiguous=True)
        DMA("sp", W00.ap, w_sp[:, 0, 0].partition_broadcast(16), writes=[W00], allow_slow_non_contiguous=True)
        wspf = fs(0, 1024)
        DMA("sp", wspf.ap.rearrange("p (h s) -> p h s", h=8), w_sp.rearrange("h t s -> t h s"), writes=[wspf])
        wtf = fs(1024, 1024)
        for half in range(2):
            pw_ = pb(half, 0, 512)
            for hh in range(4):
                h = half * 4 + hh
                E("pe", lambda e, h=h, hh=hh, pw_=pw_: e.transpose(out=pw_.ap[:, hh * 128:(hh + 1) * 128],
                                                                   in_=wspf.ap[:, h * 128:(h + 1) * 128], identity=identf.ap),
                  reads=[wspf, identf], pwrites=[pw_])
            E("act", lambda e, half=half, pw_=pw_: e.copy(out=wtf.ap[:, half * 512:(half + 1) * 512], in_=pw_.ap),
              reads=[pw_], pwrites=[wtf])
        E("pool", lambda e: e.affine_select(out=WT.ap, in_=wtf.ap.rearrange("p (h t) -> p h t", h=8),
                                            pattern=[[0, 8], [1, 128]], compare_op=ALU.is_ge, fill=0.0, base=0,
                                            channel_multiplier=-1), reads=[wtf], writes=[WT])
        E("dve", lambda e: e.tensor_tensor(out=Wdiag.ap, in0=identf.ap[0:16, None, 0:16].broadcast_to([16, 8, 16]),
                                           in1=W00.ap[:, :, None].broadcast_to([16, 8, 16]), op=ALU.mult),
          reads=[identf, W00], writes=[Wdiag])

    CT = newbuf([128, 4, 2, 256], F32); nT = newbuf([128, 4, 2], F32)
    memset(CT, 0.0); memset(nT, 0.0)
    mcar = newbuf([1, 4], F32)
    memset(mcar, 0.0)
    tails = newbuf([128, 8, 3], BF16)
    memset(tails, 0.0)
    G_g = newbuf([128, 8, 9], F32)
    E1 = newbuf([128, 4, 8], F32); LFN = newbuf([128, 4, 8], F32); BNs = newbuf([128, 4, 8], F32)
    Csb = newbuf([128, 4, 8], F32); A1 = newbuf([128, 4, 8], F32)
    WS = newbuf([128, 4, 8], F32); FL = newbuf([128, 4, 8], F32); GSb = newbuf([128, 4, 8], F32)
    cmaxc = newbuf([32, 1], F32)
    rowA = newbuf([1, 32], F32); rowB = newbuf([1, 32], F32); MN = newbuf([1, 32], F32); MPV = newbuf([1, 32], F32)
    RG = newbuf([1, 64], F32); rtmp = newbuf([1, 32], F32)
    S12 = newbuf([16, 12], F32); sg = newbuf([16, 64], F32)
    m_s = newbuf([16, 4], F32)
    BD = newbuf([16, 16, 12], F32); bcs = newbuf([128, 16, 12], F32)
    rinv_s = newbuf([16, 4], F32)
    vTs = newbuf([128, 8, 16], F32); bufT = newbuf([128, 8, 3, 16], F32)
    qks = newbuf([16, 2, 1024], BF16)
    sgs = newbuf([16, 1024], BF16)
    vns = newbuf([16, 1024], BF16)
    wvtok = newbuf([16, 1024], BF16)
    CqT = newbuf([128, 4, 2, 16], F32); wvT = newbuf([128, 4, 2, 16], F32); numTs = newbuf([128, 4, 2, 16], F32)
    stat_g = newbuf([128, 64], F32)
    stat2_g = newbuf([128, 64], F32)
    ptmp = newbuf([128, 80], F32)
    ptmps = [newbuf([128, 2], F32), newbuf([128, 2], F32)]
    S1_g = newbuf([128, 9, 8], F32); S2_g = newbuf([128, 9, 8], F32); MEAN = newbuf([128, 9, 8], F32)
    RSTD = newbuf([128, 9, 8], F32)
    memset(S1_g, 0.0); memset(S2_g, 0.0)

    DMA("sp", m_s.ap, m_in, writes=[m_s])
    out_dmas.append(P.dma("sp", cvs[:, 0:2, :], cv_in[:, 1:3, :], "cvcp"))

    NWB = 2
    wblk = [newbuf([128, 8, 512], BF16) for _ in range(NWB)]
    wbi = [0]

    def load_wblk(blk):
        b = wblk[wbi[0] % NWB]
        DMA("pool", b.ap, w_in[:, blk * 512:(blk + 1) * 512].rearrange("(k p) n -> p k n", p=128), writes=[b],
            key=rkey("wb", wbi[0], NWB))
        wbi[0] += 1
        return b

    psrot = [0]

    def colgroups(pp):
        g = [(0, 512), (512, 512)]
        if pp == 0:
            g.append((1024, 16))
        return g

    def ntiles(pp):
        return 9 if pp == 0 else 8

    def tinfo(i):
        return (128, i * 128) if i < 8 else (NS, TH)

    evrot = [0]

    def evac(out_ap, in_ap, reads, pw=None, w=None, eng=None, func=None, scale=1.0, bias=None):
        if eng is None:
            eng = "act" if (func is not None or evrot[0] % 2 == 0) else "dve"
            evrot[0] += 1
        kw = dict(reads=list(reads), pwrites=[pw] if pw else [], writes=[w] if w else [])
        if eng == "act":
            f = func or AF.Copy
            if bias is not None:
                kw["reads"].append(bias[0])
                return E("act", lambda e: e.activation(out=out_ap, in_=in_ap, func=f, scale=scale, bias=bias[1]), **kw)
            return E("act", lambda e: e.activation(out=out_ap, in_=in_ap, func=f, scale=scale), **kw)
        return E(eng, lambda e: e.tensor_copy(out=out_ap, in_=in_ap), **kw)

    def pipeline(n, stages):
        S = len(stages)
        for t in range(n + S - 1):
            for s_ in reversed(range(S)):
                i = t - s_
                if 0 <= i < n:
                    stages[s_](i)

    def phase_hT(pp, hT, hook=None, hTg=None):
        t0 = pp * TH
        stat = rebuf(stat_g)
        gin = cs_(0)
        DMA("sp", gin.ap, norm_g.partition_broadcast(128), writes=[gin])
        NXB = 6
        xts = [ar(2 + i // 4, (i % 4) * 2048, 2048, F32) for i in range(NXB)]
        hbs = [fs(3072, 512, BF16), fs(3584, 512, BF16)]
        ptrs = [pb(0, 0, 512, BF16), pb(1, 0, 512, BF16)]
        pjunk = Buf(cst[:, 1, 0:512].bitcast(BF16), "cst", 4096, 4096 + 2048)
        hTv_ = hT.ap.rearrange("p (c t) -> p c t", c=8)

        def s0(i):
            tp, c0 = tinfo(i)
            xt = xts[i % NXB]
            src = xp[t0 + i * 128:t0 + (i + 1) * 128, :] if i < 8 else xs
            DMA("sp", xt.ap[:tp], src, writes=[xt], key=rkey("x", i, NXB))
            E("act", lambda e: e.activation(out=pjunk.ap[:tp], in_=xt.ap[:tp], func=AF.Square,
                                            accum_out=stat.ap[:tp, i:i + 1]), reads=[xt], pwrites=[stat], writes=[pjunk])
            rstd_pool(stat.ap[:tp, 16 + i:17 + i], stat.ap[:tp, i:i + 1], c1024, 1, tp, [stat], stat, ptmps[i % 2])
            if i == 2 and hook is not None:
                hook()

        def s1(i):
            tp, c0 = tinfo(i)
            xt = xts[i % NXB]; hb = hbs[i % 2]
            E("dve", lambda e: e.scalar_tensor_tensor(
                out=hb.ap[:tp], in0=xt.ap[:tp], scalar=stat.ap[:tp, 16 + i:17 + i], in1=gin.ap[:tp],
                op0=ALU.mult, op1=ALU.mult), reads=[xt, stat, gin], writes=[hb])

        def s2(i):
            tp, c0 = tinfo(i)
            hb = hbs[i % 2]; ptr = ptrs[i % 2]
            for k in range(8):
                E("pe", lambda e, k=k: e.transpose(
                    out=ptr.ap[:, k * 128:k * 128 + tp], in_=hb.ap[:tp, k * 128:(k + 1) * 128],
                    identity=ident.ap[:tp, :tp]), reads=[hb, ident], writes=[ptr] if k == 0 else [], pwrites=[] if k == 0 else [ptr])
            evac(hTv_[:, :, c0:c0 + tp], ptr.ap.rearrange("p (c t) -> p c t", c=8)[:, :, 0:tp], [ptr],
                 pw=hTg[i // 4 if i < 8 else 2])

        pipeline(ntiles(pp), [s0, s1, s2])

    def projB(pp, lhs_fn, nk, rhs_fn, out_fn, lbufs, rbufs, groups=None, rb_fn=None):
        for (c0, n) in (groups if groups is not None else colgroups(pp)):
            ps = pb(psrot[0] % 6, 0, 512); psrot[0] += 1
            rb = list(rbufs) if rb_fn is None else [rb_fn(c0)]
            for k in range(nk):
                E("pe", lambda e, k=k, c0=c0, n=n, ps=ps: e.matmul(ps.ap[:, 0:n], lhsT=lhs_fn(k), rhs=rhs_fn(k, c0, n),
                                                                   start=(k == 0), stop=(k == nk - 1)),
                  reads=list(lbufs) + rb, writes=[ps] if k == 0 else [], pwrites=[] if k == 0 else [ps])
            out_fn(ps, c0, n)

    def projA(lhs_fn, nk, rhs_fn, tp, lbufs, rbufs, bank=None):
        if bank is None:
            bank = psrot[0] % 6; psrot[0] += 1
        ps = pb(bank, 0, 512)
        for k in range(nk):
            E("pe", lambda e, k=k, ps=ps: e.matmul(ps.ap[:tp, :], lhsT=lhs_fn(k), rhs=rhs_fn(k),
                                                   start=(k == 0), stop=(k == nk - 1)),
              reads=list(lbufs) + list(rbufs), writes=[ps] if k == 0 else [], pwrites=[] if k == 0 else [ps])
        return ps

    V3 = lambda b, c: b.ap.rearrange("p (c t) -> p c t", c=c)

    for pp in range(2):
        t0 = pp * TH
        G = rebuf(G_g); stat2 = rebuf(stat2_g); S1 = rebuf(S1_g); S2 = rebuf(S2_g)
        NC_ = TC if pp == 0 else TH
        nt = ntiles(pp)
        wxb = []
        hT = ar(0, 0, 8 * TC)
        hTg = [Buf(hT.ap, "arena", 0, 8 * TC * 2) for _ in range(3)]
        hTl = hTg if pp == 0 else hTg[0:2]
        hgrp = lambda c0: hTg[min(c0 // 512, 2)]
        phase_hT(pp, hT, hook=lambda: wxb.extend([load_wblk(6), load_wblk(7)]), hTg=hTg)
        if pp == 0:
            deferred_setup()
        hTv = V3(hT, 8)
        hscrb = Buf(hscr, "hscr", 0, 1)
        DMA("sp", hscr.rearrange("p (c t) -> p c t", c=8)[:, :, 0:NC_], hTv[:, :, 0:NC_], reads=hTl, writes=[hscrb])
        xbT = ar(1, 0, 8 * XW); xbv = V3(xbT, 8)
        xcT = ar(2, 0, 8 * TC); xcv = V3(xcT, 8)
        qT = ar(3, 0, 8 * TC); qv = V3(qT, 8)
        kT = ar(4, 0, 8 * TC); kv = V3(kT, 8)
        vaug = ar(5, 0, 8 * 4 * 257); vav = vaug.ap.rearrange("p (i h e) -> p i h e", i=8, h=4)
        vT = ar(6, 0, 8 * TC); vv = V3(vT, 8)

        if pp == 0:
            cvtok = ar(6, 0, 6144, F32)
            DMA("sp", cvtok.ap[0:16, :], cv_in.rearrange("b j c -> b (j c)"), writes=[cvtok])
            pcv = pb(7, 0, 384)
            for j in range(3):
                for c in range(8):
                    idx = c * 3 + j
                    E("pe", lambda e, j=j, c=c, idx=idx: e.transpose(
                        out=pcv.ap[:, idx * 16:(idx + 1) * 16],
                        in_=cvtok.ap[0:16, j * 1024 + c * 128:j * 1024 + (c + 1) * 128], identity=identf.ap[0:16, 0:16]),
                      reads=[cvtok, identf], pwrites=[pcv])
            evac(bufT.ap.rearrange("p c j b -> p (c j b)"), pcv.ap, [pcv], w=bufT, eng="dve")
        E("dve", lambda e: e.tensor_copy(out=xbv[:, :, 0:3], in_=tails.ap), reads=[tails], pwrites=[xbT])
        xbtok = fs(3072, 1024)
        tok_tile = 8 if pp == 0 else 7
        for grp in colgroups(pp):
            for bi, blk in enumerate((6, 7)):
                wb = wxb[bi]
                for cc in range(4):
                    c = bi * 4 + cc

                    def outf(ps, c0, n, c=c):
                        evac(xbv[:, c, 3 + c0:3 + c0 + n], ps.ap[:, 0:n], [ps], pw=xbT)
                    projB(pp, lambda k, cc=cc, wb=wb: wb.ap[:, k, cc * 128:(cc + 1) * 128], 8,
                          lambda k, c0, n: hTv[:, k, c0:c0 + n], outf, [wb], [], groups=[grp], rb_fn=hgrp)
        for bi, blk in enumerate((6, 7)):
            wb = wxb[bi]
            tp, tc0 = tinfo(tok_tile)
            ps = projA(lambda k: hTv[:, k, tc0:tc0 + tp], 8, lambda k, wb=wb: wb.ap[:, k, :], tp, [hgrp(tc0)], [wb])
            evac(xbtok.ap[:tp, bi * 512:(bi + 1) * 512], ps.ap[:tp, :], [ps], pw=xbtok)
        if pp == 0:
            out_dmas.append(DMA("sp", cvs[:, 2, :], xbtok.ap[0:16, :], reads=[xbtok]))
        else:
            out_dmas.append(DMA("sp", cvp, xbtok.ap[125:128, :], reads=[xbtok]))
        if pp == 0:
            E("dve", lambda e: e.tensor_copy(out=tails.ap, in_=xbv[:, :, 3 + TH - 3:3 + TH]), reads=[xbT], writes=[tails])

        Dws = [fs(0, 256, BF16), fs(256, 256, BF16)]
        for c in range(8):
            Dw = Dws[c % 2]
            d3 = Dw.ap.rearrange("p (j m) -> p j m", j=4)
            E("dve", lambda e, c=c, d3=d3: e.tensor_tensor(out=d3, in0=identf.ap[:, None, :].broadcast_to([128, 4, 128]),
                                                          in1=cwT.ap[:, :, c:c + 1].broadcast_to([128, 4, 128]), op=ALU.mult),
              reads=[identf, cwT], writes=[Dw])
            for g_ in range(2):
                ps = pb(psrot[0] % 6, 0, 512); psrot[0] += 1
                for j in range(4):
                    E("pe", lambda e, c=c, j=j, g_=g_, ps=ps, d3=d3: e.matmul(
                        ps.ap, lhsT=d3[:, j, :], rhs=xbv[:, c, j + g_ * 512:j + g_ * 512 + 512], start=(j == 0), stop=(j == 3)),
                      reads=[Dw, xbT], writes=[ps] if j == 0 else [], pwrites=[] if j == 0 else [ps])
                evac(xcv[:, c, g_ * 512:(g_ + 1) * 512], ps.ap, [ps], pw=xcT, eng="act", func=AF.Silu, bias=(cbT, cbT.ap[:, c:c + 1]))
        if pp == 0:
            accS = fs(2048, 128)
            av = accS.ap.rearrange("p (c b) -> p c b", c=8)
            for c in range(8):
                E("dve", lambda e, c=c: e.tensor_scalar(out=av[:, c, :], in0=xbv[:, c, 3 + TH:3 + TC], scalar1=cwT.ap[:, 3, c:c + 1],
                                                       scalar2=None, op0=ALU.mult), reads=[xbT, cwT], pwrites=[accS])
                for j in range(3):
                    E("dve", lambda e, c=c, j=j: e.scalar_tensor_tensor(
                        out=av[:, c, :], in0=bufT.ap[:, c, j, :], scalar=cwT.ap[:, j, c:c + 1], in1=av[:, c, :],
                        op0=ALU.mult, op1=ALU.add), reads=[bufT, cwT, accS], pwrites=[accS])
                evac(xcv[:, c, TH:TC], av[:, c, :], [accS], pw=xcT, eng="act", func=AF.Silu, bias=(cbT, cbT.ap[:, c:c + 1]))

        E("pool", lambda e: e.memset(vav[:, :, :, 256:257], 1.0), pwrites=[vaug])
        for (dst, dv, wsrc, src, sv, soff) in ((qT, qv, wq, xcT, xcv, 0), (kT, kv, wk, xcT, xcv, 0), (vT, vv, wv, xbT, xbv, 3)):
            for h in range(4):
                for ec in range(2):
                    def outf(ps, c0, n, h=h, ec=ec, dst=dst, dv=dv):
                        evac(dv[:, 2 * h + ec, c0:c0 + n], ps.ap[:, 0:n], [ps], pw=dst)
                        if dst is vT and c0 == TH:
                            evac(vTs.ap[:, 2 * h + ec, :], ps.ap[:, 0:n], [ps], pw=vTs, eng="dve")
                    projB(pp, lambda k, h=h, ec=ec, wsrc=wsrc: wsrc.ap[:, h, k, ec * 128:(ec + 1) * 128], 2,
                          lambda k, c0, n, h=h, sv=sv, soff=soff: sv[:, 2 * h + k, soff + c0:soff + c0 + n], outf, [wsrc], [src])
        for i in range(8):
            for hb2 in range(2):
                ps = pb(psrot[0] % 6, 0, 512); psrot[0] += 1
                for hh in range(2):
                    h = hb2 * 2 + hh
                    for kk in range(2):
                        E("pe", lambda e, i=i, h=h, hh=hh, kk=kk, ps=ps: e.matmul(
                            ps.ap[:, hh * 256:(hh + 1) * 256], lhsT=xbv[:, 2 * h + kk, 3 + i * 128:3 + (i + 1) * 128],
                            rhs=wv.ap[:, h, kk, :], start=(kk == 0), stop=(kk == 1)),
                          reads=[xbT, wv], writes=[ps] if (hh == 0 and kk == 0) else [], pwrites=[] if (hh == 0 and kk == 0) else [ps])
                evac(vav[:, i, hb2 * 2:hb2 * 2 + 2, 0:256], ps.ap.rearrange("p (h e) -> p h e", h=2), [ps], pw=vaug)
        if pp == 0:
            q_sf = fs(0, 1024); k_sf = fs(1024, 1024); n_sf = fs(2048, 1024); tmp_s = fs(3072, 1024)
            DMA("sp", n_sf.ap[0:16, :], n_in, writes=[n_sf])
            for (wsrc, dstf, qi) in ((wq, q_sf, 0), (wk, k_sf, 1)):
                for hb2 in range(2):
                    ps = pb(psrot[0] % 6, 0, 512); psrot[0] += 1
                    for hh in range(2):
                        h = hb2 * 2 + hh
                        for kk in range(2):
                            E("pe", lambda e, h=h, hh=hh, kk=kk, ps=ps, wsrc=wsrc: e.matmul(
                                ps.ap[0:16, hh * 256:(hh + 1) * 256], lhsT=xcv[:, 2 * h + kk, TH:TC],
                                rhs=wsrc.ap[:, h, kk, :], start=(kk == 0), stop=(kk == 1)),
                              reads=[xcT, wsrc], writes=[ps] if (hh == 0 and kk == 0) else [],
                              pwrites=[] if (hh == 0 and kk == 0) else [ps])
                    evac(dstf.ap[0:16, hb2 * 512:(hb2 + 1) * 512], ps.ap[0:16, :], [ps], pw=dstf, eng="act")
                    evac(qks.ap[0:16, qi, hb2 * 512:(hb2 + 1) * 512], ps.ap[0:16, :], [ps], pw=qks, eng="dve")

        gps = pb(6, 0, 72)
        for i in range(nt):
            tp, c0 = tinfo(i)
            for kc in range(24):
                srcb, srcv = ((qT, qv), (kT, kv), (vT, vv))[kc // 8]
                E("pe", lambda e, i=i, kc=kc, tp=tp, c0=c0, srcv=srcv: e.matmul(
                    gps.ap[:tp, i * 8:(i + 1) * 8], lhsT=srcv[:, kc % 8, c0:c0 + tp], rhs=wif.ap[:, kc, :],
                    start=(kc == 0), stop=(kc == 23)), reads=[srcb, wif],
                  writes=[gps] if (i == 0 and kc == 0) else [], pwrites=[] if (i == 0 and kc == 0) else [gps])
        E("dve", lambda e: e.tensor_tensor(out=G.ap[:, :, 0:8], in0=gps.ap[:, 0:64].rearrange("p (i g) -> p g i", g=8),
                                           in1=bifb.ap[:, :, None].broadcast_to([128, 8, 8]), op=ALU.add),
          reads=[gps, bifb], pwrites=[G])
        if pp == 0:
            E("dve", lambda e: e.tensor_tensor(out=G.ap[0:16, :, 8], in0=gps.ap[0:16, 64:72], in1=bifb.ap[0:16, :], op=ALU.add),
              reads=[gps, bifb], pwrites=[G])
        def gate_math():
            f32v = lambda b: b.ap.rearrange("p h j -> p (h j)")
            E("act", lambda e: e.activation(out=E1.ap, in_=G.ap[:, 4:8, 0:8], func=AF.Exp, scale=-1.0), reads=[G], writes=[E1])
            E("act", lambda e: e.activation(out=LFN.ap, in_=E1.ap, func=AF.Ln, bias=onec.ap[:, 0:1]), reads=[E1, onec], writes=[LFN])
            pbn = pb(7, 0, 32); prB = pb(6, 64, 32)
            yield
            E("pe", lambda e: e.matmul(pbn.ap, lhsT=tri.ap, rhs=f32v(LFN), start=True, stop=True), reads=[tri, LFN], writes=[pbn])
            yield
            E("pe", lambda e: e.matmul(prB.ap[0:1, :], lhsT=onesf.ap[:, 0:1], rhs=f32v(LFN), start=True, stop=True),
              reads=[onesf, LFN], writes=[prB])
            E("act", lambda e: e.copy(out=f32v(BNs), in_=pbn.ap), reads=[pbn], writes=[BNs])
            E("act", lambda e: e.copy(out=rowB.ap, in_=prB.ap[0:1, :]), reads=[prB], writes=[rowB])
            E("dve", lambda e: e.tensor_tensor(out=Csb.ap, in0=G.ap[:, 0:4, 0:8], in1=pbn.ap.rearrange("p (h j) -> p h j", h=4), op=ALU.add),
              reads=[G, pbn], writes=[Csb])
            yield
            pct = pb(7, 128, 128)
            E("pe", lambda e: e.transpose(out=pct.ap[0:32, :], in_=f32v(Csb), identity=identf.ap), reads=[Csb, identf], writes=[pct])
            E("dve", lambda e: e.tensor_reduce(out=cmaxc.ap, in_=pct.ap[0:32, :], axis=AX.X, op=ALU.max), reads=[pct], writes=[cmaxc])
            yield
            prA = pb(6, 128, 32)
            E("pe", lambda e: e.transpose(out=prA.ap[0:1, :], in_=cmaxc.ap, identity=identf.ap[0:32, 0:32]),
              reads=[cmaxc, identf], writes=[prA])
            E("dve", lambda e: e.tensor_copy(out=rowA.ap, in_=prA.ap[0:1, :]), reads=[prA], writes=[rowA])
            for h in range(4):
                E("dve", lambda e, h=h: e.tensor_tensor_scan(out=MN.ap[0:1, h * 8:(h + 1) * 8], data0=rowA.ap[0:1, h * 8:(h + 1) * 8],
                                                             data1=rowB.ap[0:1, h * 8:(h + 1) * 8], initial=mcar.ap[0:1, h:h + 1],
                                                             op0=ALU.max, op1=ALU.subtract),
                  reads=[rowA, rowB, mcar], writes=[MN] if h == 0 else [], pwrites=[] if h == 0 else [MN])
            MN3 = MN.ap.rearrange("p (h j) -> p h j", h=4); MP3 = MPV.ap.rearrange("p (h j) -> p h j", h=4)
            E("dve", lambda e: e.tensor_copy(out=MP3[:, :, 0:1], in_=mcar.ap[:, :, None]), reads=[mcar], writes=[MPV])
            E("dve", lambda e: e.tensor_copy(out=MP3[:, :, 1:8], in_=MN3[:, :, 0:7]), reads=[MN], pwrites=[MPV])
            E("dve", lambda e: e.tensor_tensor(out=RG.ap[0:1, 0:32], in0=MN.ap, in1=rowB.ap, op=ALU.add), reads=[MN, rowB], writes=[RG])
            E("dve", lambda e: e.tensor_tensor(out=rtmp.ap, in0=MPV.ap, in1=RG.ap[0:1, 0:32], op=ALU.subtract),
              reads=[MPV, RG], writes=[rtmp])
            E("act", lambda e: e.activation(out=RG.ap[0:1, 32:64], in_=rtmp.ap, func=AF.Exp), reads=[rtmp], pwrites=[RG])
            E("dve", lambda e: e.tensor_copy(out=mcar.ap[:, :, None], in_=MN3[:, :, 7:8]), reads=[MN, MPV], writes=[mcar])
            if pp == 1:
                out_dmas.append(DMA("sp", mp_o, mcar.ap, reads=[mcar]))
            yield
            pbc = pb(7, 256, 64)
            E("pe", lambda e: e.matmul(pbc.ap, lhsT=onesf.ap[0:1, :], rhs=RG.ap[0:1, :], start=True, stop=True),
              reads=[onesf, RG], writes=[pbc])
            E("dve", lambda e: e.tensor_tensor(out=f32v(A1), in0=f32v(Csb), in1=pbc.ap[:, 0:32], op=ALU.subtract),
              reads=[Csb, pbc], writes=[A1])
            E("act", lambda e: e.activation(out=WS.ap, in_=A1.ap, func=AF.Exp, bias=nln16.ap[:, 0:1]), reads=[A1, nln16], writes=[WS])
            E("dve", lambda e: e.tensor_tensor(out=f32v(E1), in0=f32v(BNs), in1=pbc.ap[:, 0:32], op=ALU.subtract),
              reads=[BNs, pbc], writes=[E1])
            E("act", lambda e: e.activation(out=FL.ap, in_=E1.ap, func=AF.Exp), reads=[E1], writes=[FL])
            E("dve", lambda e: e.tensor_copy(out=f32v(GSb), in_=pbc.ap[:, 32:64]), reads=[pbc], writes=[GSb])

            if pp == 0:
                lis = G.ap[0:16, 0:4, 8]; lfs = G.ap[0:16, 4:8, 8]
                sgv = lambda a, b: sg.ap[0:16, a:b]
                E("act", lambda e: e.activation(out=sgv(0, 4), in_=lfs, func=AF.Exp, scale=-1.0), reads=[G], pwrites=[sg])
                E("act", lambda e: e.activation(out=sgv(4, 8), in_=sgv(0, 4), func=AF.Ln, bias=onec.ap[0:16, 0:1]),
                  reads=[sg, onec], pwrites=[sg])
                E("dve", lambda e: e.tensor_tensor(out=sgv(8, 12), in0=lis, in1=sgv(4, 8), op=ALU.add), reads=[G, sg], pwrites=[sg])
                E("dve", lambda e: e.tensor_tensor(out=sgv(12, 16), in0=sgv(8, 12), in1=m_s.ap, op=ALU.max), reads=[sg, m_s], pwrites=[sg])
                E("dve", lambda e: e.tensor_tensor(out=sgv(16, 20), in0=sgv(12, 16), in1=sgv(4, 8), op=ALU.subtract),
                  reads=[sg], pwrites=[sg])
                out_dmas.append(DMA("sp", ms_o, sgv(16, 20), reads=[sg]))
                E("dve", lambda e: e.tensor_tensor(out=sgv(20, 24), in0=sgv(8, 12), in1=sgv(12, 16), op=ALU.subtract),
                  reads=[sg], pwrites=[sg])
                E("act", lambda e: e.activation(out=S12.ap[:, 0:4], in_=sgv(20, 24), func=AF.Exp, bias=nln16.ap[0:16, 0:1]),
                  reads=[sg, nln16], pwrites=[S12])
                E("dve", lambda e: e.tensor_tensor(out=sgv(24, 28), in0=m_s.ap, in1=sgv(12, 16), op=ALU.subtract),
                  reads=[sg, m_s], pwrites=[sg])
                E("act", lambda e: e.activation(out=S12.ap[:, 4:8], in_=sgv(24, 28), func=AF.Exp), reads=[sg], pwrites=[S12])
                E("act", lambda e: e.activation(out=sgv(28, 32), in_=sgv(16, 20), func=AF.Exp, scale=-1.0), reads=[sg], pwrites=[sg])
                E("dve", lambda e: e.tensor_tensor(out=tmp_s.ap[0:16, :], in0=q_sf.ap[0:16, :], in1=k_sf.ap[0:16, :], op=ALU.mult),
                  reads=[q_sf, k_sf], writes=[tmp_s])
                E("dve", lambda e: e.tensor_reduce(out=sgv(32, 36), in_=tmp_s.ap[0:16, :].rearrange("p (h d) -> p h d", h=4),
                                                   axis=AX.X, op=ALU.add), reads=[tmp_s], pwrites=[sg])
                E("dve", lambda e: e.tensor_tensor(out=tmp_s.ap[0:16, :], in0=q_sf.ap[0:16, :], in1=n_sf.ap[0:16, :], op=ALU.mult),
                  reads=[q_sf, n_sf, sg], writes=[tmp_s])
                E("dve", lambda e: e.tensor_reduce(out=sgv(36, 40), in_=tmp_s.ap[0:16, :].rearrange("p (h d) -> p h d", h=4),
                                                   axis=AX.X, op=ALU.add), reads=[tmp_s], pwrites=[sg])
                E("dve", lambda e: e.tensor_tensor(out=S12.ap[:, 8:12], in0=S12.ap[:, 0:4], in1=sgv(32, 36), op=ALU.mult),
                  reads=[S12, sg], pwrites=[S12])
                E("dve", lambda e: e.tensor_tensor(out=sgv(40, 44), in0=S12.ap[:, 4:8], in1=sgv(36, 40), op=ALU.mult),
                  reads=[S12, sg], pwrites=[sg])
                E("dve", lambda e: e.tensor_tensor(out=sgv(44, 48), in0=sgv(40, 44), in1=S12.ap[:, 8:12], op=ALU.add),
                  reads=[S12, sg], pwrites=[sg])
                E("dve", lambda e: e.scalar_tensor_tensor(out=sgv(48, 52), in0=sgv(44, 48), scalar=-1.0, in1=sgv(44, 48),
                                                          op0=ALU.mult, op1=ALU.max), reads=[sg], pwrites=[sg])
                E("dve", lambda e: e.tensor_tensor(out=sgv(52, 56), in0=sgv(48, 52), in1=sgv(28, 32), op=ALU.max), reads=[sg], pwrites=[sg])
                E("dve", lambda e: e.reciprocal(out=rinv_s.ap, in_=sgv(52, 56)), reads=[sg], writes=[rinv_s])
                n3 = lambda b: b.ap[0:16, :].rearrange("p (h d) -> p h d", h=4)
                E("dve", lambda e: e.tensor_tensor(out=n3(n_sf), in0=n3(n_sf), in1=S12.ap[:, 4:8, None].broadcast_to([16, 4, 256]),
                                                   op=ALU.mult), reads=[n_sf, S12, tmp_s], writes=[n_sf])
                E("dve", lambda e: e.tensor_tensor(out=n3(tmp_s), in0=n3(k_sf), in1=S12.ap[:, 0:4, None].broadcast_to([16, 4, 256]),
                                                   op=ALU.mult), reads=[k_sf, S12], writes=[tmp_s])
                E("dve", lambda e: e.tensor_tensor(out=n_sf.ap[0:16, :], in0=n_sf.ap[0:16, :], in1=tmp_s.ap[0:16, :], op=ALU.add),
                  reads=[n_sf, tmp_s], writes=[n_sf])
                out_dmas.append(DMA("sp", ns_o, n_sf.ap[0:16, :], reads=[n_sf]))
                E("dve", lambda e: e.tensor_tensor(out=BD.ap, in0=S12.ap[:, None, :].broadcast_to([16, 16, 12]),
                                                   in1=identf.ap[0:16, 0:16, None].broadcast_to([16, 16, 12]), op=ALU.mult),
                  reads=[S12, identf], writes=[BD])
                pbd = pb(6, 192, 192)
                yield
                E("pe", lambda e: e.matmul(pbd.ap, lhsT=onesf.ap[0:16, :], rhs=BD.ap.rearrange("p b q -> p (b q)"),
                                           start=True, stop=True), reads=[onesf, BD], writes=[pbd])
                E("act", lambda e: e.copy(out=bcs.ap.rearrange("p b q -> p (b q)"), in_=pbd.ap), reads=[pbd], writes=[bcs])


            yield
        gm = gate_math()

        sigo = ar(1, 0, 8192); sgo = sigo.ap.rearrange("p (i f) -> p i f", i=8)
        for bi, blk in enumerate((8, 9)):
            wb = load_wblk(blk)
            for i in range(nt):
                tp, c0 = tinfo(i)
                ps = projA(lambda k, c0=c0, tp=tp: hTv[:, k, c0:c0 + tp], 8, lambda k, wb=wb: wb.ap[:, k, :], tp, [hgrp(c0)], [wb])
                if i < 8:
                    evac(sgo[:tp, i, bi * 512:(bi + 1) * 512], ps.ap[:tp, :], [ps], pw=sigo, eng="act", func=AF.Sigmoid)
                else:
                    evac(sgs.ap[:tp, bi * 512:(bi + 1) * 512], ps.ap[:tp, :], [ps], pw=sgs, eng="act", func=AF.Sigmoid)
                if i % 2 == 1:
                    next(gm, None)
        szb = ar(6, 0, 8 * TC); szv = V3(szb, 8)

        def sxs_szg(c):
            E("dve", lambda e: e.scalar_tensor_tensor(out=xcv[:, c, 0:NC_], in0=xcv[:, c, 0:NC_], scalar=skipT.ap[:, c:c + 1],
                                                      in1=szv[:, c, 0:NC_], op0=ALU.mult, op1=ALU.mult),
              reads=[xcT, skipT, szb], pwrites=[xcT], war=[xcT])
            E("dve", lambda e: e.tensor_scalar(out=szv[:, c, 0:NC_], in0=szv[:, c, 0:NC_], scalar1=gngT.ap[:, c:c + 1],
                                               scalar2=None, op0=ALU.mult), reads=[szb, gngT, xcT], pwrites=[szb])

        for bi, blk in enumerate((10, 11)):
            wb = load_wblk(blk)
            for cc in range(4):
                c = bi * 4 + cc

                def outf(ps, c0, n, c=c):
                    evac(szv[:, c, c0:c0 + n], ps.ap[:, 0:n], [ps], pw=szb, eng="act", func=AF.Silu)
                projB(pp, lambda k, cc=cc, wb=wb: wb.ap[:, k, cc * 128:(cc + 1) * 128], 8,
                      lambda k, c0, n: hTv[:, k, c0:c0 + n], outf, [wb], [], rb_fn=hgrp)
                if c >= 1:
                    sxs_szg(c - 1)
        sxs_szg(7)
        for _ in gm:
            pass

        ybT = ar(0, 0, 8 * TC); ybv = V3(ybT, 8)
        SKs = [pb(0, 0, 512), pb(1, 0, 512)]
        KTs = [pb(6, 0, 512)]
        NHs = [pb(2, 0, 512), pb(3, 0, 512)]
        HTs = [pb(7, 0, 512)]
        Us = [pb(4, 0, 512), pb(5, 0, 512)]
        o = 0
        PTbs = [fs(o + i * 64, 64, BF16) for i in range(2)]; o += 128
        KWbs = [fs(o + i * 128, 128, BF16) for i in range(2)]; o += 256
        ABbs = [fs(o + i * 257, 257, BF16) for i in range(3)]; o += 771
        HBfs = [fs(o + i * 256, 256) for i in range(3)]; o += 768
        HBNs = [fs(o + i * 128, 128, BF16) for i in range(2)]; o += 256
        Yts = [fs(o + i * 256, 256) for i in range(2)]; o += 512
        rvs = [fs(o + i * 16, 16) for i in range(3)]; o += 48

        def post1a(tp, num_b, den_ap, fl_ap, fl_b, rv):
            E("act", lambda e: e.activation(out=rv.ap[:tp, 0:1], in_=den_ap, func=AF.Abs), reads=[num_b], writes=[rv])
            E("dve", lambda e: e.tensor_tensor(out=rv.ap[:tp, 2:3], in0=rv.ap[:tp, 0:1], in1=fl_ap, op=ALU.max),
              reads=[rv, fl_b], pwrites=[rv])
            E("pool", lambda e: e.tensor_tensor(out=rv.ap[:tp, 1:2], in0=rv.ap[:tp, 2:3], in1=monec.ap[:tp, 0:1], op=ALU.pow),
              reads=[rv, monec], pwrites=[rv])

        def post(it, tp, num_ap, num_b, den_ap, fl_ap, fl_b, sig_ap, h, c0, rinv_ap=None, rinv_b=None, sig_b=None, part=0):
            sig_b = sig_b or sigo
            rv = rvs[it % 3]; HBf = HBfs[it % 3]; HBN = HBNs[it % 2]; Yt = Yts[it % 2]
            hbtb = HTs[0]; hbt_ap = hbtb.ap[:, 0:128].bitcast(BF16)
            if part == 4:
                post1a(tp, num_b, den_ap, fl_ap, fl_b, rv)
            if part == 1:
                rinv_ap = rv.ap[:tp, 1:2]; rinv_b = rv
            if part in (0, 1):
                post1(tp, num_ap, num_b, den_ap, fl_ap, fl_b, sig_ap, rinv_ap, rinv_b, sig_b, rv, HBf, part)
            if part in (0, 2):
                post2(tp, rv, HBf, HBN, hbtb, hbt_ap)
            if part in (0, 3):
                post3(tp, h, c0, hbtb, hbt_ap, Yt)

        def post1(tp, num_ap, num_b, den_ap, fl_ap, fl_b, sig_ap, rinv_ap, rinv_b, sig_b, rv, HBf, part):
            E("dve", lambda e: e.scalar_tensor_tensor(out=HBf.ap[:tp], in0=num_ap, scalar=rinv_ap, in1=sig_ap,
                                                      op0=ALU.mult, op1=ALU.mult), reads=[num_b, rinv_b, sig_b], writes=[HBf])
            if part == 0:
                E("dve", lambda e: e.bn_stats(out=rv.ap[:tp, 4:10], in_=HBf.ap[:tp]), reads=[HBf], writes=[rv])
            else:
                E("dve", lambda e: e.bn_stats(out=rv.ap[:tp, 4:10], in_=HBf.ap[:tp]), reads=[HBf], pwrites=[rv])
            E("dve", lambda e: e.bn_aggr(out=rv.ap[:tp, 10:12], in_=rv.ap[:tp, 4:10]), reads=[rv], pwrites=[rv])
            E("pool", lambda e: e.tensor_tensor(out=rv.ap[:tp, 12:13], in0=rv.ap[:tp, 11:12], in1=epsc.ap[:tp, 0:1], op=ALU.add),
              reads=[rv, epsc], pwrites=[rv])
            E("pool", lambda e: e.tensor_tensor(out=rv.ap[:tp, 13:14], in0=rv.ap[:tp, 12:13], in1=mhalf.ap[:tp, 0:1], op=ALU.pow),
              reads=[rv, mhalf], pwrites=[rv])

        def post2(tp, rv, HBf, HBN, hbtb, hbt_ap):
            E("dve", lambda e: e.tensor_scalar(out=HBN.ap[:tp], in0=HBf.ap[:tp], scalar1=rv.ap[:tp, 10:11], scalar2=rv.ap[:tp, 13:14],
                                               op0=ALU.subtract, op1=ALU.mult), reads=[HBf, rv], writes=[HBN])
            for ec in range(2):
                E("pe", lambda e, ec=ec: e.transpose(out=hbt_ap[:, ec * 128:ec * 128 + tp], in_=HBN.ap[:tp, ec * 128:(ec + 1) * 128],
                                                     identity=ident.ap[:tp, :tp]), reads=[HBN, ident],
                  writes=[hbtb] if ec == 0 else [], pwrites=[] if ec == 0 else [hbtb])

        def post3(tp, h, c0, hbtb, hbt_ap, Yt):
            E("dve", lambda e: e.tensor_tensor(out=Yt.ap.rearrange("p (a t) -> p a t", a=2)[:, :, 0:tp],
                                               in0=hbt_ap.rearrange("p (a t) -> p a t", a=2)[:, :, 0:tp],
                                               in1=szv[:, 2 * h:2 * h + 2, c0:c0 + tp], op=ALU.mult), reads=[hbtb, szb], writes=[Yt])
            E("pool", lambda e: e.tensor_tensor(out=ybv[:, 2 * h:2 * h + 2, c0:c0 + tp],
                                                in0=Yt.ap.rearrange("p (a t) -> p a t", a=2)[:, :, 0:tp],
                                                in1=xcv[:, 2 * h:2 * h + 2, c0:c0 + tp], op=ALU.add), reads=[Yt, xcT], pwrites=[ybT])

        def stageA(it, j, h, part):
            jc = j * 128
            SK = SKs[it % 2]; PTb = PTbs[it % 2]; KWb = KWbs[it % 2]; ABb = ABbs[it % 3]
            KT = KTs[0]
            st_ap = SK.ap[:, 0:128]; ktr_ap = KT.ap[:, 0:128].bitcast(BF16); NU_ap = SK.ap[:, 256:258]
            NH = NHs[it % 2]; num_ap = NH.ap[:, 0:257]; U = Us[it % 2]
            wcol = WS.ap[:, h, j:j + 1]; gcol = GSb.ap[:, h, j:j + 1]
            if part == 2:
                return stageA2(it, j, h, jc, SK, PTb, KWb, ABb, KT, st_ap, ktr_ap, NU_ap, NH, num_ap, U, wcol, gcol)
            for ec in range(2):
                E("pe", lambda e, ec=ec: e.matmul(st_ap, lhsT=kv[:, 2 * h + ec, jc:jc + 128], rhs=qv[:, 2 * h + ec, jc:jc + 128],
                                                  start=(ec == 0), stop=(ec == 1)), reads=[kT, qT],
                  writes=[SK] if ec == 0 else [], pwrites=[] if ec == 0 else [SK])
            for dc in range(2):
                E("pe", lambda e, dc=dc: e.transpose(out=ktr_ap[:, dc * 128:(dc + 1) * 128], in_=kv[:, 2 * h + dc, jc:jc + 128],
                                                     identity=ident.ap), reads=[kT, ident],
                  writes=[KT] if dc == 0 else [], pwrites=[] if dc == 0 else [KT])

        def stageA2(it, j, h, jc, SK, PTb, KWb, ABb, KT, st_ap, ktr_ap, NU_ap, NH, num_ap, U, wcol, gcol):
            E("dve", lambda e: e.scalar_tensor_tensor(out=PTb.ap, in0=st_ap, scalar=wcol, in1=maskT.ap, op0=ALU.mult, op1=ALU.mult),
              reads=[SK, WS, maskT], writes=[PTb])
            E("act", lambda e: e.activation(out=KWb.ap, in_=ktr_ap, func=AF.Copy, scale=wcol), reads=[KT, WS], writes=[KWb])
            ab3 = ABb.ap.rearrange("p (a e) -> p a e", a=2)
            E("pe", lambda e: e.matmul(num_ap, lhsT=PTb.ap, rhs=vav[:, j, h, :], start=True, stop=False),
              reads=[PTb, vaug], writes=[NH])
            for dc in range(2):
                E("pe", lambda e, dc=dc: e.matmul(num_ap, lhsT=qv[:, 2 * h + dc, jc:jc + 128], rhs=ab3[:, dc, :],
                                                  start=False, stop=(dc == 1)), reads=[qT, ABb], pwrites=[NH])
            for dc in range(2):
                E("pe", lambda e, dc=dc: e.matmul(U.ap[:, dc * 256:(dc + 1) * 256], lhsT=KWb.ap[:, dc * 128:(dc + 1) * 128],
                                                  rhs=vav[:, j, h, 0:256], start=True, stop=True), reads=[KWb, vaug],
                  writes=[U] if dc == 0 else [], pwrites=[] if dc == 0 else [U])
            for dc in range(2):
                E("pe", lambda e, dc=dc: e.matmul(NU_ap[:, dc:dc + 1], lhsT=KWb.ap[:, dc * 128:(dc + 1) * 128],
                                                  rhs=onesb.ap[:, 0:1], start=True, stop=True), reads=[KWb, onesb], pwrites=[SK])

        def stageAb(it, j, h):
            ABb = ABbs[it % 3]
            gcol = GSb.ap[:, h, j:j + 1]
            ab3 = ABb.ap.rearrange("p (a e) -> p a e", a=2)
            E("act", lambda e: e.activation(out=ab3[:, :, 0:256], in_=CT.ap[:, h, :, :], func=AF.Copy, scale=gcol),
              reads=[CT, GSb], writes=[ABb])
            E("act", lambda e: e.activation(out=ab3[:, :, 256], in_=nT.ap[:, h, :], func=AF.Copy, scale=gcol),
              reads=[nT, GSb], pwrites=[ABb])

        def stageU(it, j, h):
            SK = SKs[it % 2]; ABb = ABbs[it % 3]; U = Us[it % 2]
            NU_ap = SK.ap[:, 256:258]
            gcol = GSb.ap[:, h, j:j + 1]
            E("dve", lambda e: e.scalar_tensor_tensor(out=CT.ap[:, h, :, :].rearrange("p a e -> p (a e)"),
                                                      in0=CT.ap[:, h, :, :].rearrange("p a e -> p (a e)"), scalar=gcol, in1=U.ap,
                                                      op0=ALU.mult, op1=ALU.add), reads=[CT, GSb, U, ABb], pwrites=[CT])
            E("dve", lambda e: e.scalar_tensor_tensor(out=nT.ap[:, h, :], in0=nT.ap[:, h, :], scalar=gcol, in1=NU_ap,
                                                      op0=ALU.mult, op1=ALU.add), reads=[nT, GSb, SK, ABb], pwrites=[nT])

        def stageB(it, j, h, part):
            num = NHs[it % 2]
            post(it, 128, num.ap[:, 0:256], num, num.ap[:, 256:257], FL.ap[:, h, j:j + 1], FL,
                 sgo[:, j, h * 256:(h + 1) * 256], h, j * 128, part=part)

        wvb = [load_wblk(2), load_wblk(3)]
        its = [(j, h) for j in range(8) for h in range(4)]
        nit = len(its)
        stageAb(0, *its[0])
        for t in range(nit + 3):
            if t < nit:
                stageA(t, *its[t], part=1)
            if 0 <= t - 3 < nit:
                stageB(t - 3, *its[t - 3], part=3)
            if 0 <= t - 2 < nit:
                stageB(t - 2, *its[t - 2], part=2)
            if t < nit:
                stageA(t, *its[t], part=2)
            if 0 <= t - 1 < nit:
                stageB(t - 1, *its[t - 1], part=1)
            if t < nit:
                stageU(t, *its[t])
                stageB(t, *its[t], part=4)
            if t + 1 < nit:
                stageAb(t + 1, *its[t + 1])

        hT2 = ar(1, 0, 8 * TC)
        h2v = V3(hT2, 8)
        DMA("sp", h2v[:, :, 0:NC_], hscr.rearrange("p (c t) -> p c t", c=8)[:, :, 0:NC_], reads=[hscrb], writes=[hT2])
        if pp == 1:
            Cst = ar(3, 0, 4096, F32)
            csv = Cst.ap.rearrange("p (h a d) -> p h a d", h=4, a=2)
            for h in range(4):
                for eh in range(2):
                    pt_ = pb((h * 2 + eh) % 2 + 2, 0, 256)
                    for dc in range(2):
                        E("pe", lambda e, h=h, eh=eh, dc=dc, pt_=pt_: e.transpose(
                            out=pt_.ap[:, dc * 128:(dc + 1) * 128], in_=CT.ap[:, h, dc, eh * 128:(eh + 1) * 128],
                            identity=identf.ap), reads=[CT, identf], writes=[pt_] if dc == 0 else [], pwrites=[] if dc == 0 else [pt_])
                    evac(csv[:, h, eh, :], pt_.ap, [pt_], pw=Cst)
            out_dmas.append(DMA("sp", Cp.rearrange("h (a p) d -> p h a d", p=128), csv, reads=[Cst]))
            out_dmas.append(DMA("sp", np_o.rearrange("h (a p) -> p h a", p=128), nT.ap, reads=[nT], allow_slow_non_contiguous=True))

        if pp == 0:
            NCB = 12
            Cins = [ar(3 + i // 8, (i % 8) * 1024, 1024, F32) for i in range(NCB)]
            qkb = fs(2752, 1024, BF16)
            junkc = fs(3776, 128, BF16)
            E("dve", lambda e: e.tensor_tensor(
                out=wvT.ap, in0=vTs.ap.rearrange("p (h a) b -> p h a b", h=4),
                in1=bcs.ap[:, :, 0:4].rearrange("p b h -> p h b")[:, :, None, :].broadcast_to([128, 4, 2, 16]),
                op=ALU.mult), reads=[vTs, bcs], writes=[wvT])
            units = [(b, h) for b in range(NS) for h in range(4)]
            pqs = {}

            def c_in(u):
                b, h = units[u]
                Cin = Cins[u % NCB]
                c3 = Cin.ap.rearrange("p (a d) -> p a d", a=2)
                DMA("sp", c3, C_in[b, h].rearrange("(a p) d -> p a d", p=128), writes=[Cin], key=rkey("ci", u, NCB))

            def c_nop(u):
                pass

            rgs = fs(3904, 64)
            E("dve", lambda e: e.reciprocal(out=rgs.ap.rearrange("p (b h) -> p b h", b=16), in_=bcs.ap[:, :, 4:8]),
              reads=[bcs], writes=[rgs])
            wvTp = fs(3968, 128)
            wp4 = wvTp.ap.rearrange("p (h a b) -> p h a b", h=4, a=2)
            E("dve", lambda e: e.tensor_tensor(
                out=wp4, in0=wvT.ap,
                in1=rgs.ap.rearrange("p (b h) -> p h b", b=16)[:, :, None, :].broadcast_to([128, 4, 2, 16]), op=ALU.mult),
              reads=[wvT, rgs], writes=[wvTp])
            for hb2 in range(2):
                pw2 = pb(hb2, 0, 512)
                for hh in range(2):
                    for a in range(2):
                        E("pe", lambda e, hb2=hb2, hh=hh, a=a, pw2=pw2: e.transpose(
                            out=pw2.ap[0:16, hh * 256 + a * 128:hh * 256 + (a + 1) * 128], in_=wp4[:, hb2 * 2 + hh, a, :],
                            identity=identf.ap), reads=[wvTp, identf],
                          writes=[pw2] if (hh == 0 and a == 0) else [], pwrites=[] if (hh == 0 and a == 0) else [pw2])
                evac(wvtok.ap[0:16, hb2 * 512:(hb2 + 1) * 512], pw2.ap[0:16, :], [pw2], pw=wvtok, eng="act")

            def c_s0(u):
                b, h = units[u]
                Cin = Cins[u % NCB]
                c3 = Cin.ap.rearrange("p (a d) -> p a d", a=2)
                if h == 0:
                    E("dve", lambda e: e.tensor_scalar(out=qkb.ap[0:16, :], in0=qks.ap.rearrange("p a f -> p (a f)"),
                                                       scalar1=identf.ap[0:16, b:b + 1], scalar2=None, op0=ALU.mult),
                      reads=[qks, identf], writes=[qkb])
                q3 = qkb.ap[0:16, :].rearrange("p (a f) -> p a f", a=2)
                pqa = pb(u % 4, 0, 256); pc = pb(4 + u % 3, 0, 512)
                pqs[u] = (pqa, pc)
                E("pe", lambda e: e.matmul(pqa.ap, lhsT=onesb.ap[0:16, :], rhs=q3[:, 0, h * 256:(h + 1) * 256], start=True, stop=True),
                  reads=[qkb, onesb], writes=[pqa])
                for a in range(2):
                    E("pe", lambda e, a=a: e.matmul(pc.ap[:, a * 256:(a + 1) * 256], lhsT=identf.ap, rhs=c3[:, a, :], start=True, stop=False),
                      reads=[identf, Cin], writes=[pc] if a == 0 else [], pwrites=[] if a == 0 else [pc])
                    E("pe", lambda e, a=a: e.matmul(pc.ap[:, a * 256:(a + 1) * 256],
                                                    lhsT=wvtok.ap[0:16, h * 256 + a * 128:h * 256 + (a + 1) * 128],
                                                    rhs=q3[:, 1, h * 256:(h + 1) * 256], start=False, stop=True),
                      reads=[wvtok, qkb], pwrites=[pc])

            def c_s1(u):
                b, h = units[u]
                Cin = Cins[u % NCB]; pqa, pc = pqs[u]
                c3 = Cin.ap.rearrange("p (a d) -> p a d", a=2)
                for a in range(2):
                    E("dve", lambda e, a=a: e.scalar_tensor_tensor(
                        out=junkc.ap, in0=c3[:, a, :], scalar=1.0, in1=pqa.ap, op0=ALU.mult, op1=ALU.mult,
                        accum_out=CqT.ap[:, h, a, b:b + 1]), reads=[Cin, pqa], writes=[junkc], pwrites=[CqT])

            def c_s2(u):
                b, h = units[u]
                Cin = Cins[u % NCB]; pqa, pc = pqs[u]
                c3 = Cin.ap.rearrange("p (a d) -> p a d", a=2)
                E("act", lambda e: e.activation(out=Cin.ap, in_=pc.ap, func=AF.Copy, scale=bcs.ap[:, b, 4 + h:5 + h]),
                  reads=[pc, bcs], writes=[Cin])
                out_dmas.append(DMA("sp", Cs[b, h].rearrange("(a p) d -> p a d", p=128), c3, reads=[Cin], key=rkey("co", u, NCB)))

            pipeline(len(units), [c_in, c_nop, c_nop, c_nop, c_nop, c_nop, c_s0, c_s1, c_s2])
            bq = lambda q: bcs.ap[:, :, q * 4:(q + 1) * 4].rearrange("p b h -> p h b")[:, :, None, :].broadcast_to([128, 4, 2, 16])
            E("dve", lambda e: e.tensor_tensor(out=numTs.ap, in0=vTs.ap.rearrange("p (h a) b -> p h a b", h=4), in1=bq(2), op=ALU.mult),
              reads=[vTs, bcs], writes=[numTs])
            E("dve", lambda e: e.tensor_tensor(out=CqT.ap, in0=CqT.ap, in1=bq(1), op=ALU.mult), reads=[CqT, bcs], writes=[CqT])
            E("dve", lambda e: e.tensor_tensor(out=numTs.ap, in0=numTs.ap, in1=CqT.ap, op=ALU.add), reads=[numTs, CqT], writes=[numTs])
            for hb2 in range(2):
                pn = pb(4 + hb2, 0, 512)
                for hh in range(2):
                    h = hb2 * 2 + hh
                    for a in range(2):
                        E("pe", lambda e, h=h, hh=hh, a=a, pn=pn: e.transpose(
                            out=pn.ap[0:16, hh * 256 + a * 128:hh * 256 + (a + 1) * 128], in_=numTs.ap[:, h, a, :], identity=identf.ap),
                          reads=[numTs, identf], writes=[pn] if (hh == 0 and a == 0) else [], pwrites=[] if (hh == 0 and a == 0) else [pn])
                for hh in range(2):
                    h = hb2 * 2 + hh
                    post(h, 16, pn.ap[0:16, hh * 256:(hh + 1) * 256], pn, None, None, None,
                         sgs.ap[0:16, h * 256:(h + 1) * 256], h, TH, rinv_ap=rinv_s.ap[:, h:h + 1], rinv_b=rinv_s, sig_b=sgs)

        lg = cs_(1); lb = cs_(2); bspb = cs_(0)
        DMA("sp", lg.ap, lnv_g.partition_broadcast(128), writes=[lg])
        DMA("sp", lb.ap, lnv_b.partition_broadcast(128), writes=[lb])
        VG = ar(4, 0, 16384, F32); vgv = VG.ap.rearrange("p (i f) -> p i f", i=8)
        vn = ar(2, 0, 8192); vnv = vn.ap.rearrange("p (i f) -> p i f", i=8)
        vsb = fs(2048, 1024)
        yaT = ar(3, 0, 8 * TC); yav = V3(yaT, 8)
        SQs = [fs(3072, 512), fs(3584, 512)]
        for bi, blk in enumerate((2, 3)):
            wb = wvb[bi]
            for i in range(nt):
                tp, c0 = tinfo(i)
                ps = projA(lambda k, c0=c0, tp=tp: h2v[:, k, c0:c0 + tp], 8, lambda k, wb=wb: wb.ap[:, k, :], tp, [hT2], [wb])
                vg_ap = vgv[:tp, i, bi * 512:(bi + 1) * 512] if i < 8 else vsb.ap[:tp, bi * 512:(bi + 1) * 512]
                VGb = VG if i < 8 else vsb
                evac(vg_ap, ps.ap[:tp, :], [ps], pw=VGb, eng="act", func=AF.Gelu)
                SQ = SQs[i % 2]
                E("dve", lambda e, vg_ap=vg_ap, SQ=SQ, tp=tp: e.tensor_tensor(out=SQ.ap[:tp], in0=vg_ap, in1=vg_ap, op=ALU.mult),
                  reads=[VGb], writes=[SQ])
                E("dve", lambda e, vg_ap=vg_ap, tp=tp, i=i, bi=bi: e.tensor_reduce(
                    out=S1.ap[:tp, i, bi * 4:(bi + 1) * 4], in_=vg_ap.rearrange("p (h d) -> p h d", h=4), axis=AX.X, op=ALU.add),
                  reads=[VGb], pwrites=[S1])
                E("dve", lambda e, SQ=SQ, tp=tp, i=i, bi=bi: e.tensor_reduce(
                    out=S2.ap[:tp, i, bi * 4:(bi + 1) * 4], in_=SQ.ap[:tp].rearrange("p (h d) -> p h d", h=4), axis=AX.X, op=ALU.add),
                  reads=[SQ], pwrites=[S2])
        for bi, blk in enumerate((0, 1)):
            wb = load_wblk(blk)
            for cc in range(4):
                c = bi * 4 + cc

                def outf(ps, c0, n, c=c):
                    evac(yav[:, c, c0:c0 + n], ps.ap[:, 0:n], [ps], pw=yaT, eng="act", func=AF.Gelu)
                projB(pp, lambda k, cc=cc, wb=wb: wb.ap[:, k, cc * 128:(cc + 1) * 128], 8,
                      lambda k, c0, n: h2v[:, k, c0:c0 + n], outf, [wb], [hT2])
        nst = nt * 8
        fl2 = lambda b: b.ap.rearrange("p i g -> p (i g)")[:, 0:nst]
        cb128 = c128.ap[:, 0:1].broadcast_to([128, nst])
        E("pool", lambda e: e.tensor_tensor(out=fl2(MEAN), in0=fl2(S1), in1=cb128, op=ALU.mult), reads=[S1, c128], writes=[MEAN])
        E("pool", lambda e: e.tensor_tensor(out=fl2(S1), in0=fl2(MEAN), in1=fl2(MEAN), op=ALU.mult), reads=[MEAN], writes=[S1])
        E("pool", lambda e: e.tensor_tensor(out=fl2(S2), in0=fl2(S2), in1=cb128, op=ALU.mult), reads=[S2, c128], writes=[S2])
        E("pool", lambda e: e.tensor_tensor(out=fl2(S2), in0=fl2(S2), in1=fl2(S1), op=ALU.subtract), reads=[S2, S1], writes=[S2])
        E("pool", lambda e: e.tensor_tensor(out=fl2(S2), in0=fl2(S2), in1=epsc.ap[:, 0:1].broadcast_to([128, nst]), op=ALU.add),
          reads=[S2, epsc], writes=[S2])
        E("pool", lambda e: e.tensor_tensor(out=fl2(RSTD), in0=fl2(S2), in1=mhalf.ap[:, 0:1].broadcast_to([128, nst]), op=ALU.pow),
          reads=[S2, mhalf], writes=[RSTD])
        wza = [load_wblk(4), load_wblk(5)]
        NTs = [fs(0, 1024), fs(1024, 1024)]
        vs_f = vsb
        for i in range(nt):
            tp, c0 = tinfo(i)
            NT_ = NTs[i % 2]
            n3_ = NT_.ap[:tp].rearrange("p (h d) -> p h d", h=8)
            vsrc = vgv[:tp, i, :] if i < 8 else vsb.ap[:tp, :]
            E("dve", lambda e, i=i, tp=tp, n3_=n3_, vsrc=vsrc: e.tensor_tensor(
                out=n3_, in0=vsrc.rearrange("p (h d) -> p h d", h=8),
                in1=MEAN.ap[:tp, i, :, None].broadcast_to([tp, 8, 128]), op=ALU.subtract), reads=[VG if i < 8 else vsb, MEAN], writes=[NT_])
            if i < 8:
                E("dve", lambda e, i=i, tp=tp, n3_=n3_: e.tensor_tensor(
                    out=vnv[:tp, i, :].rearrange("p (h d) -> p h d", h=8), in0=n3_,
                    in1=RSTD.ap[:tp, i, :, None].broadcast_to([tp, 8, 128]), op=ALU.mult), reads=[NT_, RSTD], pwrites=[vn])
            else:
                E("dve", lambda e, i=i, tp=tp, n3_=n3_: e.tensor_tensor(
                    out=n3_, in0=n3_, in1=RSTD.ap[:tp, i, :, None].broadcast_to([tp, 8, 128]), op=ALU.mult), reads=[NT_, RSTD], writes=[NT_])
                E("pool", lambda e, tp=tp, NT_=NT_: e.tensor_tensor(out=NT_.ap[:tp], in0=NT_.ap[:tp], in1=lg.ap[:tp], op=ALU.mult),
                  reads=[NT_, lg], writes=[NT_])
                E("pool", lambda e, tp=tp, NT_=NT_: e.tensor_tensor(out=vs_f.ap[:tp], in0=NT_.ap[:tp], in1=lb.ap[:tp], op=ALU.add),
                  reads=[NT_, lb], writes=[vs_f])
                E("pool", lambda e, i=i, tp=tp: e.tensor_copy(out=vns.ap[:tp, :], in_=vs_f.ap[:tp]), reads=[vs_f], writes=[vns])
                out_dmas.append(DMA("sp", vs_o, vs_f.ap[0:16, :], reads=[vs_f]))
        wo = ar(4, 0, 16 * 1024); wov = wo.ap.rearrange("p (k n) -> p k n", k=16)
        DMA("pool", wov, w_out.rearrange("(k p) n -> p k n", p=128), writes=[wo])
        wg = ar(6, 0, 8 * 1024); wgv = wg.ap.rearrange("p (k n) -> p k n", k=8)
        DMA("pool", wgv, w_pg.rearrange("(k p) n -> p k n", p=128), writes=[wg])
        DMA("sp", bspb.ap, b_sp.rearrange("h t -> (h t)").partition_broadcast(128), writes=[bspb])
        SZs = [fs(3072, 512), fs(3584, 512)]
        T1s = [fs(0, 512), fs(512, 512)]
        for hf in range(2):
            prs = pb(4 + hf, 0, 512)
            E("pe", lambda e, hf=hf, prs=prs: e.matmul(prs.ap, lhsT=onesb.ap, rhs=WT.ap.rearrange("p h t -> p (h t)")[:, hf * 512:(hf + 1) * 512],
                                                   start=True, stop=True), reads=[onesb, WT], writes=[prs])
            for cc in range(4):
                c = hf * 4 + cc
                E("dve", lambda e, c=c, cc=cc, prs=prs: e.scalar_tensor_tensor(
                    out=bspb.ap[:, c * 128:(c + 1) * 128], in0=prs.ap[:, cc * 128:(cc + 1) * 128], scalar=lbT.ap[:, c:c + 1],
                    in1=bspb.ap[:, c * 128:(c + 1) * 128], op0=ALU.mult, op1=ALU.add), reads=[prs, lbT, bspb], writes=[bspb])
        gi = 0
        for bi, blk in enumerate((4, 5)):
            wb = wza[bi]
            for cc in range(4):
                c = bi * 4 + cc
                for (c0, n) in colgroups(pp):
                    SZ = SZs[gi % 2]; T1 = T1s[gi % 2]; gi += 1
                    zps = pb(psrot[0] % 4, 0, 512); psrot[0] += 1
                    for k in range(8):
                        E("pe", lambda e, k=k, c0=c0, n=n, zps=zps, cc=cc, wb=wb: e.matmul(
                            zps.ap[:, 0:n], lhsT=wb.ap[:, k, cc * 128:(cc + 1) * 128], rhs=h2v[:, k, c0:c0 + n],
                            start=(k == 0), stop=(k == 7)), reads=[wb, hT2], writes=[zps] if k == 0 else [], pwrites=[] if k == 0 else [zps])
                    evac(SZ.ap[:, 0:n], zps.ap[:, 0:n], [zps], w=SZ, eng="act", func=AF.Silu)
                    sps = pb(4 + gi % 2, 0, 512)
                    if n == 512:
                        for jj in range(4):
                            j = c0 // 128 + jj
                            E("pe", lambda e, jj=jj, j=j, c=c, sps=sps: e.matmul(
                                sps.ap[:, jj * 128:(jj + 1) * 128], lhsT=vnv[:, j, c * 128:(c + 1) * 128], rhs=WT.ap[:, c, :],
                                start=True, stop=True), reads=[vn, WT], writes=[sps] if jj == 0 else [], pwrites=[] if jj == 0 else [sps])
                        E("dve", lambda e, sps=sps, T1=T1, c=c: e.scalar_tensor_tensor(
                            out=T1.ap.rearrange("p (j t) -> p j t", j=4), in0=sps.ap.rearrange("p (j t) -> p j t", j=4),
                            scalar=lgT.ap[:, c:c + 1],
                            in1=bspb.ap[:, None, c * 128:(c + 1) * 128].broadcast_to([128, 4, 128]), op0=ALU.mult, op1=ALU.add),
                          reads=[sps, bspb, lgT], writes=[T1])
                    else:
                        E("pe", lambda e, c=c, sps=sps: e.matmul(sps.ap[:, 0:16], lhsT=vns.ap[0:16, c * 128:(c + 1) * 128],
                                                                 rhs=Wdiag.ap[0:16, c, :], start=True, stop=True),
                          reads=[vns, Wdiag], writes=[sps])
                        E("dve", lambda e, sps=sps, T1=T1, c=c: e.tensor_scalar(out=T1.ap[:, 0:16], in0=sps.ap[:, 0:16],
                                                                                scalar1=bsp0.ap[:, c:c + 1], scalar2=None, op0=ALU.add),
                          reads=[sps, bsp0], writes=[T1])
                    E("dve", lambda e, c=c, c0=c0, n=n, SZ=SZ: e.tensor_tensor(out=yav[:, c, c0:c0 + n], in0=yav[:, c, c0:c0 + n],
                                                                               in1=SZ.ap[:, 0:n], op=ALU.mult),
                      reads=[yaT, SZ], pwrites=[yaT])
                    E("dve", lambda e, c=c, c0=c0, n=n, T1=T1: e.tensor_tensor(out=yav[:, c, c0:c0 + n], in0=yav[:, c, c0:c0 + n],
                                                                               in1=T1.ap[:, 0:n], op=ALU.mult),
                      reads=[yaT, T1], pwrites=[yaT])

        wp = ar(2, 0, 2 * 1024); wpv = wp.ap.rearrange("p (k n) -> p k n", k=2)
        DMA("pool", wpv, w_pp.rearrange("(k p) n -> p k n", p=128), writes=[wp])
        bpg = cs_(0); plg = cs_(1); fng = cs_(2)
        DMA("sp", bpg.ap, b_pg.partition_broadcast(128), writes=[bpg])
        DMA("sp", plg.ap, ple_g.partition_broadcast(128), writes=[plg])
        DMA("sp", fng.ap, fin_g.partition_broadcast(128), writes=[fng])
        xts = [fs(0, 1024), fs(1024, 1024)]
        X1s = [fs(2048, 1024), fs(3072, 1024)]
        pts = [ar(1, i * 512, 512, F32) for i in range(2)]
        X1b = ar(1, 1024, 1024); X1T = ar(1, 2048, 1024); ptb = ar(1, 3072, 256); PT2 = ar(1, 3328, 256)
        Gs = ar(1, 3584, 2048, F32); Ef = ar(1, 5632, 2048, F32)
        Yf = ar(2, 2048, 2048, F32); X2 = ar(2, 4096, 2048, F32); Tf = ar(2, 6144, 2048, F32)
        ptmp5 = [ptmps[0], ptmps[1]]
        PS = {}

        def f_L(i):
            tp, c0 = tinfo(i)
            xt = xts[i % 2]; pt = pts[i % 2]
            DMA("sp", xt.ap[:tp], xp[t0 + i * 128:t0 + (i + 1) * 128, :] if i < 8 else xs, writes=[xt], key=rkey("x5", i, 2))
            DMA("sp", pt.ap[:tp], pp_[t0 + i * 128:t0 + (i + 1) * 128, :] if i < 8 else psm, writes=[pt], key=rkey("p5", i, 2))

        def f_O(i, nbs=(0, 1)):
            tp, c0 = tinfo(i)
            if ("o", i) not in PS:
                PS[("o", i)] = [pb(0, 0, 512), pb(1, 0, 512)]
            pso = PS[("o", i)]
            for nb in nbs:
                for kc in range(16):
                    ysrc, yb_ = (yav, yaT) if kc < 8 else (ybv, ybT)
                    E("pe", lambda e, nb=nb, kc=kc, ysrc=ysrc: e.matmul(
                        pso[nb].ap[:tp, :], lhsT=ysrc[:, kc % 8, c0:c0 + tp], rhs=wov[:, kc, nb * 512:(nb + 1) * 512],
                        start=(kc == 0), stop=(kc == 15)), reads=[yb_, wo], writes=[pso[nb]] if kc == 0 else [],
                      pwrites=[] if kc == 0 else [pso[nb]])

        def f_X(i, nb):
            tp, c0 = tinfo(i)
            xt = xts[i % 2]; X1 = X1s[i % 2]; pso = PS[("o", i)]
            E("dve", lambda e: e.tensor_tensor(
                out=X1b.ap[:tp, nb * 512:(nb + 1) * 512], in0=pso[nb].ap[:tp, :], in1=xt.ap[:tp, nb * 512:(nb + 1) * 512], op=ALU.add),
              reads=[pso[nb], xt], writes=[X1b] if nb == 0 else [], pwrites=[] if nb == 0 else [X1b])
            E("dve", lambda e: e.tensor_tensor(
                out=X1.ap[:tp, nb * 512:(nb + 1) * 512], in0=pso[nb].ap[:tp, :], in1=xt.ap[:tp, nb * 512:(nb + 1) * 512], op=ALU.add),
              reads=[pso[nb], xt], writes=[X1] if nb == 0 else [], pwrites=[] if nb == 0 else [X1])

        def f_P(i):
            tp, c0 = tinfo(i)
            pt = pts[i % 2]
            E("dve", lambda e: e.tensor_copy(out=ptb.ap[:tp], in_=pt.ap[:tp]), reads=[pt], writes=[ptb])
            ppt = pb(7, 0, 128, BF16)
            for k in range(2):
                E("pe", lambda e, k=k: e.transpose(out=ppt.ap[:, k * 128:k * 128 + tp], in_=ptb.ap[:tp, k * 128:(k + 1) * 128],
                                                   identity=ident.ap[:tp, :tp]), reads=[ptb, ident],
                  writes=[ppt] if k == 0 else [], pwrites=[] if k == 0 else [ppt])
            evac(PT2.ap, ppt.ap, [ppt], w=PT2, eng="dve")
            pse = [pb(4, 0, 512), pb(5, 0, 512)]
            PS[("e", i)] = pse
            for nb in range(2):
                for k in range(2):
                    E("pe", lambda e, nb=nb, k=k: e.matmul(
                        pse[nb].ap[:tp, :], lhsT=PT2.ap[:, k * 128:k * 128 + tp], rhs=wpv[:, k, nb * 512:(nb + 1) * 512],
                        start=(k == 0), stop=(k == 1)), reads=[PT2, wp], writes=[pse[nb]] if k == 0 else [], pwrites=[] if k == 0 else [pse[nb]])

            for nb in range(2):
                E("act", lambda e, nb=nb: e.activation(
                    out=X2.ap[:tp, nb * 512:(nb + 1) * 512], in_=pse[nb].ap[:tp, :], func=AF.Square,
                    accum_out=stat2.ap[:tp, 2 * i + nb:2 * i + nb + 1]), reads=[pse[nb]],
                  pwrites=[stat2] if nb == 0 else [stat2, X2], writes=[X2] if nb == 0 else [])
            E("pool", lambda e: e.tensor_tensor(out=stat2.ap[:tp, 40 + i:41 + i], in0=stat2.ap[:tp, 2 * i:2 * i + 1],
                                                in1=stat2.ap[:tp, 2 * i + 1:2 * i + 2], op=ALU.add), reads=[stat2], pwrites=[stat2])
            rstd_pool(stat2.ap[:tp, 50 + i:51 + i], stat2.ap[:tp, 40 + i:41 + i], c1024, 1, tp, [stat2], stat2, ptmp5[0])

        def f_T(i, hf):
            tp, c0 = tinfo(i)
            if hf == 0:
                PS[("xt", i)] = pb(6, 0, 512, BF16)
                PS[("g", i)] = [pb(2, 0, 512), pb(3, 0, 512)]
            pxt = PS[("xt", i)]; psg = PS[("g", i)]
            ks = range(hf * 4, hf * 4 + 4)
            for k in ks:
                E("pe", lambda e, k=k: e.transpose(out=pxt.ap[:, k * 128:k * 128 + tp], in_=X1b.ap[:tp, k * 128:(k + 1) * 128],
                                                   identity=ident.ap[:tp, :tp]), reads=[X1b, ident],
                  writes=[pxt] if k == 0 else [], pwrites=[] if k == 0 else [pxt])
            E("dve", lambda e: e.tensor_copy(out=X1T.ap[:, hf * 512:(hf + 1) * 512], in_=pxt.ap[:, hf * 512:(hf + 1) * 512]),
              reads=[pxt], writes=[X1T] if hf == 0 else [], pwrites=[] if hf == 0 else [X1T])
            for nb in range(2):
                for k in ks:
                    E("pe", lambda e, nb=nb, k=k: e.matmul(
                        psg[nb].ap[:tp, :], lhsT=X1T.ap[:, k * 128:k * 128 + tp], rhs=wgv[:, k, nb * 512:(nb + 1) * 512],
                        start=(k == 0), stop=(k == 7)), reads=[X1T, wg], writes=[psg[nb]] if k == 0 else [], pwrites=[] if k == 0 else [psg[nb]])

        def f_G(i):
            tp, c0 = tinfo(i)
            psg = PS[("g", i)]
            for nb in range(2):
                E("dve", lambda e, nb=nb: e.tensor_tensor(
                    out=Tf.ap[:tp, nb * 512:(nb + 1) * 512], in0=psg[nb].ap[:tp, :], in1=bpg.ap[:tp, nb * 512:(nb + 1) * 512], op=ALU.add),
                  reads=[psg[nb], bpg], writes=[Tf] if nb == 0 else [], pwrites=[] if nb == 0 else [Tf])

        def f_E(i):
            tp, c0 = tinfo(i)
            pse = PS[("e", i)]
            for nb in range(2):
                E("dve", lambda e, nb=nb: e.scalar_tensor_tensor(
                    out=Ef.ap[:tp, nb * 512:(nb + 1) * 512], in0=pse[nb].ap[:tp, :], scalar=stat2.ap[:tp, 50 + i:51 + i],
                    in1=plg.ap[:tp, nb * 512:(nb + 1) * 512], op0=ALU.mult, op1=ALU.mult), reads=[pse[nb], stat2, plg],
                  writes=[Ef] if nb == 0 else [], pwrites=[] if nb == 0 else [Ef])
            E("act", lambda e: e.activation(out=Gs.ap[:tp], in_=Tf.ap[:tp], func=AF.Sigmoid), reads=[Tf], writes=[Gs])

        def f_R(i):
            tp, c0 = tinfo(i)
            X1 = X1s[i % 2]
            E("dve", lambda e: e.tensor_tensor(out=Gs.ap[:tp], in0=Gs.ap[:tp], in1=Ef.ap[:tp], op=ALU.mult), reads=[Gs, Ef], writes=[Gs])
            E("dve", lambda e: e.tensor_tensor(out=X2.ap[:tp], in0=X1.ap[:tp], in1=Gs.ap[:tp], op=ALU.add),
              reads=[X1, Gs], writes=[X2])
            E("act", lambda e: e.activation(out=Ef.ap[:tp], in_=X2.ap[:tp], func=AF.Square,
                                            accum_out=stat2.ap[:tp, 20 + i:21 + i]), reads=[X2], pwrites=[stat2], writes=[Ef])
            rstd_pool(stat2.ap[:tp, 30 + i:31 + i], stat2.ap[:tp, 20 + i:21 + i], c1024, 1, tp, [stat2], stat2, ptmp5[1])

        def f_Rb(i):
            tp, c0 = tinfo(i)
            E("dve", lambda e: e.scalar_tensor_tensor(out=Yf.ap[:tp], in0=X2.ap[:tp], scalar=stat2.ap[:tp, 30 + i:31 + i],
                                                      in1=fng.ap[:tp], op0=ALU.mult, op1=ALU.mult),
              reads=[X2, stat2, fng], writes=[Yf])
            dst = y_p[t0 + i * 128:t0 + (i + 1) * 128, :] if i < 8 else y_s
            out_dmas.append(DMA("sp", dst, Yf.ap[:tp], reads=[Yf], key=rkey("yo", i, 2)))

        f_L(0)
        if nt > 1:
            f_L(1)
        for t in range(nt):
            f_O(t)
            f_X(t, 0)
            f_T(t, 0)
            f_X(t, 1)
            f_T(t, 1)
            if t >= 2:
                f_Rb(t - 2)
            if t >= 1:
                f_E(t - 1)
            f_P(t)
            if t + 2 < nt:
                f_L(t + 2)
            f_G(t)
            if t >= 1:
                f_R(t - 1)
        if nt >= 2:
            f_Rb(nt - 2)
        f_E(nt - 1)
        f_R(nt - 1)
        f_Rb(nt - 1)

    if limit is not None:
        while True:
            pe_kept = [o for o in P.ops["pe"] if o.seq <= limit]
            if pe_kept and pe_kept[-1].open_group:
                limit += 1
            else:
                break
        for e_ in P.ENGS:
            P.ops[e_] = [o for o in P.ops[e_] if o.seq <= limit]
        out_dmas = [o for o in out_dmas if o.seq <= limit]
        for e_ in ("pe", "act", "dve", "pool"):
            if P.ops[e_]:
                out_dmas.append(P.ops[e_][-1])
        print("limit", limit, "of", Op._seq[0], {e_: len(P.ops[e_]) for e_ in P.ENGS})
    P.add("sp", lambda e: e.nop(), deps=out_dmas)
    P.emit()
    es.close()
    return nc


_NC_CACHE = {}


def kernel(**inp):
    f = lambda a: np.ascontiguousarray(np.asarray(a, dtype=np.float32))
    if "nc" not in _NC_CACHE:
        _NC_CACHE["nc"] = build()
    nc = _NC_CACHE["nc"]
    x_prompt = f(inp["x_prompt"]); x_sample = f(inp["x_sample"]).reshape(128, 1024)
    C = f(inp["state_mlstm_C"])[0]; n = f(inp["state_mlstm_n"])[0].reshape(128, 1024)
    m = f(inp["state_mlstm_m"])[0]; cv = f(inp["state_conv"])[0]
    p_prompt = f(inp["p_prompt"])[0]; p_sample = f(inp["p_sample"])[0].reshape(128, 256)
    shared = {
        "norm_g": f(inp["norm_in_g"])[0], "w_in": f(inp["w_in"])[0],
        "lnv_g": f(inp["ln_v_g"])[0].reshape(1024), "lnv_b": f(inp["ln_v_b"])[0].reshape(1024),
        "w_sp": f(inp["w_spatial"])[0], "b_sp": f(inp["b_spatial"])[0],
        "conv_w": f(inp["conv_w"])[0], "conv_b": f(inp["conv_b"])[0],
        "w_q": f(inp["w_q"])[0], "w_k": f(inp["w_k"])[0], "w_v": f(inp["w_v"])[0],
        "w_if": f(inp["w_if"])[0], "b_if": f(inp["b_if"])[0],
        "gn_g": f(inp["gn_g"])[0].reshape(1024), "skip": f(inp["skip"])[0],
        "w_out": f(inp["w_out"])[0], "w_pg": f(inp["w_ple_gate"])[0], "b_pg": f(inp["b_ple_gate"])[0],
        "w_pp": f(inp["w_ple_proj"])[0], "ple_g": f(inp["ple_norm_g"])[0], "fin_g": f(inp["final_norm_g"]),
    }
    in_maps = []
    for c in range(8):
        s = slice(c * 16, (c + 1) * 16)
        d = dict(shared)
        d.update({"xp": x_prompt[c], "xs": x_sample[s], "pp": p_prompt[c], "psm": p_sample[s],
                  "C_in": C[s], "n_in": n[s], "m_in": m[s], "cv_in": cv[s]})
        in_maps.append(d)
    res = run_bass_kernel_spmd(nc, in_maps, core_ids=list(range(8)))
    R = res.results
    cat = lambda k: np.concatenate([np.asarray(r[k]) for r in R], axis=0)
    stk = lambda k: np.stack([np.asarray(r[k]) for r in R], axis=0)
    y_prompt = stk("y_p")
    y_sample = cat("y_s").reshape(128, 1, 1024)
    Cp_ = stk("Cp")[None]
    np__ = stk("np_o")[None]
    mp_ = stk("mp_o").reshape(8, 4)[None]
    cvp_ = stk("cvp")[None]
    Cs_ = cat("Cs")[None]
    ns_ = cat("ns_o").reshape(128, 4, 256)[None]
    ms_ = cat("ms_o")[None]
    cvs_ = cat("cvs")[None]
    vs_ = cat("vs_o").reshape(128, 1, 1024)[None]
    return (y_prompt.astype(np.float32), y_sample.astype(np.float32), Cp_.astype(np.float32), np__.astype(np.float32),
            mp_.astype(np.float32), cvp_.astype(np.float32), Cs_.astype(np.float32), ns_.astype(np.float32),
            ms_.astype(np.float32), cvs_.astype(np.float32), vs_.astype(np.float32))
```

```python
import numpy as np
from contextlib import ExitStack
import concourse.bass as bass
import concourse.mybir as mybir
from concourse.bass_utils import run_bass_kernel_spmd

F32 = mybir.dt.float32
BF16 = mybir.dt.bfloat16
AF = mybir.ActivationFunctionType
ALU = mybir.AluOpType
AX = mybir.AxisListType

T = 2048
TH = 1024
NS = 16
TC = TH + NS
XW = TC + 3
SLOT = 8352
NSLOT = 7
LN16 = 2.772588722239781
EPS = 1e-6
DEBUG = {}


class Op:
    __slots__ = ("eng", "fn", "deps", "ticket", "needed", "dkey", "dval", "seq", "open_group")
    _seq = [0]

    def __init__(self, eng, fn, deps):
        Op._seq[0] += 1
        self.seq = Op._seq[0]
        self.eng = eng
        self.fn = fn
        self.deps = deps
        self.ticket = None
        self.needed = False
        self.dkey = None
        self.dval = None


class _Rec:
    def __getattr__(self, name):
        def f(*a, **k):
            self.__dict__["call"] = (name, a, k)
            return None
        return f


class Prog:
    ENGS = ("pe", "act", "dve", "pool", "sp")

    def __init__(self, nc):
        self.nc = nc
        self.ops = {e: [] for e in self.ENGS}
        self.dcount = {}

    def add(self, eng, fn, deps=()):
        dl = [d for d in deps if d is not None]
        rec = _Rec()
        fn(rec)
        name, a, k = rec.call
        fn = (lambda e, name=name, a=a, k=k: getattr(e, name)(*a, **k))
        op = Op(eng, fn, dl)
        op.open_group = (name == "matmul" and k.get("stop") is False)
        for d in dl:
            d.needed = True
        self.ops[eng].append(op)
        return op

    def dma(self, queue, out, in_, key, deps=(), **kw):
        def fn(e):
            return e.dma_start(out=out, in_=in_, **kw)

        op = self.add(queue, fn, deps)
        self.dcount[key] = self.dcount.get(key, 0) + 16
        op.dkey = key
        op.dval = self.dcount[key]
        return op

    def emit(self):
        nc = self.nc
        with ExitStack() as es:
            esem = {e: es.enter_context(nc.semaphore("s_" + e)) for e in self.ENGS}
            dsem = {k: es.enter_context(nc.semaphore("d_%s" % (k,))) for k in self.dcount}
            for e in self.ENGS:
                c = 0
                for op in self.ops[e]:
                    if op.dkey is None and op.needed:
                        c += 1
                        op.ticket = c
            block = es.enter_context(nc.Block())

            def run(ename, eng):
                waited = {}
                for op in self.ops[ename]:
                    for d in op.deps:
                        if d.dkey is not None:
                            s, v = dsem[d.dkey], d.dval
                        else:
                            s, v = esem[d.eng], d.ticket
                        if waited.get(s.name, 0) < v:
                            eng.wait_ge(s, v)
                            waited[s.name] = v
                    ins = op.fn(eng)
                    if op.dkey is not None:
                        ins.then_inc(dsem[op.dkey], 16)
                    elif op.needed:
                        ins.then_inc(esem[ename], 1)

            @block.tensor
            def _(e):
                run("pe", e)

            @block.scalar
            def _(e):
                run("act", e)

            @block.vector
            def _(e):
                run("dve", e)

            @block.gpsimd
            def _(e):
                run("pool", e)

            @block.sync
            def _(e):
                run("sp", e)


def _key(op):
    return ("d", op.dkey) if op.dkey is not None else ("e", op.eng)


class Buf:
    registry = []

    def __init__(self, ap, tname=None, lo=0, hi=0, psum=False):
        self.psum = psum
        self.ap = ap
        self.wr = {}
        self.rd = {}
        self.old = {}
        if tname is not None:
            for (tn, l, h, b) in Buf.registry:
                if tn == tname and l < hi and lo < h:
                    for d in (b.old, b.wr, b.rd):
                        for k, v in d.items():
                            self._put(self.old, k, v)
            Buf.registry.append((tname, lo, hi, self))

    @staticmethod
    def _put(d, k, op):
        cur = d.get(k)
        if cur is None:
            d[k] = op
        elif op.dkey is not None:
            if op.dval > cur.dval:
                d[k] = op
        elif op.seq > cur.seq:
            d[k] = op

    def __getitem__(self, idx):
        return self.ap[idx]

    def rdeps(self, eng):
        out = [op for k, op in self.wr.items() if not (eng == "pe" and k == ("e", "pe"))]
        if self.psum:
            out += [op for k, op in self.rd.items() if k != ("e", eng)]
        return out

    def wdeps(self, eng, partial=False):
        skip = ("e", "pe") if eng == "pe" else None
        out = [op for k, op in self.old.items() if k != skip]
        if not partial:
            out += [op for k, op in self.rd.items() if k != skip]
            out += [op for k, op in self.wr.items() if k != skip]
        return out

    def read(self, op):
        self._put(self.rd, _key(op), op)

    def wrote(self, op, partial=False):
        if not partial:
            self.wr = {}
            self.rd = {}
            self.old = {}
        self._put(self.wr, _key(op), op)


def build(limit=None):
    Buf.registry = []
    Op._seq[0] = 0
    nc = bass.Bass("TRN2", target_bir_lowering=False)
    P = Prog(nc)
    es = ExitStack()

    def din(name, shape):
        return nc.dram_tensor(name, list(shape), F32, kind="ExternalInput").ap()

    def dout(name, shape):
        return nc.dram_tensor(name, list(shape), F32, kind="ExternalOutput").ap()

    xp = din("xp", [T, 1024]); xs = din("xs", [NS, 1024])
    pp_ = din("pp", [T, 256]); psm = din("psm", [NS, 256])
    C_in = din("C_in", [NS, 4, 256, 256]); n_in = din("n_in", [NS, 1024]); m_in = din("m_in", [NS, 4])
    cv_in = din("cv_in", [NS, 3, 1024])
    norm_g = din("norm_g", [1024]); w_in = din("w_in", [1024, 6144])
    lnv_g = din("lnv_g", [1024]); lnv_b = din("lnv_b", [1024])
    w_sp = din("w_sp", [8, 128, 128]); b_sp = din("b_sp", [8, 128])
    conv_w = din("conv_w", [4, 1024]); conv_b = din("conv_b", [1024])
    w_q = din("w_q", [4, 256, 256]); w_k = din("w_k", [4, 256, 256]); w_v = din("w_v", [4, 256, 256])
    w_if = din("w_if", [3072, 8]); b_if = din("b_if", [8])
    gn_g = din("gn_g", [1024]); skip = din("skip", [1024])
    w_out = din("w_out", [2048, 1024]); w_pg = din("w_pg", [1024, 1024]); b_pg = din("b_pg", [1024])
    w_pp = din("w_pp", [256, 1024]); ple_g = din("ple_g", [1024]); fin_g = din("fin_g", [1024])

    y_p = dout("y_p", [T, 1024]); y_s = dout("y_s", [NS, 1024])
    Cp = dout("Cp", [4, 256, 256]); np_o = dout("np_o", [4, 256]); mp_o = dout("mp_o", [1, 4])
    cvp = dout("cvp", [3, 1024])
    Cs = dout("Cs", [NS, 4, 256, 256]); ns_o = dout("ns_o", [NS, 1024]); ms_o = dout("ms_o", [NS, 4])
    cvs = dout("cvs", [NS, 3, 1024]); vs_o = dout("vs_o", [NS, 1024])

    hscr = nc.dram_tensor("hscr", [128, 8 * TC], BF16, kind="Internal").ap()
    hscrb = Buf(hscr, "hscr", 0, 1)
    out_dmas = []
    cnt = [0]

    def sbt(shape, dt, name=None):
        cnt[0] += 1
        return es.enter_context(nc.sbuf_tensor(name or ("t%d" % cnt[0]), list(shape), dt))

    def newbuf(shape, dt):
        t = sbt(shape, dt)
        ap = t[:] if len(shape) == 2 else t[tuple(slice(None) for _ in shape)]
        b = Buf(ap, "nb%d" % cnt[0], 0, 1)
        b.tname = "nb%d" % cnt[0]
        return b

    def rebuf(b):
        nb_ = Buf(b.ap, b.tname, 0, 1)
        nb_.tname = b.tname
        return nb_

    def E(eng, fn, reads=(), writes=(), pwrites=(), extra=(), war=()):
        deps = list(extra)
        for b in war:
            deps += [op for k, op in b.rd.items()]
        for b in reads:
            deps += b.rdeps(eng)
        for b in writes:
            deps += b.wdeps(eng)
        for b in pwrites:
            deps += b.wdeps(eng, partial=True)
        op = P.add(eng, fn, deps)
        for b in reads:
            b.read(op)
        for b in writes:
            b.wrote(op)
        for b in pwrites:
            b.wrote(op, partial=True)
        return op

    dkc = [0]

    def DMA(queue, out, in_, reads=(), writes=(), pwrites=(), key=None, **kw):
        deps = []
        for b in reads:
            deps += b.rdeps(queue)
        for b in writes:
            deps += b.wdeps(queue)
        for b in pwrites:
            deps += b.wdeps(queue, partial=True)
        if key is None:
            dkc[0] += 1
            key = "k%d" % dkc[0]
        op = P.dma(queue, out, in_, key, deps, **kw)
        for b in reads:
            b.read(op)
        for b in writes:
            b.wrote(op)
        for b in pwrites:
            b.wrote(op, partial=True)
        return op

    def rkey(prefix, i, n):
        return "%s%d" % (prefix, i % n)

    arena = sbt([128, NSLOT * SLOT], BF16, "arena")
    FS = 4096
    fscr = sbt([128, FS], F32, "fscr")
    cst = sbt([128, 3, 1024], F32, "cst")

    def ar(slot, off, n, dt=BF16):
        lo = slot * SLOT + off
        ap = arena[:, lo:lo + n]
        if dt == F32:
            ap = ap.bitcast(F32)
        return Buf(ap, "arena", lo * 2, (lo + n) * 2)

    def fs(off, n, dt=F32):
        ap = fscr[:, off:off + n]
        if dt == BF16:
            ap = ap.bitcast(BF16)
        return Buf(ap, "fscr", off * 4, (off + n) * 4)

    def cs_(i):
        return Buf(cst[:, i, :], "cst", i * 4096, (i + 1) * 4096)

    banks = [es.enter_context(nc.psum_tensor("bank%d" % i, [128, 512], F32)) for i in range(8)]

    def pb(bank, off, n, dt=F32):
        ap = banks[bank][:, off:off + n]
        if dt == BF16:
            ap = ap.bitcast(BF16)
        return Buf(ap, "bank%d" % bank, 0, 2048, psum=True)

    ident = newbuf([128, 128], BF16); identf = newbuf([128, 128], F32)
    maskT = newbuf([128, 128], BF16); tri = newbuf([128, 128], F32)
    onesf = newbuf([128, 128], F32); onesb = newbuf([128, 128], BF16)
    epsc = newbuf([128, 1], F32); mhalf = newbuf([128, 1], F32); onec = newbuf([128, 1], F32)
    nln16 = newbuf([128, 1], F32)
    monec = newbuf([128, 1], F32)
    c1024 = newbuf([128, 1], F32); c128 = newbuf([128, 1], F32); c256 = newbuf([128, 1], F32)

    def memset(b, val, eng="pool"):
        return E(eng, lambda e: e.memset(b.ap, val), writes=[b])

    memset(identf, 1.0)
    E("pool", lambda e: e.affine_select(out=identf.ap, in_=identf.ap, pattern=[[-1, 128]], compare_op=ALU.is_equal,
                                        fill=0.0, base=0, channel_multiplier=1), reads=[identf], writes=[identf])
    E("pool", lambda e: e.tensor_copy(out=ident.ap, in_=identf.ap), reads=[identf], writes=[ident])
    memset(tri, 1.0)
    E("pool", lambda e: e.affine_select(out=tri.ap, in_=tri.ap, pattern=[[1, 128]], compare_op=ALU.is_ge,
                                        fill=0.0, base=0, channel_multiplier=-1), reads=[tri], writes=[tri])
    E("pool", lambda e: e.tensor_copy(out=maskT.ap, in_=tri.ap), reads=[tri], writes=[maskT])
    memset(onesf, 1.0); memset(onesb, 1.0)
    memset(epsc, EPS); memset(mhalf, -0.5); memset(onec, 1.0); memset(nln16, -LN16); memset(monec, -1.0)
    memset(c1024, 1.0 / 1024); memset(c128, 1.0 / 128); memset(c256, 1.0 / 256)

    def rstd_pool(out_ap, in_ap, cinv, n, tp, rbufs, wbuf, tmpb):
        def bc(c):
            return c.ap[:tp, 0:1] if n == 1 else c.ap[:tp, 0:1].broadcast_to([tp, n])
        E("pool", lambda e: e.tensor_tensor(out=tmpb.ap[:tp, 0:n], in0=in_ap, in1=bc(cinv), op=ALU.mult),
          reads=list(rbufs) + [cinv], writes=[tmpb])
        E("pool", lambda e: e.tensor_tensor(out=tmpb.ap[:tp, 0:n], in0=tmpb.ap[:tp, 0:n], in1=bc(epsc), op=ALU.add),
          reads=[tmpb, epsc], writes=[tmpb])
        return E("pool", lambda e: e.tensor_tensor(out=out_ap, in0=tmpb.ap[:tp, 0:n], in1=bc(mhalf), op=ALU.pow),
                 reads=[tmpb, mhalf], pwrites=[wbuf])

    wq = newbuf([128, 4, 2, 256], BF16); wk = newbuf([128, 4, 2, 256], BF16); wv = newbuf([128, 4, 2, 256], BF16)
    wif = newbuf([128, 24, 8], BF16)
    bifb = newbuf([128, 8], F32)
    gngT = newbuf([128, 8], F32); skipT = newbuf([128, 8], F32); cbT = newbuf([128, 8], F32)
    lgT = newbuf([128, 8], F32); lbT = newbuf([128, 8], F32)
    cwT = newbuf([128, 4, 8], F32)
    bsp0 = newbuf([128, 8], F32)
    W00 = newbuf([16, 8], F32)
    WT = newbuf([128, 8, 128], BF16)
    Wdiag = newbuf([16, 8, 16], BF16)

    def deferred_setup():
        for wb_, src in ((wq, w_q), (wk, w_k), (wv, w_v)):
            DMA("pool", wb_.ap, src.rearrange("h (kk p) e -> p h kk e", p=128), writes=[wb_])
        DMA("pool", wif.ap, w_if.rearrange("(k p) g -> p k g", p=128), writes=[wif])
        DMA("sp", bifb.ap, b_if.partition_broadcast(128), writes=[bifb])
        for b_, src in ((gngT, gn_g), (skipT, skip), (cbT, conv_b), (lgT, lnv_g), (lbT, lnv_b)):
            DMA("sp", b_.ap, src.rearrange("(c p) -> p c", p=128), writes=[b_], allow_slow_non_contiguous=True)
        DMA("sp", cwT.ap, conv_w.rearrange("j (c p) -> p j c", p=128), writes=[cwT], allow_slow_non_contiguous=True)
        DMA("sp", bsp0.ap, b_sp[:, 0].partition_broadcast(128), writes=[bsp0], allow_slow_non_contiguous=True)
        DMA("sp", W00.ap, w_sp[:, 0, 0].partition_broadcast(16), writes=[W00], allow_slow_non_contiguous=True)
        wspf = fs(0, 1024)
        DMA("sp", wspf.ap.rearrange("p (h s) -> p h s", h=8), w_sp.rearrange("h t s -> t h s"), writes=[wspf])
        wtf = fs(1024, 1024)
        for half in range(2):
            pw_ = pb(half, 0, 512)
            for hh in range(4):
                h = half * 4 + hh
                E("pe", lambda e, h=h, hh=hh, pw_=pw_: e.transpose(out=pw_.ap[:, hh * 128:(hh + 1) * 128],
                                                                   in_=wspf.ap[:, h * 128:(h + 1) * 128], identity=identf.ap),
                  reads=[wspf, identf], pwrites=[pw_])
            E("act", lambda e, half=half, pw_=pw_: e.copy(out=wtf.ap[:, half * 512:(half + 1) * 512], in_=pw_.ap),
              reads=[pw_], pwrites=[wtf])
        E("pool", lambda e: e.affine_select(out=WT.ap, in_=wtf.ap.rearrange("p (h t) -> p h t", h=8),
                                            pattern=[[0, 8], [1, 128]], compare_op=ALU.is_ge, fill=0.0, base=0,
                                            channel_multiplier=-1), reads=[wtf], writes=[WT])
        E("dve", lambda e: e.tensor_tensor(out=Wdiag.ap, in0=identf.ap[0:16, None, 0:16].broadcast_to([16, 8, 16]),
                                           in1=W00.ap[:, :, None].broadcast_to([16, 8, 16]), op=ALU.mult),
          reads=[identf, W00], writes=[Wdiag])

    CT = newbuf([128, 4, 2, 256], F32); nT = newbuf([128, 4, 2], F32)
    memset(CT, 0.0); memset(nT, 0.0)
    mcar = newbuf([1, 4], F32)
    memset(mcar, 0.0)
    tails = newbuf([128, 8, 3], BF16)
    memset(tails, 0.0)
    G_g = newbuf([128, 8, 9], F32)
    E1 = newbuf([128, 4, 8], F32); LFN = newbuf([128, 4, 8], F32); BNs = newbuf([128, 4, 8], F32)
    Csb = newbuf([128, 4, 8], F32); A1 = newbuf([128, 4, 8], F32)
    WS = newbuf([128, 4, 8], F32); FL = newbuf([128, 4, 8], F32); GSb = newbuf([128, 4, 8], F32)
    cmaxc = newbuf([32, 1], F32)
    rowA = newbuf([1, 32], F32); rowB = newbuf([1, 32], F32); MN = newbuf([1, 32], F32); MPV = newbuf([1, 32], F32)
    RG = newbuf([1, 64], F32); rtmp = newbuf([1, 32], F32)
    S12 = newbuf([16, 12], F32); sg = newbuf([16, 64], F32)
    m_s = newbuf([16, 4], F32)
    BD = newbuf([16, 16, 12], F32); bcs = newbuf([128, 16, 12], F32)
    rinv_s = newbuf([16, 4], F32)
    vTs = newbuf([128, 8, 16], F32); bufT = newbuf([128, 8, 3, 16], F32)
    qks = newbuf([16, 2, 1024], BF16)
    sgs = newbuf([16, 1024], BF16)
    vns = newbuf([16, 1024], BF16)
    CqT = newbuf([128, 4, 2, 16], F32); wvT = newbuf([128, 4, 2, 16], F32); numTs = newbuf([128, 4, 2, 16], F32)
    stat_g = newbuf([128, 64], F32)
    stat2_g = newbuf([128, 64], F32)
    ptmp = newbuf([128, 80], F32)
    ptmps = [newbuf([128, 2], F32), newbuf([128, 2], F32)]
    S1_g = newbuf([128, 9, 8], F32); S2_g = newbuf([128, 9, 8], F32); MEAN = newbuf([128, 9, 8], F32)
    RSTD = newbuf([128, 9, 8], F32)
    memset(S1_g, 0.0); memset(S2_g, 0.0)

    DMA("sp", m_s.ap, m_in, writes=[m_s])
    out_dmas.append(P.dma("sp", cvs[:, 0:2, :], cv_in[:, 1:3, :], "cvcp"))

    NWB = 2
    wblk = [newbuf([128, 8, 512], BF16) for _ in range(NWB)]
    wbi = [0]

    def load_wblk(blk):
        b = wblk[wbi[0] % NWB]
        DMA("pool", b.ap, w_in[:, blk * 512:(blk + 1) * 512].rearrange("(k p) n -> p k n", p=128), writes=[b],
            key=rkey("wb", wbi[0], NWB))
        wbi[0] += 1
        return b

    psrot = [0]

    def colgroups(pp):
        g = [(0, 512), (512, 512)]
        if pp == 0:
            g.append((1024, 16))
        return g

    def ntiles(pp):
        return 9 if pp == 0 else 8

    def tinfo(i):
        return (128, i * 128) if i < 8 else (NS, TH)

    evrot = [0]

    def evac(out_ap, in_ap, reads, pw=None, w=None, eng=None, func=None, scale=1.0, bias=None):
        if eng is None:
            eng = "act" if (func is not None or evrot[0] % 2 == 0) else "dve"
            evrot[0] += 1
        kw = dict(reads=list(reads), pwrites=[pw] if pw else [], writes=[w] if w else [])
        if eng == "act":
            f = func or AF.Copy
            if bias is not None:
                kw["reads"].append(bias[0])
                return E("act", lambda e: e.activation(out=out_ap, in_=in_ap, func=f, scale=scale, bias=bias[1]), **kw)
            return E("act", lambda e: e.activation(out=out_ap, in_=in_ap, func=f, scale=scale), **kw)
        return E(eng, lambda e: e.tensor_copy(out=out_ap, in_=in_ap), **kw)

    def pipeline(n, stages):
        S = len(stages)
        for t in range(n + S - 1):
            for s_ in reversed(range(S)):
                i = t - s_
                if 0 <= i < n:
                    stages[s_](i)

    def phase_hT(pp, hT, hook=None, hTg=None):
        t0 = pp * TH
        stat = rebuf(stat_g)
        gin = cs_(0)
        DMA("sp", gin.ap, norm_g.partition_broadcast(128), writes=[gin])
        NXB = 6
        xts = [ar(2 + i // 4, (i % 4) * 2048, 2048, F32) for i in range(NXB)]
        hbs = [fs(3072, 512, BF16), fs(3584, 512, BF16)]
        ptrs = [pb(0, 0, 512, BF16), pb(1, 0, 512, BF16)]
        pjunk = Buf(cst[:, 1, 0:512].bitcast(BF16), "cst", 4096, 4096 + 2048)
        hTv_ = hT.ap.rearrange("p (c t) -> p c t", c=8)

        def s0(i):
            tp, c0 = tinfo(i)
            xt = xts[i % NXB]
            src = xp[t0 + i * 128:t0 + (i + 1) * 128, :] if i < 8 else xs
            DMA("sp", xt.ap[:tp], src, writes=[xt], key=rkey("x", i, NXB))
            E("act", lambda e: e.activation(out=pjunk.ap[:tp], in_=xt.ap[:tp], func=AF.Square,
                                            accum_out=stat.ap[:tp, i:i + 1]), reads=[xt], pwrites=[stat], writes=[pjunk])
            rstd_pool(stat.ap[:tp, 16 + i:17 + i], stat.ap[:tp, i:i + 1], c1024, 1, tp, [stat], stat, ptmps[i % 2])
            if i == 2 and hook is not None:
                hook()

        def s1(i):
            tp, c0 = tinfo(i)
            xt = xts[i % NXB]; hb = hbs[i % 2]
            E("dve", lambda e: e.scalar_tensor_tensor(
                out=hb.ap[:tp], in0=xt.ap[:tp], scalar=stat.ap[:tp, 16 + i:17 + i], in1=gin.ap[:tp],
                op0=ALU.mult, op1=ALU.mult), reads=[xt, stat, gin], writes=[hb])

        def s2(i):
            tp, c0 = tinfo(i)
            hb = hbs[i % 2]; ptr = ptrs[i % 2]
            for k in range(8):
                E("pe", lambda e, k=k: e.transpose(
                    out=ptr.ap[:, k * 128:k * 128 + tp], in_=hb.ap[:tp, k * 128:(k + 1) * 128],
                    identity=ident.ap[:tp, :tp]), reads=[hb, ident], writes=[ptr] if k == 0 else [], pwrites=[] if k == 0 else [ptr])
            evac(hTv_[:, :, c0:c0 + tp], ptr.ap.rearrange("p (c t) -> p c t", c=8)[:, :, 0:tp], [ptr],
                 pw=hTg[i // 4 if i < 8 else 2])

        pipeline(ntiles(pp), [s0, s1, s2])

    def projB(pp, lhs_fn, nk, rhs_fn, out_fn, lbufs, rbufs, groups=None, rb_fn=None):
        for (c0, n) in (groups if groups is not None else colgroups(pp)):
            ps = pb(psrot[0] % 6, 0, 512); psrot[0] += 1
            rb = list(rbufs) if rb_fn is None else [rb_fn(c0)]
            for k in range(nk):
                E("pe", lambda e, k=k, c0=c0, n=n, ps=ps: e.matmul(ps.ap[:, 0:n], lhsT=lhs_fn(k), rhs=rhs_fn(k, c0, n),
                                                                   start=(k == 0), stop=(k == nk - 1)),
                  reads=list(lbufs) + rb, writes=[ps] if k == 0 else [], pwrites=[] if k == 0 else [ps])
            out_fn(ps, c0, n)

    def projA(lhs_fn, nk, rhs_fn, tp, lbufs, rbufs, bank=None):
        if bank is None:
            bank = psrot[0] % 6; psrot[0] += 1
        ps = pb(bank, 0, 512)
        for k in range(nk):
            E("pe", lambda e, k=k, ps=ps: e.matmul(ps.ap[:tp, :], lhsT=lhs_fn(k), rhs=rhs_fn(k),
                                                   start=(k == 0), stop=(k == nk - 1)),
              reads=list(lbufs) + list(rbufs), writes=[ps] if k == 0 else [], pwrites=[] if k == 0 else [ps])
        return ps

    V3 = lambda b, c: b.ap.rearrange("p (c t) -> p c t", c=c)

    for pp in range(2):
        t0 = pp * TH
        G = rebuf(G_g); stat2 = rebuf(stat2_g); S1 = rebuf(S1_g); S2 = rebuf(S2_g)
        NC_ = TC if pp == 0 else TH
        nt = ntiles(pp)
        wxb = []
        hT = ar(0, 0, 8 * TC)
        hTg = [Buf(hT.ap, "arena", 0, 8 * TC * 2) for _ in range(3)]
        hTl = hTg if pp == 0 else hTg[0:2]
        hgrp = lambda c0: hTg[min(c0 // 512, 2)]
        phase_hT(pp, hT, hook=lambda: wxb.extend([load_wblk(6), load_wblk(7)]), hTg=hTg)
        if pp == 0:
            deferred_setup()
        hTv = V3(hT, 8)
        hscrb = Buf(hscr, "hscr", 0, 1)
        DMA("sp", hscr.rearrange("p (c t) -> p c t", c=8)[:, :, 0:NC_], hTv[:, :, 0:NC_], reads=hTl, writes=[hscrb])
        xbT = ar(1, 0, 8 * XW); xbv = V3(xbT, 8)
        xcT = ar(2, 0, 8 * TC); xcv = V3(xcT, 8)
        qT = ar(3, 0, 8 * TC); qv = V3(qT, 8)
        kT = ar(4, 0, 8 * TC); kv = V3(kT, 8)
        vaug = ar(5, 0, 8 * 4 * 257); vav = vaug.ap.rearrange("p (i h e) -> p i h e", i=8, h=4)
        vT = ar(6, 0, 8 * TC); vv = V3(vT, 8)

        if pp == 0:
            cvtok = ar(6, 0, 6144, F32)
            DMA("sp", cvtok.ap[0:16, :], cv_in.rearrange("b j c -> b (j c)"), writes=[cvtok])
            pcv = pb(7, 0, 384)
            for j in range(3):
                for c in range(8):
                    idx = c * 3 + j
                    E("pe", lambda e, j=j, c=c, idx=idx: e.transpose(
                        out=pcv.ap[:, idx * 16:(idx + 1) * 16],
                        in_=cvtok.ap[0:16, j * 1024 + c * 128:j * 1024 + (c + 1) * 128], identity=identf.ap[0:16, 0:16]),
                      reads=[cvtok, identf], pwrites=[pcv])
            evac(bufT.ap.rearrange("p c j b -> p (c j b)"), pcv.ap, [pcv], w=bufT, eng="dve")
        E("dve", lambda e: e.tensor_copy(out=xbv[:, :, 0:3], in_=tails.ap), reads=[tails], pwrites=[xbT])
        xbtok = fs(3072, 1024)
        tok_tile = 8 if pp == 0 else 7
        for grp in colgroups(pp):
            for bi, blk in enumerate((6, 7)):
                wb = wxb[bi]
                for cc in range(4):
                    c = bi * 4 + cc

                    def outf(ps, c0, n, c=c):
                        evac(xbv[:, c, 3 + c0:3 + c0 + n], ps.ap[:, 0:n], [ps], pw=xbT)
                    projB(pp, lambda k, cc=cc, wb=wb: wb.ap[:, k, cc * 128:(cc + 1) * 128], 8,
                          lambda k, c0, n: hTv[:, k, c0:c0 + n], outf, [wb], [], groups=[grp], rb_fn=hgrp)
        for bi, blk in enumerate((6, 7)):
            wb = wxb[bi]
            tp, tc0 = tinfo(tok_tile)
            ps = projA(lambda k: hTv[:, k, tc0:tc0 + tp], 8, lambda k, wb=wb: wb.ap[:, k, :], tp, [hgrp(tc0)], [wb])
            evac(xbtok.ap[:tp, bi * 512:(bi + 1) * 512], ps.ap[:tp, :], [ps], pw=xbtok)
        if pp == 0:
            out_dmas.append(DMA("sp", cvs[:, 2, :], xbtok.ap[0:16, :], reads=[xbtok]))
        else:
            out_dmas.append(DMA("sp", cvp, xbtok.ap[125:128, :], reads=[xbtok]))
        if pp == 0:
            E("dve", lambda e: e.tensor_copy(out=tails.ap, in_=xbv[:, :, 3 + TH - 3:3 + TH]), reads=[xbT], writes=[tails])

        Dws = [fs(0, 256, BF16), fs(256, 256, BF16)]
        for c in range(8):
            Dw = Dws[c % 2]
            d3 = Dw.ap.rearrange("p (j m) -> p j m", j=4)
            E("dve", lambda e, c=c, d3=d3: e.tensor_tensor(out=d3, in0=identf.ap[:, None, :].broadcast_to([128, 4, 128]),
                                                          in1=cwT.ap[:, :, c:c + 1].broadcast_to([128, 4, 128]), op=ALU.mult),
              reads=[identf, cwT], writes=[Dw])
            for g_ in range(2):
                ps = pb(psrot[0] % 6, 0, 512); psrot[0] += 1
                for j in range(4):
                    E("pe", lambda e, c=c, j=j, g_=g_, ps=ps, d3=d3: e.matmul(
                        ps.ap, lhsT=d3[:, j, :], rhs=xbv[:, c, j + g_ * 512:j + g_ * 512 + 512], start=(j == 0), stop=(j == 3)),
                      reads=[Dw, xbT], writes=[ps] if j == 0 else [], pwrites=[] if j == 0 else [ps])
                evac(xcv[:, c, g_ * 512:(g_ + 1) * 512], ps.ap, [ps], pw=xcT, eng="act", func=AF.Silu, bias=(cbT, cbT.ap[:, c:c + 1]))
        if pp == 0:
            accS = fs(2048, 128)
            av = accS.ap.rearrange("p (c b) -> p c b", c=8)
            for c in range(8):
                E("dve", lambda e, c=c: e.tensor_scalar(out=av[:, c, :], in0=xbv[:, c, 3 + TH:3 + TC], scalar1=cwT.ap[:, 3, c:c + 1],
                                                       scalar2=None, op0=ALU.mult), reads=[xbT, cwT], pwrites=[accS])
                for j in range(3):
                    E("dve", lambda e, c=c, j=j: e.scalar_tensor_tensor(
                        out=av[:, c, :], in0=bufT.ap[:, c, j, :], scalar=cwT.ap[:, j, c:c + 1], in1=av[:, c, :],
                        op0=ALU.mult, op1=ALU.add), reads=[bufT, cwT, accS], pwrites=[accS])
                evac(xcv[:, c, TH:TC], av[:, c, :], [accS], pw=xcT, eng="act", func=AF.Silu, bias=(cbT, cbT.ap[:, c:c + 1]))

        E("pool", lambda e: e.memset(vav[:, :, :, 256:257], 1.0), pwrites=[vaug])
        for (dst, dv, wsrc, src, sv, soff) in ((qT, qv, wq, xcT, xcv, 0), (kT, kv, wk, xcT, xcv, 0), (vT, vv, wv, xbT, xbv, 3)):
            for h in range(4):
                for ec in range(2):
                    def outf(ps, c0, n, h=h, ec=ec, dst=dst, dv=dv):
                        evac(dv[:, 2 * h + ec, c0:c0 + n], ps.ap[:, 0:n], [ps], pw=dst)
                        if dst is vT and c0 == TH:
                            evac(vTs.ap[:, 2 * h + ec, :], ps.ap[:, 0:n], [ps], pw=vTs, eng="dve")
                    projB(pp, lambda k, h=h, ec=ec, wsrc=wsrc: wsrc.ap[:, h, k, ec * 128:(ec + 1) * 128], 2,
                          lambda k, c0, n, h=h, sv=sv, soff=soff: sv[:, 2 * h + k, soff + c0:soff + c0 + n], outf, [wsrc], [src])
        for i in range(8):
            for hb2 in range(2):
                ps = pb(psrot[0] % 6, 0, 512); psrot[0] += 1
                for hh in range(2):
                    h = hb2 * 2 + hh
                    for kk in range(2):
                        E("pe", lambda e, i=i, h=h, hh=hh, kk=kk, ps=ps: e.matmul(
                            ps.ap[:, hh * 256:(hh + 1) * 256], lhsT=xbv[:, 2 * h + kk, 3 + i * 128:3 + (i + 1) * 128],
                            rhs=wv.ap[:, h, kk, :], start=(kk == 0), stop=(kk == 1)),
                          reads=[xbT, wv], writes=[ps] if (hh == 0 and kk == 0) else [], pwrites=[] if (hh == 0 and kk == 0) else [ps])
                evac(vav[:, i, hb2 * 2:hb2 * 2 + 2, 0:256], ps.ap.rearrange("p (h e) -> p h e", h=2), [ps], pw=vaug)
        if pp == 0:
            q_sf = fs(0, 1024); k_sf = fs(1024, 1024); n_sf = fs(2048, 1024); tmp_s = fs(3072, 1024)
            DMA("sp", n_sf.ap[0:16, :], n_in, writes=[n_sf])
            for (wsrc, dstf, qi) in ((wq, q_sf, 0), (wk, k_sf, 1)):
                for hb2 in range(2):
                    ps = pb(psrot[0] % 6, 0, 512); psrot[0] += 1
                    for hh in range(2):
                        h = hb2 * 2 + hh
                        for kk in range(2):
                            E("pe", lambda e, h=h, hh=hh, kk=kk, ps=ps, wsrc=wsrc: e.matmul(
                                ps.ap[0:16, hh * 256:(hh + 1) * 256], lhsT=xcv[:, 2 * h + kk, TH:TC],
                                rhs=wsrc.ap[:, h, kk, :], start=(kk == 0), stop=(kk == 1)),
                              reads=[xcT, wsrc], writes=[ps] if (hh == 0 and kk == 0) else [],
                              pwrites=[] if (hh == 0 and kk == 0) else [ps])
                    evac(dstf.ap[0:16, hb2 * 512:(hb2 + 1) * 512], ps.ap[0:16, :], [ps], pw=dstf, eng="act")
                    evac(qks.ap[0:16, qi, hb2 * 512:(hb2 + 1) * 512], ps.ap[0:16, :], [ps], pw=qks, eng="dve")

        gps = pb(6, 0, 72)
        for i in range(nt):
            tp, c0 = tinfo(i)
            for kc in range(24):
                srcb, srcv = ((qT, qv), (kT, kv), (vT, vv))[kc // 8]
                E("pe", lambda e, i=i, kc=kc, tp=tp, c0=c0, srcv=srcv: e.matmul(
                    gps.ap[:tp, i * 8:(i + 1) * 8], lhsT=srcv[:, kc % 8, c0:c0 + tp], rhs=wif.ap[:, kc, :],
                    start=(kc == 0), stop=(kc == 23)), reads=[srcb, wif],
                  writes=[gps] if (i == 0 and kc == 0) else [], pwrites=[] if (i == 0 and kc == 0) else [gps])
        E("dve", lambda e: e.tensor_tensor(out=G.ap[:, :, 0:8], in0=gps.ap[:, 0:64].rearrange("p (i g) -> p g i", g=8),
                                           in1=bifb.ap[:, :, None].broadcast_to([128, 8, 8]), op=ALU.add),
          reads=[gps, bifb], pwrites=[G])
        if pp == 0:
            E("dve", lambda e: e.tensor_tensor(out=G.ap[0:16, :, 8], in0=gps.ap[0:16, 64:72], in1=bifb.ap[0:16, :], op=ALU.add),
              reads=[gps, bifb], pwrites=[G])
        def gate_math():
            f32v = lambda b: b.ap.rearrange("p h j -> p (h j)")
            E("act", lambda e: e.activation(out=E1.ap, in_=G.ap[:, 4:8, 0:8], func=AF.Exp, scale=-1.0), reads=[G], writes=[E1])
            E("act", lambda e: e.activation(out=LFN.ap, in_=E1.ap, func=AF.Ln, bias=onec.ap[:, 0:1]), reads=[E1, onec], writes=[LFN])
            pbn = pb(7, 0, 32); prB = pb(6, 64, 32)
            yield
            E("pe", lambda e: e.matmul(pbn.ap, lhsT=tri.ap, rhs=f32v(LFN), start=True, stop=True), reads=[tri, LFN], writes=[pbn])
            yield
            E("pe", lambda e: e.matmul(prB.ap[0:1, :], lhsT=onesf.ap[:, 0:1], rhs=f32v(LFN), start=True, stop=True),
              reads=[onesf, LFN], writes=[prB])
            E("act", lambda e: e.copy(out=f32v(BNs), in_=pbn.ap), reads=[pbn], writes=[BNs])
            E("act", lambda e: e.copy(out=rowB.ap, in_=prB.ap[0:1, :]), reads=[prB], writes=[rowB])
            E("dve", lambda e: e.tensor_tensor(out=Csb.ap, in0=G.ap[:, 0:4, 0:8], in1=pbn.ap.rearrange("p (h j) -> p h j", h=4), op=ALU.add),
              reads=[G, pbn], writes=[Csb])
            yield
            pct = pb(7, 128, 128)
            E("pe", lambda e: e.transpose(out=pct.ap[0:32, :], in_=f32v(Csb), identity=identf.ap), reads=[Csb, identf], writes=[pct])
            E("dve", lambda e: e.tensor_reduce(out=cmaxc.ap, in_=pct.ap[0:32, :], axis=AX.X, op=ALU.max), reads=[pct], writes=[cmaxc])
            yield
            prA = pb(6, 128, 32)
            E("pe", lambda e: e.transpose(out=prA.ap[0:1, :], in_=cmaxc.ap, identity=identf.ap[0:32, 0:32]),
              reads=[cmaxc, identf], writes=[prA])
            E("dve", lambda e: e.tensor_copy(out=rowA.ap, in_=prA.ap[0:1, :]), reads=[prA], writes=[rowA])
            for h in range(4):
                E("dve", lambda e, h=h: e.tensor_tensor_scan(out=MN.ap[0:1, h * 8:(h + 1) * 8], data0=rowA.ap[0:1, h * 8:(h + 1) * 8],
                                                             data1=rowB.ap[0:1, h * 8:(h + 1) * 8], initial=mcar.ap[0:1, h:h + 1],
                                                             op0=ALU.max, op1=ALU.subtract),
                  reads=[rowA, rowB, mcar], writes=[MN] if h == 0 else [], pwrites=[] if h == 0 else [MN])
            MN3 = MN.ap.rearrange("p (h j) -> p h j", h=4); MP3 = MPV.ap.rearrange("p (h j) -> p h j", h=4)
            E("dve", lambda e: e.tensor_copy(out=MP3[:, :, 0:1], in_=mcar.ap[:, :, None]), reads=[mcar], writes=[MPV])
            E("dve", lambda e: e.tensor_copy(out=MP3[:, :, 1:8], in_=MN3[:, :, 0:7]), reads=[MN], pwrites=[MPV])
            E("dve", lambda e: e.tensor_tensor(out=RG.ap[0:1, 0:32], in0=MN.ap, in1=rowB.ap, op=ALU.add), reads=[MN, rowB], writes=[RG])
            E("dve", lambda e: e.tensor_tensor(out=rtmp.ap, in0=MPV.ap, in1=RG.ap[0:1, 0:32], op=ALU.subtract),
              reads=[MPV, RG], writes=[rtmp])
            E("act", lambda e: e.activation(out=RG.ap[0:1, 32:64], in_=rtmp.ap, func=AF.Exp), reads=[rtmp], pwrites=[RG])
            E("dve", lambda e: e.tensor_copy(out=mcar.ap[:, :, None], in_=MN3[:, :, 7:8]), reads=[MN, MPV], writes=[mcar])
            if pp == 1:
                out_dmas.append(DMA("sp", mp_o, mcar.ap, reads=[mcar]))
            yield
            pbc = pb(7, 256, 64)
            E("pe", lambda e: e.matmul(pbc.ap, lhsT=onesf.ap[0:1, :], rhs=RG.ap[0:1, :], start=True, stop=True),
              reads=[onesf, RG], writes=[pbc])
            E("dve", lambda e: e.tensor_tensor(out=f32v(A1), in0=f32v(Csb), in1=pbc.ap[:, 0:32], op=ALU.subtract),
              reads=[Csb, pbc], writes=[A1])
            E("act", lambda e: e.activation(out=WS.ap, in_=A1.ap, func=AF.Exp, bias=nln16.ap[:, 0:1]), reads=[A1, nln16], writes=[WS])
            E("dve", lambda e: e.tensor_tensor(out=f32v(E1), in0=f32v(BNs), in1=pbc.ap[:, 0:32], op=ALU.subtract),
              reads=[BNs, pbc], writes=[E1])
            E("act", lambda e: e.activation(out=FL.ap, in_=E1.ap, func=AF.Exp), reads=[E1], writes=[FL])
            E("dve", lambda e: e.tensor_copy(out=f32v(GSb), in_=pbc.ap[:, 32:64]), reads=[pbc], writes=[GSb])

            if pp == 0:
                lis = G.ap[0:16, 0:4, 8]; lfs = G.ap[0:16, 4:8, 8]
                sgv = lambda a, b: sg.ap[0:16, a:b]
                E("act", lambda e: e.activation(out=sgv(0, 4), in_=lfs, func=AF.Exp, scale=-1.0), reads=[G], pwrites=[sg])
                E("act", lambda e: e.activation(out=sgv(4, 8), in_=sgv(0, 4), func=AF.Ln, bias=onec.ap[0:16, 0:1]),
                  reads=[sg, onec], pwrites=[sg])
                E("dve", lambda e: e.tensor_tensor(out=sgv(8, 12), in0=lis, in1=sgv(4, 8), op=ALU.add), reads=[G, sg], pwrites=[sg])
                E("dve", lambda e: e.tensor_tensor(out=sgv(12, 16), in0=sgv(8, 12), in1=m_s.ap, op=ALU.max), reads=[sg, m_s], pwrites=[sg])
                E("dve", lambda e: e.tensor_tensor(out=sgv(16, 20), in0=sgv(12, 16), in1=sgv(4, 8), op=ALU.subtract),
                  reads=[sg], pwrites=[sg])
                out_dmas.append(DMA("sp", ms_o, sgv(16, 20), reads=[sg]))
                E("dve", lambda e: e.tensor_tensor(out=sgv(20, 24), in0=sgv(8, 12), in1=sgv(12, 16), op=ALU.subtract),
                  reads=[sg], pwrites=[sg])
                E("act", lambda e: e.activation(out=S12.ap[:, 0:4], in_=sgv(20, 24), func=AF.Exp, bias=nln16.ap[0:16, 0:1]),
                  reads=[sg, nln16], pwrites=[S12])
                E("dve", lambda e: e.tensor_tensor(out=sgv(24, 28), in0=m_s.ap, in1=sgv(12, 16), op=ALU.subtract),
                  reads=[sg, m_s], pwrites=[sg])
                E("act", lambda e: e.activation(out=S12.ap[:, 4:8], in_=sgv(24, 28), func=AF.Exp), reads=[sg], pwrites=[S12])
                E("act", lambda e: e.activation(out=sgv(28, 32), in_=sgv(16, 20), func=AF.Exp, scale=-1.0), reads=[sg], pwrites=[sg])
                E("dve", lambda e: e.tensor_tensor(out=tmp_s.ap[0:16, :], in0=q_sf.ap[0:16, :], in1=k_sf.ap[0:16, :], op=ALU.mult),
                  reads=[q_sf, k_sf], writes=[tmp_s])
                E("dve", lambda e: e.tensor_reduce(out=sgv(32, 36), in_=tmp_s.ap[0:16, :].rearrange("p (h d) -> p h d", h=4),
                                                   axis=AX.X, op=ALU.add), reads=[tmp_s], pwrites=[sg])
                E("dve", lambda e: e.tensor_tensor(out=tmp_s.ap[0:16, :], in0=q_sf.ap[0:16, :], in1=n_sf.ap[0:16, :], op=ALU.mult),
                  reads=[q_sf, n_sf, sg], writes=[tmp_s])
                E("dve", lambda e: e.tensor_reduce(out=sgv(36, 40), in_=tmp_s.ap[0:16, :].rearrange("p (h d) -> p h d", h=4),
                                                   axis=AX.X, op=ALU.add), reads=[tmp_s], pwrites=[sg])
                E("dve", lambda e: e.tensor_tensor(out=S12.ap[:, 8:12], in0=S12.ap[:, 0:4], in1=sgv(32, 36), op=ALU.mult),
                  reads=[S12, sg], pwrites=[S12])
                E("dve", lambda e: e.tensor_tensor(out=sgv(40, 44), in0=S12.ap[:, 4:8], in1=sgv(36, 40), op=ALU.mult),
                  reads=[S12, sg], pwrites=[sg])
                E("dve", lambda e: e.tensor_tensor(out=sgv(44, 48), in0=sgv(40, 44), in1=S12.ap[:, 8:12], op=ALU.add),
                  reads=[S12, sg], pwrites=[sg])
                E("dve", lambda e: e.scalar_tensor_tensor(out=sgv(48, 52), in0=sgv(44, 48), scalar=-1.0, in1=sgv(44, 48),
                                                          op0=ALU.mult, op1=ALU.max), reads=[sg], pwrites=[sg])
                E("dve", lambda e: e.tensor_tensor(out=sgv(52, 56), in0=sgv(48, 52), in1=sgv(28, 32), op=ALU.max), reads=[sg], pwrites=[sg])
                E("dve", lambda e: e.reciprocal(out=rinv_s.ap, in_=sgv(52, 56)), reads=[sg], writes=[rinv_s])
                n3 = lambda b: b.ap[0:16, :].rearrange("p (h d) -> p h d", h=4)
                E("dve", lambda e: e.tensor_tensor(out=n3(n_sf), in0=n3(n_sf), in1=S12.ap[:, 4:8, None].broadcast_to([16, 4, 256]),
                                                   op=ALU.mult), reads=[n_sf, S12, tmp_s], writes=[n_sf])
                E("dve", lambda e: e.tensor_tensor(out=n3(tmp_s), in0=n3(k_sf), in1=S12.ap[:, 0:4, None].broadcast_to([16, 4, 256]),
                                                   op=ALU.mult), reads=[k_sf, S12], writes=[tmp_s])
                E("dve", lambda e: e.tensor_tensor(out=n_sf.ap[0:16, :], in0=n_sf.ap[0:16, :], in1=tmp_s.ap[0:16, :], op=ALU.add),
                  reads=[n_sf, tmp_s], writes=[n_sf])
                out_dmas.append(DMA("sp", ns_o, n_sf.ap[0:16, :], reads=[n_sf]))
                E("dve", lambda e: e.tensor_tensor(out=BD.ap, in0=S12.ap[:, None, :].broadcast_to([16, 16, 12]),
                                                   in1=identf.ap[0:16, 0:16, None].broadcast_to([16, 16, 12]), op=ALU.mult),
                  reads=[S12, identf], writes=[BD])
                pbd = pb(6, 192, 192)
                yield
                E("pe", lambda e: e.matmul(pbd.ap, lhsT=onesf.ap[0:16, :], rhs=BD.ap.rearrange("p b q -> p (b q)"),
                                           start=True, stop=True), reads=[onesf, BD], writes=[pbd])
                E("act", lambda e: e.copy(out=bcs.ap.rearrange("p b q -> p (b q)"), in_=pbd.ap), reads=[pbd], writes=[bcs])


            yield
        gm = gate_math()

        sigo = ar(1, 0, 8192); sgo = sigo.ap.rearrange("p (i f) -> p i f", i=8)
        for bi, blk in enumerate((8, 9)):
            wb = load_wblk(blk)
            for i in range(nt):
                tp, c0 = tinfo(i)
                ps = projA(lambda k, c0=c0, tp=tp: hTv[:, k, c0:c0 + tp], 8, lambda k, wb=wb: wb.ap[:, k, :], tp, [hgrp(c0)], [wb])
                if i < 8:
                    evac(sgo[:tp, i, bi * 512:(bi + 1) * 512], ps.ap[:tp, :], [ps], pw=sigo, eng="act", func=AF.Sigmoid)
                else:
                    evac(sgs.ap[:tp, bi * 512:(bi + 1) * 512], ps.ap[:tp, :], [ps], pw=sgs, eng="act", func=AF.Sigmoid)
                if i % 2 == 1:
                    next(gm, None)
        szb = ar(6, 0, 8 * TC); szv = V3(szb, 8)

        def sxs_szg(c):
            E("dve", lambda e: e.scalar_tensor_tensor(out=xcv[:, c, 0:NC_], in0=xcv[:, c, 0:NC_], scalar=skipT.ap[:, c:c + 1],
                                                      in1=szv[:, c, 0:NC_], op0=ALU.mult, op1=ALU.mult),
              reads=[xcT, skipT, szb], pwrites=[xcT], war=[xcT])
            E("dve", lambda e: e.tensor_scalar(out=szv[:, c, 0:NC_], in0=szv[:, c, 0:NC_], scalar1=gngT.ap[:, c:c + 1],
                                               scalar2=None, op0=ALU.mult), reads=[szb, gngT, xcT], pwrites=[szb])

        for bi, blk in enumerate((10, 11)):
            wb = load_wblk(blk)
            for cc in range(4):
                c = bi * 4 + cc

                def outf(ps, c0, n, c=c):
                    evac(szv[:, c, c0:c0 + n], ps.ap[:, 0:n], [ps], pw=szb, eng="act", func=AF.Silu)
                projB(pp, lambda k, cc=cc, wb=wb: wb.ap[:, k, cc * 128:(cc + 1) * 128], 8,
                      lambda k, c0, n: hTv[:, k, c0:c0 + n], outf, [wb], [], rb_fn=hgrp)
                if c >= 1:
                    sxs_szg(c - 1)
        sxs_szg(7)
        for _ in gm:
            pass

        ybT = ar(0, 0, 8 * TC); ybv = V3(ybT, 8)
        SKs = [pb(0, 0, 512), pb(1, 0, 512)]
        KTs = [pb(6, 0, 512)]
        NHs = [pb(2, 0, 512), pb(3, 0, 512)]
        HTs = [pb(7, 0, 512)]
        Us = [pb(4, 0, 512), pb(5, 0, 512)]
        o = 0
        PTbs = [fs(o + i * 64, 64, BF16) for i in range(2)]; o += 128
        KWbs = [fs(o + i * 128, 128, BF16) for i in range(2)]; o += 256
        ABbs = [fs(o + i * 257, 257, BF16) for i in range(3)]; o += 771
        HBfs = [fs(o + i * 256, 256) for i in range(3)]; o += 768
        HBNs = [fs(o + i * 128, 128, BF16) for i in range(2)]; o += 256
        Yts = [fs(o + i * 256, 256) for i in range(2)]; o += 512
        rvs = [fs(o + i * 16, 16) for i in range(3)]; o += 48

        def post1a(tp, num_b, den_ap, fl_ap, fl_b, rv):
            E("act", lambda e: e.activation(out=rv.ap[:tp, 0:1], in_=den_ap, func=AF.Abs), reads=[num_b], writes=[rv])
            E("dve", lambda e: e.tensor_tensor(out=rv.ap[:tp, 2:3], in0=rv.ap[:tp, 0:1], in1=fl_ap, op=ALU.max),
              reads=[rv, fl_b], pwrites=[rv])
            E("pool", lambda e: e.tensor_tensor(out=rv.ap[:tp, 1:2], in0=rv.ap[:tp, 2:3], in1=monec.ap[:tp, 0:1], op=ALU.pow),
              reads=[rv, monec], pwrites=[rv])

        def post(it, tp, num_ap, num_b, den_ap, fl_ap, fl_b, sig_ap, h, c0, rinv_ap=None, rinv_b=None, sig_b=None, part=0):
            sig_b = sig_b or sigo
            rv = rvs[it % 3]; HBf = HBfs[it % 3]; HBN = HBNs[it % 2]; Yt = Yts[it % 2]
            hbtb = HTs[0]; hbt_ap = hbtb.ap[:, 0:128].bitcast(BF16)
            if part == 4:
                post1a(tp, num_b, den_ap, fl_ap, fl_b, rv)
            if part == 1:
                rinv_ap = rv.ap[:tp, 1:2]; rinv_b = rv
            if part in (0, 1):
                post1(tp, num_ap, num_b, den_ap, fl_ap, fl_b, sig_ap, rinv_ap, rinv_b, sig_b, rv, HBf, part)
            if part in (0, 2):
                post2(tp, rv, HBf, HBN, hbtb, hbt_ap)
            if part in (0, 3):
                post3(tp, h, c0, hbtb, hbt_ap, Yt)

        def post1(tp, num_ap, num_b, den_ap, fl_ap, fl_b, sig_ap, rinv_ap, rinv_b, sig_b, rv, HBf, part):
            E("dve", lambda e: e.scalar_tensor_tensor(out=HBf.ap[:tp], in0=num_ap, scalar=rinv_ap, in1=sig_ap,
                                                      op0=ALU.mult, op1=ALU.mult), reads=[num_b, rinv_b, sig_b], writes=[HBf])
            if part == 0:
                E("dve", lambda e: e.bn_stats(out=rv.ap[:tp, 4:10], in_=HBf.ap[:tp]), reads=[HBf], writes=[rv])
            else:
                E("dve", lambda e: e.bn_stats(out=rv.ap[:tp, 4:10], in_=HBf.ap[:tp]), reads=[HBf], pwrites=[rv])
            E("dve", lambda e: e.bn_aggr(out=rv.ap[:tp, 10:12], in_=rv.ap[:tp, 4:10]), reads=[rv], pwrites=[rv])
            E("pool", lambda e: e.tensor_tensor(out=rv.ap[:tp, 12:13], in0=rv.ap[:tp, 11:12], in1=epsc.ap[:tp, 0:1], op=ALU.add),
              reads=[rv, epsc], pwrites=[rv])
            E("pool", lambda e: e.tensor_tensor(out=rv.ap[:tp, 13:14], in0=rv.ap[:tp, 12:13], in1=mhalf.ap[:tp, 0:1], op=ALU.pow),
              reads=[rv, mhalf], pwrites=[rv])

        def post2(tp, rv, HBf, HBN, hbtb, hbt_ap):
            E("dve", lambda e: e.tensor_scalar(out=HBN.ap[:tp], in0=HBf.ap[:tp], scalar1=rv.ap[:tp, 10:11], scalar2=rv.ap[:tp, 13:14],
                                               op0=ALU.subtract, op1=ALU.mult), reads=[HBf, rv], writes=[HBN])
            for ec in range(2):
                E("pe", lambda e, ec=ec: e.transpose(out=hbt_ap[:, ec * 128:ec * 128 + tp], in_=HBN.ap[:tp, ec * 128:(ec + 1) * 128],
                                                     identity=ident.ap[:tp, :tp]), reads=[HBN, ident],
                  writes=[hbtb] if ec == 0 else [], pwrites=[] if ec == 0 else [hbtb])

        def post3(tp, h, c0, hbtb, hbt_ap, Yt):
            E("dve", lambda e: e.tensor_tensor(out=Yt.ap.rearrange("p (a t) -> p a t", a=2)[:, :, 0:tp],
                                               in0=hbt_ap.rearrange("p (a t) -> p a t", a=2)[:, :, 0:tp],
                                               in1=szv[:, 2 * h:2 * h + 2, c0:c0 + tp], op=ALU.mult), reads=[hbtb, szb], writes=[Yt])
            E("pool", lambda e: e.tensor_tensor(out=ybv[:, 2 * h:2 * h + 2, c0:c0 + tp],
                                                in0=Yt.ap.rearrange("p (a t) -> p a t", a=2)[:, :, 0:tp],
                                                in1=xcv[:, 2 * h:2 * h + 2, c0:c0 + tp], op=ALU.add), reads=[Yt, xcT], pwrites=[ybT])

        def stageA(it, j, h, part):
            jc = j * 128
            SK = SKs[it % 2]; PTb = PTbs[it % 2]; KWb = KWbs[it % 2]; ABb = ABbs[it % 3]
            KT = KTs[0]
            st_ap = SK.ap[:, 0:128]; ktr_ap = KT.ap[:, 0:128].bitcast(BF16); NU_ap = SK.ap[:, 256:258]
            NH = NHs[it % 2]; num_ap = NH.ap[:, 0:257]; U = Us[it % 2]
            wcol = WS.ap[:, h, j:j + 1]; gcol = GSb.ap[:, h, j:j + 1]
            if part == 2:
                return stageA2(it, j, h, jc, SK, PTb, KWb, ABb, KT, st_ap, ktr_ap, NU_ap, NH, num_ap, U, wcol, gcol)
            for ec in range(2):
                E("pe", lambda e, ec=ec: e.matmul(st_ap, lhsT=kv[:, 2 * h + ec, jc:jc + 128], rhs=qv[:, 2 * h + ec, jc:jc + 128],
                                                  start=(ec == 0), stop=(ec == 1)), reads=[kT, qT],
                  writes=[SK] if ec == 0 else [], pwrites=[] if ec == 0 else [SK])
            for dc in range(2):
                E("pe", lambda e, dc=dc: e.transpose(out=ktr_ap[:, dc * 128:(dc + 1) * 128], in_=kv[:, 2 * h + dc, jc:jc + 128],
                                                     identity=ident.ap), reads=[kT, ident],
                  writes=[KT] if dc == 0 else [], pwrites=[] if dc == 0 else [KT])

        def stageA2(it, j, h, jc, SK, PTb, KWb, ABb, KT, st_ap, ktr_ap, NU_ap, NH, num_ap, U, wcol, gcol):
            E("dve", lambda e: e.scalar_tensor_tensor(out=PTb.ap, in0=st_ap, scalar=wcol, in1=maskT.ap, op0=ALU.mult, op1=ALU.mult),
              reads=[SK, WS, maskT], writes=[PTb])
            E("act", lambda e: e.activation(out=KWb.ap, in_=ktr_ap, func=AF.Copy, scale=wcol), reads=[KT, WS], writes=[KWb])
            ab3 = ABb.ap.rearrange("p (a e) -> p a e", a=2)
            E("pe", lambda e: e.matmul(num_ap, lhsT=PTb.ap, rhs=vav[:, j, h, :], start=True, stop=False),
              reads=[PTb, vaug], writes=[NH])
            for dc in range(2):
                E("pe", lambda e, dc=dc: e.matmul(num_ap, lhsT=qv[:, 2 * h + dc, jc:jc + 128], rhs=ab3[:, dc, :],
                                                  start=False, stop=(dc == 1)), reads=[qT, ABb], pwrites=[NH])
            for dc in range(2):
                E("pe", lambda e, dc=dc: e.matmul(U.ap[:, dc * 256:(dc + 1) * 256], lhsT=KWb.ap[:, dc * 128:(dc + 1) * 128],
                                                  rhs=vav[:, j, h, 0:256], start=True, stop=True), reads=[KWb, vaug],
                  writes=[U] if dc == 0 else [], pwrites=[] if dc == 0 else [U])
            for dc in range(2):
                E("pe", lambda e, dc=dc: e.matmul(NU_ap[:, dc:dc + 1], lhsT=KWb.ap[:, dc * 128:(dc + 1) * 128],
                                                  rhs=onesb.ap[:, 0:1], start=True, stop=True), reads=[KWb, onesb], pwrites=[SK])

        def stageAb(it, j, h):
            ABb = ABbs[it % 3]
            gcol = GSb.ap[:, h, j:j + 1]
            ab3 = ABb.ap.rearrange("p (a e) -> p a e", a=2)
            E("act", lambda e: e.activation(out=ab3[:, :, 0:256], in_=CT.ap[:, h, :, :], func=AF.Copy, scale=gcol),
              reads=[CT, GSb], writes=[ABb])
            E("act", lambda e: e.activation(out=ab3[:, :, 256], in_=nT.ap[:, h, :], func=AF.Copy, scale=gcol),
              reads=[nT, GSb], pwrites=[ABb])

        def stageU(it, j, h):
            SK = SKs[it % 2]; ABb = ABbs[it % 3]; U = Us[it % 2]
            NU_ap = SK.ap[:, 256:258]
            gcol = GSb.ap[:, h, j:j + 1]
            E("dve", lambda e: e.scalar_tensor_tensor(out=CT.ap[:, h, :, :].rearrange("p a e -> p (a e)"),
                                                      in0=CT.ap[:, h, :, :].rearrange("p a e -> p (a e)"), scalar=gcol, in1=U.ap,
                                                      op0=ALU.mult, op1=ALU.add), reads=[CT, GSb, U, ABb], pwrites=[CT])
            E("dve", lambda e: e.scalar_tensor_tensor(out=nT.ap[:, h, :], in0=nT.ap[:, h, :], scalar=gcol, in1=NU_ap,
                                                      op0=ALU.mult, op1=ALU.add), reads=[nT, GSb, SK, ABb], pwrites=[nT])

        def stageB(it, j, h, part):
            num = NHs[it % 2]
            post(it, 128, num.ap[:, 0:256], num, num.ap[:, 256:257], FL.ap[:, h, j:j + 1], FL,
                 sgo[:, j, h * 256:(h + 1) * 256], h, j * 128, part=part)

        wvb = [load_wblk(2), load_wblk(3)]
        its = [(j, h) for j in range(8) for h in range(4)]
        nit = len(its)
        stageAb(0, *its[0])
        for t in range(nit + 3):
            if t < nit:
                stageA(t, *its[t], part=1)
            if 0 <= t - 3 < nit:
                stageB(t - 3, *its[t - 3], part=3)
            if 0 <= t - 2 < nit:
                stageB(t - 2, *its[t - 2], part=2)
            if t < nit:
                stageA(t, *its[t], part=2)
            if 0 <= t - 1 < nit:
                stageB(t - 1, *its[t - 1], part=1)
            if t < nit:
                stageU(t, *its[t])
                stageB(t, *its[t], part=4)
            if t + 1 < nit:
                stageAb(t + 1, *its[t + 1])

        hT2 = ar(1, 0, 8 * TC)
        h2v = V3(hT2, 8)
        DMA("sp", h2v[:, :, 0:NC_], hscr.rearrange("p (c t) -> p c t", c=8)[:, :, 0:NC_], reads=[hscrb], writes=[hT2])
        if pp == 1:
            Cst = ar(3, 0, 4096, F32)
            csv = Cst.ap.rearrange("p (h a d) -> p h a d", h=4, a=2)
            for h in range(4):
                for eh in range(2):
                    pt_ = pb((h * 2 + eh) % 2 + 2, 0, 256)
                    for dc in range(2):
                        E("pe", lambda e, h=h, eh=eh, dc=dc, pt_=pt_: e.transpose(
                            out=pt_.ap[:, dc * 128:(dc + 1) * 128], in_=CT.ap[:, h, dc, eh * 128:(eh + 1) * 128],
                            identity=identf.ap), reads=[CT, identf], writes=[pt_] if dc == 0 else [], pwrites=[] if dc == 0 else [pt_])
                    evac(csv[:, h, eh, :], pt_.ap, [pt_], pw=Cst)
            out_dmas.append(DMA("sp", Cp.rearrange("h (a p) d -> p h a d", p=128), csv, reads=[Cst]))
            out_dmas.append(DMA("sp", np_o.rearrange("h (a p) -> p h a", p=128), nT.ap, reads=[nT], allow_slow_non_contiguous=True))

        if pp == 0:
            NCB = 12
            Cins = [ar(3 + i // 8, (i % 8) * 1024, 1024, F32) for i in range(NCB)]
            qkb = fs(2752, 1024, BF16)
            junkc = fs(3776, 128, BF16)
            E("dve", lambda e: e.tensor_tensor(
                out=wvT.ap, in0=vTs.ap.rearrange("p (h a) b -> p h a b", h=4),
                in1=bcs.ap[:, :, 0:4].rearrange("p b h -> p h b")[:, :, None, :].broadcast_to([128, 4, 2, 16]),
                op=ALU.mult), reads=[vTs, bcs], writes=[wvT])
            units = [(b, h) for b in range(NS) for h in range(4)]
            pqs = {}

            def c_in(u):
                b, h = units[u]
                Cin = Cins[u % NCB]
                c3 = Cin.ap.rearrange("p (a d) -> p a d", a=2)
                DMA("sp", c3, C_in[b, h].rearrange("(a p) d -> p a d", p=128), writes=[Cin], key=rkey("ci", u, NCB))

            def c_nop(u):
                pass

            def c_s0(u):
                b, h = units[u]
                if h == 0:
                    E("dve", lambda e: e.tensor_scalar(out=qkb.ap[0:16, :], in0=qks.ap.rearrange("p a f -> p (a f)"),
                                                       scalar1=identf.ap[0:16, b:b + 1], scalar2=None, op0=ALU.mult),
                      reads=[qks, identf], writes=[qkb])
                pq = pb(u % 8, 0, 512)
                pqs[u] = pq
                E("pe", lambda e: e.matmul(
                    pq.ap.rearrange("p (a d) -> p a d", a=2), lhsT=onesb.ap[0:16, :],
                    rhs=qkb.ap[0:16, :].rearrange("p (a f) -> p a f", a=2)[:, :, h * 256:(h + 1) * 256], start=True, stop=True),
                  reads=[qkb, onesb], writes=[pq])

            def c_s1(u):
                b, h = units[u]
                Cin = Cins[u % NCB]; pq = pqs[u]
                c3 = Cin.ap.rearrange("p (a d) -> p a d", a=2)
                for a in range(2):
                    E("dve", lambda e, a=a: e.scalar_tensor_tensor(
                        out=junkc.ap, in0=c3[:, a, :], scalar=1.0, in1=pq.ap[:, 0:256], op0=ALU.mult, op1=ALU.mult,
                        accum_out=CqT.ap[:, h, a, b:b + 1]), reads=[Cin, pq], writes=[junkc], pwrites=[CqT])

            def c_s2(u):
                b, h = units[u]
                Cin = Cins[u % NCB]
                E("act", lambda e: e.activation(out=Cin.ap, in_=Cin.ap, func=AF.Copy, scale=bcs.ap[:, b, 4 + h:5 + h]),
                  reads=[Cin, bcs], writes=[Cin])

            def c_s3(u):
                b, h = units[u]
                Cin = Cins[u % NCB]; pq = pqs[u]
                c3 = Cin.ap.rearrange("p (a d) -> p a d", a=2)
                for a in range(2):
                    E("dve", lambda e, a=a: e.scalar_tensor_tensor(
                        out=c3[:, a, :], in0=pq.ap[:, 256:512], scalar=wvT.ap[:, h, a, b:b + 1], in1=c3[:, a, :],
                        op0=ALU.mult, op1=ALU.add), reads=[Cin, pq, wvT], writes=[Cin])
                out_dmas.append(DMA("sp", Cs[b, h].rearrange("(a p) d -> p a d", p=128), c3, reads=[Cin], key=rkey("co", u, NCB)))

            pipeline(len(units), [c_in, c_nop, c_nop, c_nop, c_nop, c_nop, c_s0, c_s1, c_s2, c_s3])
            bq = lambda q: bcs.ap[:, :, q * 4:(q + 1) * 4].rearrange("p b h -> p h b")[:, :, None, :].broadcast_to([128, 4, 2, 16])
            E("dve", lambda e: e.tensor_tensor(out=numTs.ap, in0=vTs.ap.rearrange("p (h a) b -> p h a b", h=4), in1=bq(2), op=ALU.mult),
              reads=[vTs, bcs], writes=[numTs])
            E("dve", lambda e: e.tensor_tensor(out=CqT.ap, in0=CqT.ap, in1=bq(1), op=ALU.mult), reads=[CqT, bcs], writes=[CqT])
            E("dve", lambda e: e.tensor_tensor(out=numTs.ap, in0=numTs.ap, in1=CqT.ap, op=ALU.add), reads=[numTs, CqT], writes=[numTs])
            for hb2 in range(2):
                pn = pb(4 + hb2, 0, 512)
                for hh in range(2):
                    h = hb2 * 2 + hh
                    for a in range(2):
                        E("pe", lambda e, h=h, hh=hh, a=a, pn=pn: e.transpose(
                            out=pn.ap[0:16, hh * 256 + a * 128:hh * 256 + (a + 1) * 128], in_=numTs.ap[:, h, a, :], identity=identf.ap),
                          reads=[numTs, identf], writes=[pn] if (hh == 0 and a == 0) else [], pwrites=[] if (hh == 0 and a == 0) else [pn])
                for hh in range(2):
                    h = hb2 * 2 + hh
                    post(h, 16, pn.ap[0:16, hh * 256:(hh + 1) * 256], pn, None, None, None,
                         sgs.ap[0:16, h * 256:(h + 1) * 256], h, TH, rinv_ap=rinv_s.ap[:, h:h + 1], rinv_b=rinv_s, sig_b=sgs)

        lg = cs_(1); lb = cs_(2); bspb = cs_(0)
        DMA("sp", lg.ap, lnv_g.partition_broadcast(128), writes=[lg])
        DMA("sp", lb.ap, lnv_b.partition_broadcast(128), writes=[lb])
        DMA("sp", bspb.ap, b_sp.rearrange("h t -> (h t)").partition_broadcast(128), writes=[bspb])
        for hf in range(2):
            prs = pb(4 + hf, 0, 512)
            E("pe", lambda e, hf=hf, prs=prs: e.matmul(prs.ap, lhsT=onesb.ap, rhs=WT.ap.rearrange("p h t -> p (h t)")[:, hf * 512:(hf + 1) * 512],
                                                   start=True, stop=True), reads=[onesb, WT], writes=[prs])
            for cc in range(4):
                c = hf * 4 + cc
                E("dve", lambda e, c=c, cc=cc, prs=prs: e.scalar_tensor_tensor(
                    out=bspb.ap[:, c * 128:(c + 1) * 128], in0=prs.ap[:, cc * 128:(cc + 1) * 128], scalar=lbT.ap[:, c:c + 1],
                    in1=bspb.ap[:, c * 128:(c + 1) * 128], op0=ALU.mult, op1=ALU.add), reads=[prs, lbT, bspb], writes=[bspb])

        VG = ar(4, 0, 16384, F32); vgv = VG.ap.rearrange("p (i f) -> p i f", i=8)
        vn = ar(2, 0, 8192); vnv = vn.ap.rearrange("p (i f) -> p i f", i=8)
        vsb = fs(2048, 1024)
        yaT = ar(3, 0, 8 * TC); yav = V3(yaT, 8)
        SQs = [fs(3072, 512), fs(3584, 512)]
        for bi, blk in enumerate((2, 3)):
            wb = wvb[bi]
            for i in range(nt):
                tp, c0 = tinfo(i)
                ps = projA(lambda k, c0=c0, tp=tp: h2v[:, k, c0:c0 + tp], 8, lambda k, wb=wb: wb.ap[:, k, :], tp, [hT2], [wb])
                vg_ap = vgv[:tp, i, bi * 512:(bi + 1) * 512] if i < 8 else vsb.ap[:tp, bi * 512:(bi + 1) * 512]
                VGb = VG if i < 8 else vsb
                evac(vg_ap, ps.ap[:tp, :], [ps], pw=VGb, eng="act", func=AF.Gelu)
                SQ = SQs[i % 2]
                E("dve", lambda e, vg_ap=vg_ap, SQ=SQ, tp=tp: e.tensor_tensor(out=SQ.ap[:tp], in0=vg_ap, in1=vg_ap, op=ALU.mult),
                  reads=[VGb], writes=[SQ])
                E("dve", lambda e, vg_ap=vg_ap, tp=tp, i=i, bi=bi: e.tensor_reduce(
                    out=S1.ap[:tp, i, bi * 4:(bi + 1) * 4], in_=vg_ap.rearrange("p (h d) -> p h d", h=4), axis=AX.X, op=ALU.add),
                  reads=[VGb], pwrites=[S1])
                E("dve", lambda e, SQ=SQ, tp=tp, i=i, bi=bi: e.tensor_reduce(
                    out=S2.ap[:tp, i, bi * 4:(bi + 1) * 4], in_=SQ.ap[:tp].rearrange("p (h d) -> p h d", h=4), axis=AX.X, op=ALU.add),
                  reads=[SQ], pwrites=[S2])
        for bi, blk in enumerate((0, 1)):
            wb = load_wblk(blk)
            for cc in range(4):
                c = bi * 4 + cc

                def outf(ps, c0, n, c=c):
                    evac(yav[:, c, c0:c0 + n], ps.ap[:, 0:n], [ps], pw=yaT, eng="act", func=AF.Gelu)
                projB(pp, lambda k, cc=cc, wb=wb: wb.ap[:, k, cc * 128:(cc + 1) * 128], 8,
                      lambda k, c0, n: h2v[:, k, c0:c0 + n], outf, [wb], [hT2])
        nst = nt * 8
        fl2 = lambda b: b.ap.rearrange("p i g -> p (i g)")[:, 0:nst]
        cb128 = c128.ap[:, 0:1].broadcast_to([128, nst])
        E("pool", lambda e: e.tensor_tensor(out=fl2(MEAN), in0=fl2(S1), in1=cb128, op=ALU.mult), reads=[S1, c128], writes=[MEAN])
        E("pool", lambda e: e.tensor_tensor(out=fl2(S1), in0=fl2(MEAN), in1=fl2(MEAN), op=ALU.mult), reads=[MEAN], writes=[S1])
        E("pool", lambda e: e.tensor_tensor(out=fl2(S2), in0=fl2(S2), in1=cb128, op=ALU.mult), reads=[S2, c128], writes=[S2])
        E("pool", lambda e: e.tensor_tensor(out=fl2(S2), in0=fl2(S2), in1=fl2(S1), op=ALU.subtract), reads=[S2, S1], writes=[S2])
        E("pool", lambda e: e.tensor_tensor(out=fl2(S2), in0=fl2(S2), in1=epsc.ap[:, 0:1].broadcast_to([128, nst]), op=ALU.add),
          reads=[S2, epsc], writes=[S2])
        E("pool", lambda e: e.tensor_tensor(out=fl2(RSTD), in0=fl2(S2), in1=mhalf.ap[:, 0:1].broadcast_to([128, nst]), op=ALU.pow),
          reads=[S2, mhalf], writes=[RSTD])
        wza = [load_wblk(4), load_wblk(5)]
        NTs = [fs(0, 1024), fs(1024, 1024)]
        vs_f = vsb
        for i in range(nt):
            tp, c0 = tinfo(i)
            NT_ = NTs[i % 2]
            n3_ = NT_.ap[:tp].rearrange("p (h d) -> p h d", h=8)
            vsrc = vgv[:tp, i, :] if i < 8 else vsb.ap[:tp, :]
            E("dve", lambda e, i=i, tp=tp, n3_=n3_, vsrc=vsrc: e.tensor_tensor(
                out=n3_, in0=vsrc.rearrange("p (h d) -> p h d", h=8),
                in1=MEAN.ap[:tp, i, :, None].broadcast_to([tp, 8, 128]), op=ALU.subtract), reads=[VG if i < 8 else vsb, MEAN], writes=[NT_])
            if i < 8:
                E("dve", lambda e, i=i, tp=tp, n3_=n3_: e.tensor_tensor(
                    out=vnv[:tp, i, :].rearrange("p (h d) -> p h d", h=8), in0=n3_,
                    in1=RSTD.ap[:tp, i, :, None].broadcast_to([tp, 8, 128]), op=ALU.mult), reads=[NT_, RSTD], pwrites=[vn])
            else:
                E("dve", lambda e, i=i, tp=tp, n3_=n3_: e.tensor_tensor(
                    out=n3_, in0=n3_, in1=RSTD.ap[:tp, i, :, None].broadcast_to([tp, 8, 128]), op=ALU.mult), reads=[NT_, RSTD], writes=[NT_])
                E("pool", lambda e, tp=tp, NT_=NT_: e.tensor_tensor(out=NT_.ap[:tp], in0=NT_.ap[:tp], in1=lg.ap[:tp], op=ALU.mult),
                  reads=[NT_, lg], writes=[NT_])
                E("pool", lambda e, tp=tp, NT_=NT_: e.tensor_tensor(out=vs_f.ap[:tp], in0=NT_.ap[:tp], in1=lb.ap[:tp], op=ALU.add),
                  reads=[NT_, lb], writes=[vs_f])
                E("pool", lambda e, i=i, tp=tp: e.tensor_copy(out=vns.ap[:tp, :], in_=vs_f.ap[:tp]), reads=[vs_f], writes=[vns])
                out_dmas.append(DMA("sp", vs_o, vs_f.ap[0:16, :], reads=[vs_f]))
        wo = ar(4, 0, 16 * 1024); wov = wo.ap.rearrange("p (k n) -> p k n", k=16)
        DMA("pool", wov, w_out.rearrange("(k p) n -> p k n", p=128), writes=[wo])
        wg = ar(6, 0, 8 * 1024); wgv = wg.ap.rearrange("p (k n) -> p k n", k=8)
        DMA("pool", wgv, w_pg.rearrange("(k p) n -> p k n", p=128), writes=[wg])
        SZs = [fs(3072, 512), fs(3584, 512)]
        T1s = [fs(0, 512), fs(512, 512)]
        gi = 0
        for bi, blk in enumerate((4, 5)):
            wb = wza[bi]
            for cc in range(4):
                c = bi * 4 + cc
                for (c0, n) in colgroups(pp):
                    SZ = SZs[gi % 2]; T1 = T1s[gi % 2]; gi += 1
                    zps = pb(psrot[0] % 4, 0, 512); psrot[0] += 1
                    for k in range(8):
                        E("pe", lambda e, k=k, c0=c0, n=n, zps=zps, cc=cc, wb=wb: e.matmul(
                            zps.ap[:, 0:n], lhsT=wb.ap[:, k, cc * 128:(cc + 1) * 128], rhs=h2v[:, k, c0:c0 + n],
                            start=(k == 0), stop=(k == 7)), reads=[wb, hT2], writes=[zps] if k == 0 else [], pwrites=[] if k == 0 else [zps])
                    evac(SZ.ap[:, 0:n], zps.ap[:, 0:n], [zps], w=SZ, eng="act", func=AF.Silu)
                    sps = pb(4 + gi % 2, 0, 512)
                    if n == 512:
                        for jj in range(4):
                            j = c0 // 128 + jj
                            E("pe", lambda e, jj=jj, j=j, c=c, sps=sps: e.matmul(
                                sps.ap[:, jj * 128:(jj + 1) * 128], lhsT=vnv[:, j, c * 128:(c + 1) * 128], rhs=WT.ap[:, c, :],
                                start=True, stop=True), reads=[vn, WT], writes=[sps] if jj == 0 else [], pwrites=[] if jj == 0 else [sps])
                        E("dve", lambda e, sps=sps, T1=T1, c=c: e.scalar_tensor_tensor(
                            out=T1.ap.rearrange("p (j t) -> p j t", j=4), in0=sps.ap.rearrange("p (j t) -> p j t", j=4),
                            scalar=lgT.ap[:, c:c + 1],
                            in1=bspb.ap[:, None, c * 128:(c + 1) * 128].broadcast_to([128, 4, 128]), op0=ALU.mult, op1=ALU.add),
                          reads=[sps, bspb, lgT], writes=[T1])
                    else:
                        E("pe", lambda e, c=c, sps=sps: e.matmul(sps.ap[:, 0:16], lhsT=vns.ap[0:16, c * 128:(c + 1) * 128],
                                                                 rhs=Wdiag.ap[0:16, c, :], start=True, stop=True),
                          reads=[vns, Wdiag], writes=[sps])
                        E("dve", lambda e, sps=sps, T1=T1, c=c: e.tensor_scalar(out=T1.ap[:, 0:16], in0=sps.ap[:, 0:16],
                                                                                scalar1=bsp0.ap[:, c:c + 1], scalar2=None, op0=ALU.add),
                          reads=[sps, bsp0], writes=[T1])
                    E("dve", lambda e, c=c, c0=c0, n=n, SZ=SZ: e.tensor_tensor(out=yav[:, c, c0:c0 + n], in0=yav[:, c, c0:c0 + n],
                                                                               in1=SZ.ap[:, 0:n], op=ALU.mult),
                      reads=[yaT, SZ], pwrites=[yaT])
                    E("dve", lambda e, c=c, c0=c0, n=n, T1=T1: e.tensor_tensor(out=yav[:, c, c0:c0 + n], in0=yav[:, c, c0:c0 + n],
                                                                               in1=T1.ap[:, 0:n], op=ALU.mult),
                      reads=[yaT, T1], pwrites=[yaT])

        wp = ar(2, 0, 2 * 1024); wpv = wp.ap.rearrange("p (k n) -> p k n", k=2)
        DMA("pool", wpv, w_pp.rearrange("(k p) n -> p k n", p=128), writes=[wp])
        bpg = cs_(0); plg = cs_(1); fng = cs_(2)
        DMA("sp", bpg.ap, b_pg.partition_broadcast(128), writes=[bpg])
        DMA("sp", plg.ap, ple_g.partition_broadcast(128), writes=[plg])
        DMA("sp", fng.ap, fin_g.partition_broadcast(128), writes=[fng])
        xts = [fs(0, 1024), fs(1024, 1024)]
        X1s = [fs(2048, 1024), fs(3072, 1024)]
        pts = [ar(1, i * 512, 512, F32) for i in range(2)]
        X1b = ar(1, 1024, 1024); X1T = ar(1, 2048, 1024); ptb = ar(1, 3072, 256); PT2 = ar(1, 3328, 256)
        Gs = ar(1, 3584, 2048, F32); Ef = ar(1, 5632, 2048, F32)
        Yf = ar(2, 2048, 2048, F32); X2 = ar(2, 4096, 2048, F32); Tf = ar(2, 6144, 2048, F32)
        ptmp5 = [ptmps[0], ptmps[1]]
        PS = {}

        def f_L(i):
            tp, c0 = tinfo(i)
            xt = xts[i % 2]; pt = pts[i % 2]
            DMA("sp", xt.ap[:tp], xp[t0 + i * 128:t0 + (i + 1) * 128, :] if i < 8 else xs, writes=[xt], key=rkey("x5", i, 2))
            DMA("sp", pt.ap[:tp], pp_[t0 + i * 128:t0 + (i + 1) * 128, :] if i < 8 else psm, writes=[pt], key=rkey("p5", i, 2))

        def f_O(i, nbs=(0, 1)):
            tp, c0 = tinfo(i)
            if ("o", i) not in PS:
                PS[("o", i)] = [pb(0, 0, 512), pb(1, 0, 512)]
            pso = PS[("o", i)]
            for nb in nbs:
                for kc in range(16):
                    ysrc, yb_ = (yav, yaT) if kc < 8 else (ybv, ybT)
                    E("pe", lambda e, nb=nb, kc=kc, ysrc=ysrc: e.matmul(
                        pso[nb].ap[:tp, :], lhsT=ysrc[:, kc % 8, c0:c0 + tp], rhs=wov[:, kc, nb * 512:(nb + 1) * 512],
                        start=(kc == 0), stop=(kc == 15)), reads=[yb_, wo], writes=[pso[nb]] if kc == 0 else [],
                      pwrites=[] if kc == 0 else [pso[nb]])

        def f_X(i, nb):
            tp, c0 = tinfo(i)
            xt = xts[i % 2]; X1 = X1s[i % 2]; pso = PS[("o", i)]
            E("dve", lambda e: e.tensor_tensor(
                out=X1b.ap[:tp, nb * 512:(nb + 1) * 512], in0=pso[nb].ap[:tp, :], in1=xt.ap[:tp, nb * 512:(nb + 1) * 512], op=ALU.add),
              reads=[pso[nb], xt], writes=[X1b] if nb == 0 else [], pwrites=[] if nb == 0 else [X1b])
            E("dve", lambda e: e.tensor_tensor(
                out=X1.ap[:tp, nb * 512:(nb + 1) * 512], in0=pso[nb].ap[:tp, :], in1=xt.ap[:tp, nb * 512:(nb + 1) * 512], op=ALU.add),
              reads=[pso[nb], xt], writes=[X1] if nb == 0 else [], pwrites=[] if nb == 0 else [X1])

        def f_P(i):
            tp, c0 = tinfo(i)
            pt = pts[i % 2]
            E("dve", lambda e: e.tensor_copy(out=ptb.ap[:tp], in_=pt.ap[:tp]), reads=[pt], writes=[ptb])
            ppt = pb(7, 0, 128, BF16)
            for k in range(2):
                E("pe", lambda e, k=k: e.transpose(out=ppt.ap[:, k * 128:k * 128 + tp], in_=ptb.ap[:tp, k * 128:(k + 1) * 128],
                                                   identity=ident.ap[:tp, :tp]), reads=[ptb, ident],
                  writes=[ppt] if k == 0 else [], pwrites=[] if k == 0 else [ppt])
            evac(PT2.ap, ppt.ap, [ppt], w=PT2, eng="dve")
            pse = [pb(4, 0, 512), pb(5, 0, 512)]
            PS[("e", i)] = pse
            for nb in range(2):
                for k in range(2):
                    E("pe", lambda e, nb=nb, k=k: e.matmul(
                        pse[nb].ap[:tp, :], lhsT=PT2.ap[:, k * 128:k * 128 + tp], rhs=wpv[:, k, nb * 512:(nb + 1) * 512],
                        start=(k == 0), stop=(k == 1)), reads=[PT2, wp], writes=[pse[nb]] if k == 0 else [], pwrites=[] if k == 0 else [pse[nb]])

            for nb in range(2):
                E("act", lambda e, nb=nb: e.activation(
                    out=X2.ap[:tp, nb * 512:(nb + 1) * 512], in_=pse[nb].ap[:tp, :], func=AF.Square,
                    accum_out=stat2.ap[:tp, 2 * i + nb:2 * i + nb + 1]), reads=[pse[nb]],
                  pwrites=[stat2] if nb == 0 else [stat2, X2], writes=[X2] if nb == 0 else [])
            E("pool", lambda e: e.tensor_tensor(out=stat2.ap[:tp, 40 + i:41 + i], in0=stat2.ap[:tp, 2 * i:2 * i + 1],
                                                in1=stat2.ap[:tp, 2 * i + 1:2 * i + 2], op=ALU.add), reads=[stat2], pwrites=[stat2])
            rstd_pool(stat2.ap[:tp, 50 + i:51 + i], stat2.ap[:tp, 40 + i:41 + i], c1024, 1, tp, [stat2], stat2, ptmp5[0])

        def f_T(i, hf):
            tp, c0 = tinfo(i)
            if hf == 0:
                PS[("xt", i)] = pb(6, 0, 512, BF16)
                PS[("g", i)] = [pb(2, 0, 512), pb(3, 0, 512)]
            pxt = PS[("xt", i)]; psg = PS[("g", i)]
            ks = range(hf * 4, hf * 4 + 4)
            for k in ks:
                E("pe", lambda e, k=k: e.transpose(out=pxt.ap[:, k * 128:k * 128 + tp], in_=X1b.ap[:tp, k * 128:(k + 1) * 128],
                                                   identity=ident.ap[:tp, :tp]), reads=[X1b, ident],
                  writes=[pxt] if k == 0 else [], pwrites=[] if k == 0 else [pxt])
            E("dve", lambda e: e.tensor_copy(out=X1T.ap[:, hf * 512:(hf + 1) * 512], in_=pxt.ap[:, hf * 512:(hf + 1) * 512]),
              reads=[pxt], writes=[X1T] if hf == 0 else [], pwrites=[] if hf == 0 else [X1T])
            for nb in range(2):
                for k in ks:
                    E("pe", lambda e, nb=nb, k=k: e.matmul(
                        psg[nb].ap[:tp, :], lhsT=X1T.ap[:, k * 128:k * 128 + tp], rhs=wgv[:, k, nb * 512:(nb + 1) * 512],
                        start=(k == 0), stop=(k == 7)), reads=[X1T, wg], writes=[psg[nb]] if k == 0 else [], pwrites=[] if k == 0 else [psg[nb]])

        def f_G(i):
            tp, c0 = tinfo(i)
            psg = PS[("g", i)]
            for nb in range(2):
                E("dve", lambda e, nb=nb: e.tensor_tensor(
                    out=Tf.ap[:tp, nb * 512:(nb + 1) * 512], in0=psg[nb].ap[:tp, :], in1=bpg.ap[:tp, nb * 512:(nb + 1) * 512], op=ALU.add),
                  reads=[psg[nb], bpg], writes=[Tf] if nb == 0 else [], pwrites=[] if nb == 0 else [Tf])

        def f_E(i):
            tp, c0 = tinfo(i)
            pse = PS[("e", i)]
            for nb in range(2):
                E("dve", lambda e, nb=nb: e.scalar_tensor_tensor(
                    out=Ef.ap[:tp, nb * 512:(nb + 1) * 512], in0=pse[nb].ap[:tp, :], scalar=stat2.ap[:tp, 50 + i:51 + i],
                    in1=plg.ap[:tp, nb * 512:(nb + 1) * 512], op0=ALU.mult, op1=ALU.mult), reads=[pse[nb], stat2, plg],
                  writes=[Ef] if nb == 0 else [], pwrites=[] if nb == 0 else [Ef])
            E("act", lambda e: e.activation(out=Gs.ap[:tp], in_=Tf.ap[:tp], func=AF.Sigmoid), reads=[Tf], writes=[Gs])

        def f_R(i):
            tp, c0 = tinfo(i)
            X1 = X1s[i % 2]
            E("dve", lambda e: e.tensor_tensor(out=Gs.ap[:tp], in0=Gs.ap[:tp], in1=Ef.ap[:tp], op=ALU.mult), reads=[Gs, Ef], writes=[Gs])
            E("dve", lambda e: e.tensor_tensor(out=X2.ap[:tp], in0=X1.ap[:tp], in1=Gs.ap[:tp], op=ALU.add),
              reads=[X1, Gs], writes=[X2])
            E("act", lambda e: e.activation(out=Ef.ap[:tp], in_=X2.ap[:tp], func=AF.Square,
                                            accum_out=stat2.ap[:tp, 20 + i:21 + i]), reads=[X2], pwrites=[stat2], writes=[Ef])
            rstd_pool(stat2.ap[:tp, 30 + i:31 + i], stat2.ap[:tp, 20 + i:21 + i], c1024, 1, tp, [stat2], stat2, ptmp5[1])

        def f_Rb(i):
            tp, c0 = tinfo(i)
            E("dve", lambda e: e.scalar_tensor_tensor(out=Yf.ap[:tp], in0=X2.ap[:tp], scalar=stat2.ap[:tp, 30 + i:31 + i],
                                                      in1=fng.ap[:tp], op0=ALU.mult, op1=ALU.mult),
              reads=[X2, stat2, fng], writes=[Yf])
            dst = y_p[t0 + i * 128:t0 + (i + 1) * 128, :] if i < 8 else y_s
            out_dmas.append(DMA("sp", dst, Yf.ap[:tp], reads=[Yf], key=rkey("yo", i, 2)))

        f_L(0)
        if nt > 1:
            f_L(1)
        for t in range(nt):
            f_O(t)
            f_X(t, 0)
            f_T(t, 0)
            f_X(t, 1)
            f_T(t, 1)
            if t >= 2:
                f_Rb(t - 2)
            if t >= 1:
                f_E(t - 1)
            f_P(t)
            if t + 2 < nt:
                f_L(t + 2)
            f_G(t)
            if t >= 1:
                f_R(t - 1)
        if nt >= 2:
            f_Rb(nt - 2)
        f_E(nt - 1)
        f_R(nt - 1)
        f_Rb(nt - 1)

    if limit is not None:
        while True:
            pe_kept = [o for o in P.ops["pe"] if o.seq <= limit]
            if pe_kept and pe_kept[-1].open_group:
                limit += 1
            else:
                break
        for e_ in P.ENGS:
            P.ops[e_] = [o for o in P.ops[e_] if o.seq <= limit]
        out_dmas = [o for o in out_dmas if o.seq <= limit]
        for e_ in ("pe", "act", "dve", "pool"):
            if P.ops[e_]:
                out_dmas.append(P.ops[e_][-1])
        print("limit", limit, "of", Op._seq[0], {e_: len(P.ops[e_]) for e_ in P.ENGS})
    P.add("sp", lambda e: e.nop(), deps=out_dmas)
    P.emit()
    es.close()
    return nc


_NC_CACHE = {}


def kernel(**inp):
    f = lambda a: np.ascontiguousarray(np.asarray(a, dtype=np.float32))
    if "nc" not in _NC_CACHE:
        _NC_CACHE["nc"] = build()
    nc = _NC_CACHE["nc"]
    x_prompt = f(inp["x_prompt"]); x_sample = f(inp["x_sample"]).reshape(128, 1024)
    C = f(inp["state_mlstm_C"])[0]; n = f(inp["state_mlstm_n"])[0].reshape(128, 1024)
    m = f(inp["state_mlstm_m"])[0]; cv = f(inp["state_conv"])[0]
    p_prompt = f(inp["p_prompt"])[0]; p_sample = f(inp["p_sample"])[0].reshape(128, 256)
    shared = {
        "norm_g": f(inp["norm_in_g"])[0], "w_in": f(inp["w_in"])[0],
        "lnv_g": f(inp["ln_v_g"])[0].reshape(1024), "lnv_b": f(inp["ln_v_b"])[0].reshape(1024),
        "w_sp": f(inp["w_spatial"])[0], "b_sp": f(inp["b_spatial"])[0],
        "conv_w": f(inp["conv_w"])[0], "conv_b": f(inp["conv_b"])[0],
        "w_q": f(inp["w_q"])[0], "w_k": f(inp["w_k"])[0], "w_v": f(inp["w_v"])[0],
        "w_if": f(inp["w_if"])[0], "b_if": f(inp["b_if"])[0],
        "gn_g": f(inp["gn_g"])[0].reshape(1024), "skip": f(inp["skip"])[0],
        "w_out": f(inp["w_out"])[0], "w_pg": f(inp["w_ple_gate"])[0], "b_pg": f(inp["b_ple_gate"])[0],
        "w_pp": f(inp["w_ple_proj"])[0], "ple_g": f(inp["ple_norm_g"])[0], "fin_g": f(inp["final_norm_g"]),
    }
    in_maps = []
    for c in range(8):
        s = slice(c * 16, (c + 1) * 16)
        d = dict(shared)
        d.update({"xp": x_prompt[c], "xs": x_sample[s], "pp": p_prompt[c], "psm": p_sample[s],
                  "C_in": C[s], "n_in": n[s], "m_in": m[s], "cv_in": cv[s]})
        in_maps.append(d)
    res = run_bass_kernel_spmd(nc, in_maps, core_ids=list(range(8)))
    R = res.results
    cat = lambda k: np.concatenate([np.asarray(r[k]) for r in R], axis=0)
    stk = lambda k: np.stack([np.asarray(r[k]) for r in R], axis=0)
    y_prompt = stk("y_p")
    y_sample = cat("y_s").reshape(128, 1, 1024)
    Cp_ = stk("Cp")[None]
    np__ = stk("np_o")[None]
    mp_ = stk("mp_o").reshape(8, 4)[None]
    cvp_ = stk("cvp")[None]
    Cs_ = cat("Cs")[None]
    ns_ = cat("ns_o").reshape(128, 4, 256)[None]
    ms_ = cat("ms_o")[None]
    cvs_ = cat("cvs")[None]
    vs_ = cat("vs_o").reshape(128, 1, 1024)[None]
    return (y_prompt.astype(np.float32), y_sample.astype(np.float32), Cp_.astype(np.float32), np__.astype(np.float32),
            mp_.astype(np.float32), cvp_.astype(np.float32), Cs_.astype(np.float32), ns_.astype(np.float32),
            ms_.astype(np.float32), cvs_.astype(np.float32), vs_.astype(np.float32))
```

```python
import numpy as np
from contextlib import ExitStack
import concourse.bass as bass
import concourse.mybir as mybir
from concourse.bass_utils import run_bass_kernel_spmd

F32 = mybir.dt.float32
BF16 = mybir.dt.bfloat16
AF = mybir.ActivationFunctionType
ALU = mybir.AluOpType
AX = mybir.AxisListType

T = 2048
TH = 1024
NS = 16
TC = TH + NS
XW = TC + 3
SLOT = 8352
NSLOT = 7
LN16 = 2.772588722239781
EPS = 1e-6
DEBUG = {}


class Op:
    __slots__ = ("eng", "fn", "deps", "ticket", "needed", "dkey", "dval", "seq", "open_group")
    _seq = [0]

    def __init__(self, eng, fn, deps):
        Op._seq[0] += 1
        self.seq = Op._seq[0]
        self.eng = eng
        self.fn = fn
        self.deps = deps
        self.ticket = None
        self.needed = False
        self.dkey = None
        self.dval = None


class _Rec:
    def __getattr__(self, name):
        def f(*a, **k):
            self.__dict__["call"] = (name, a, k)
            return None
        return f


class Prog:
    ENGS = ("pe", "act", "dve", "pool", "sp")

    def __init__(self, nc):
        self.nc = nc
        self.ops = {e: [] for e in self.ENGS}
        self.dcount = {}

    def add(self, eng, fn, deps=()):
        dl = [d for d in deps if d is not None]
        rec = _Rec()
        fn(rec)
        name, a, k = rec.call
        fn = (lambda e, name=name, a=a, k=k: getattr(e, name)(*a, **k))
        op = Op(eng, fn, dl)
        op.open_group = (name == "matmul" and k.get("stop") is False)
        for d in dl:
            d.needed = True
        self.ops[eng].append(op)
        return op

    def dma(self, queue, out, in_, key, deps=(), **kw):
        def fn(e):
            return e.dma_start(out=out, in_=in_, **kw)

        op = self.add(queue, fn, deps)
        self.dcount[key] = self.dcount.get(key, 0) + 16
        op.dkey = key
        op.dval = self.dcount[key]
        return op

    def emit(self):
        nc = self.nc
        with ExitStack() as es:
            esem = {e: es.enter_context(nc.semaphore("s_" + e)) for e in self.ENGS}
            dsem = {k: es.enter_context(nc.semaphore("d_%s" % (k,))) for k in self.dcount}
            for e in self.ENGS:
                c = 0
                for op in self.ops[e]:
                    if op.dkey is None and op.needed:
                        c += 1
                        op.ticket = c
            block = es.enter_context(nc.Block())

            def run(ename, eng):
                waited = {}
                for op in self.ops[ename]:
                    for d in op.deps:
                        if d.dkey is not None:
                            s, v = dsem[d.dkey], d.dval
                        else:
                            s, v = esem[d.eng], d.ticket
                        if waited.get(s.name, 0) < v:
                            eng.wait_ge(s, v)
                            waited[s.name] = v
                    ins = op.fn(eng)
                    if op.dkey is not None:
                        ins.then_inc(dsem[op.dkey], 16)
                    elif op.needed:
                        ins.then_inc(esem[ename], 1)

            @block.tensor
            def _(e):
                run("pe", e)

            @block.scalar
            def _(e):
                run("act", e)

            @block.vector
            def _(e):
                run("dve", e)

            @block.gpsimd
            def _(e):
                run("pool", e)

            @block.sync
            def _(e):
                run("sp", e)


def _key(op):
    return ("d", op.dkey) if op.dkey is not None else ("e", op.eng)


class Buf:
    registry = []

    def __init__(self, ap, tname=None, lo=0, hi=0, psum=False):
        self.psum = psum
        self.ap = ap
        self.wr = {}
        self.rd = {}
        self.old = {}
        if tname is not None:
            for (tn, l, h, b) in Buf.registry:
                if tn == tname and l < hi and lo < h:
                    for d in (b.old, b.wr, b.rd):
                        for k, v in d.items():
                            self._put(self.old, k, v)
            Buf.registry.append((tname, lo, hi, self))

    @staticmethod
    def _put(d, k, op):
        cur = d.get(k)
        if cur is None:
            d[k] = op
        elif op.dkey is not None:
            if op.dval > cur.dval:
                d[k] = op
        elif op.seq > cur.seq:
            d[k] = op

    def __getitem__(self, idx):
        return self.ap[idx]

    def rdeps(self, eng):
        out = [op for k, op in self.wr.items() if not (eng == "pe" and k == ("e", "pe"))]
        if self.psum:
            out += [op for k, op in self.rd.items() if k != ("e", eng)]
        return out

    def wdeps(self, eng, partial=False):
        skip = ("e", "pe") if eng == "pe" else None
        out = [op for k, op in self.old.items() if k != skip]
        if not partial:
            out += [op for k, op in self.rd.items() if k != skip]
            out += [op for k, op in self.wr.items() if k != skip]
        return out

    def read(self, op):
        self._put(self.rd, _key(op), op)

    def wrote(self, op, partial=False):
        if not partial:
            self.wr = {}
            self.rd = {}
            self.old = {}
        self._put(self.wr, _key(op), op)


def build(limit=None):
    Buf.registry = []
    Op._seq[0] = 0
    nc = bass.Bass("TRN2", target_bir_lowering=False)
    P = Prog(nc)
    es = ExitStack()

    def din(name, shape):
        return nc.dram_tensor(name, list(shape), F32, kind="ExternalInput").ap()

    def dout(name, shape):
        return nc.dram_tensor(name, list(shape), F32, kind="ExternalOutput").ap()

    xp = din("xp", [T, 1024]); xs = din("xs", [NS, 1024])
    pp_ = din("pp", [T, 256]); psm = din("psm", [NS, 256])
    C_in = din("C_in", [NS, 4, 256, 256]); n_in = din("n_in", [NS, 1024]); m_in = din("m_in", [NS, 4])
    cv_in = din("cv_in", [NS, 3, 1024])
    norm_g = din("norm_g", [1024]); w_in = din("w_in", [1024, 6144])
    lnv_g = din("lnv_g", [1024]); lnv_b = din("lnv_b", [1024])
    w_sp = din("w_sp", [8, 128, 128]); b_sp = din("b_sp", [8, 128])
    conv_w = din("conv_w", [4, 1024]); conv_b = din("conv_b", [1024])
    w_q = din("w_q", [4, 256, 256]); w_k = din("w_k", [4, 256, 256]); w_v = din("w_v", [4, 256, 256])
    w_if = din("w_if", [3072, 8]); b_if = din("b_if", [8])
    gn_g = din("gn_g", [1024]); skip = din("skip", [1024])
    w_out = din("w_out", [2048, 1024]); w_pg = din("w_pg", [1024, 1024]); b_pg = din("b_pg", [1024])
    w_pp = din("w_pp", [256, 1024]); ple_g = din("ple_g", [1024]); fin_g = din("fin_g", [1024])

    y_p = dout("y_p", [T, 1024]); y_s = dout("y_s", [NS, 1024])
    Cp = dout("Cp", [4, 256, 256]); np_o = dout("np_o", [4, 256]); mp_o = dout("mp_o", [1, 4])
    cvp = dout("cvp", [3, 1024])
    Cs = dout("Cs", [NS, 4, 256, 256]); ns_o = dout("ns_o", [NS, 1024]); ms_o = dout("ms_o", [NS, 4])
    cvs = dout("cvs", [NS, 3, 1024]); vs_o = dout("vs_o", [NS, 1024])

    hscr = nc.dram_tensor("hscr", [128, 8 * TC], BF16, kind="Internal").ap()
    hscrb = Buf(hscr, "hscr", 0, 1)
    out_dmas = []
    cnt = [0]

    def sbt(shape, dt, name=None):
        cnt[0] += 1
        return es.enter_context(nc.sbuf_tensor(name or ("t%d" % cnt[0]), list(shape), dt))

    def newbuf(shape, dt):
        t = sbt(shape, dt)
        ap = t[:] if len(shape) == 2 else t[tuple(slice(None) for _ in shape)]
        b = Buf(ap, "nb%d" % cnt[0], 0, 1)
        b.tname = "nb%d" % cnt[0]
        return b

    def rebuf(b):
        nb_ = Buf(b.ap, b.tname, 0, 1)
        nb_.tname = b.tname
        return nb_

    def E(eng, fn, reads=(), writes=(), pwrites=(), extra=(), war=()):
        deps = list(extra)
        for b in war:
            deps += [op for k, op in b.rd.items()]
        for b in reads:
            deps += b.rdeps(eng)
        for b in writes:
            deps += b.wdeps(eng)
        for b in pwrites:
            deps += b.wdeps(eng, partial=True)
        op = P.add(eng, fn, deps)
        for b in reads:
            b.read(op)
        for b in writes:
            b.wrote(op)
        for b in pwrites:
            b.wrote(op, partial=True)
        return op

    dkc = [0]

    def DMA(queue, out, in_, reads=(), writes=(), pwrites=(), key=None, **kw):
        deps = []
        for b in reads:
            deps += b.rdeps(queue)
        for b in writes:
            deps += b.wdeps(queue)
        for b in pwrites:
            deps += b.wdeps(queue, partial=True)
        if key is None:
            dkc[0] += 1
            key = "k%d" % dkc[0]
        op = P.dma(queue, out, in_, key, deps, **kw)
        for b in reads:
            b.read(op)
        for b in writes:
            b.wrote(op)
        for b in pwrites:
            b.wrote(op, partial=True)
        return op

    def rkey(prefix, i, n):
        return "%s%d" % (prefix, i % n)

    arena = sbt([128, NSLOT * SLOT], BF16, "arena")
    FS = 4096
    fscr = sbt([128, FS], F32, "fscr")
    cst = sbt([128, 3, 1024], F32, "cst")

    def ar(slot, off, n, dt=BF16):
        lo = slot * SLOT + off
        ap = arena[:, lo:lo + n]
        if dt == F32:
            ap = ap.bitcast(F32)
        return Buf(ap, "arena", lo * 2, (lo + n) * 2)

    def fs(off, n, dt=F32):
        ap = fscr[:, off:off + n]
        if dt == BF16:
            ap = ap.bitcast(BF16)
        return Buf(ap, "fscr", off * 4, (off + n) * 4)

    def cs_(i):
        return Buf(cst[:, i, :], "cst", i * 4096, (i + 1) * 4096)

    banks = [es.enter_context(nc.psum_tensor("bank%d" % i, [128, 512], F32)) for i in range(8)]

    def pb(bank, off, n, dt=F32):
        ap = banks[bank][:, off:off + n]
        if dt == BF16:
            ap = ap.bitcast(BF16)
        return Buf(ap, "bank%d" % bank, 0, 2048, psum=True)

    ident = newbuf([128, 128], BF16); identf = newbuf([128, 128], F32)
    maskT = newbuf([128, 128], BF16); tri = newbuf([128, 128], F32)
    onesf = newbuf([128, 128], F32); onesb = newbuf([128, 128], BF16)
    epsc = newbuf([128, 1], F32); mhalf = newbuf([128, 1], F32); onec = newbuf([128, 1], F32)
    nln16 = newbuf([128, 1], F32)
    monec = newbuf([128, 1], F32)
    c1024 = newbuf([128, 1], F32); c128 = newbuf([128, 1], F32); c256 = newbuf([128, 1], F32)

    def memset(b, val, eng="pool"):
        return E(eng, lambda e: e.memset(b.ap, val), writes=[b])

    memset(identf, 1.0)
    E("pool", lambda e: e.affine_select(out=identf.ap, in_=identf.ap, pattern=[[-1, 128]], compare_op=ALU.is_equal,
                                        fill=0.0, base=0, channel_multiplier=1), reads=[identf], writes=[identf])
    E("pool", lambda e: e.tensor_copy(out=ident.ap, in_=identf.ap), reads=[identf], writes=[ident])
    memset(tri, 1.0)
    E("pool", lambda e: e.affine_select(out=tri.ap, in_=tri.ap, pattern=[[1, 128]], compare_op=ALU.is_ge,
                                        fill=0.0, base=0, channel_multiplier=-1), reads=[tri], writes=[tri])
    E("pool", lambda e: e.tensor_copy(out=maskT.ap, in_=tri.ap), reads=[tri], writes=[maskT])
    memset(onesf, 1.0); memset(onesb, 1.0)
    memset(epsc, EPS); memset(mhalf, -0.5); memset(onec, 1.0); memset(nln16, -LN16); memset(monec, -1.0)
    memset(c1024, 1.0 / 1024); memset(c128, 1.0 / 128); memset(c256, 1.0 / 256)

    def rstd_pool(out_ap, in_ap, cinv, n, tp, rbufs, wbuf, tmpb):
        def bc(c):
            return c.ap[:tp, 0:1] if n == 1 else c.ap[:tp, 0:1].broadcast_to([tp, n])
        E("pool", lambda e: e.tensor_tensor(out=tmpb.ap[:tp, 0:n], in0=in_ap, in1=bc(cinv), op=ALU.mult),
          reads=list(rbufs) + [cinv], writes=[tmpb])
        E("pool", lambda e: e.tensor_tensor(out=tmpb.ap[:tp, 0:n], in0=tmpb.ap[:tp, 0:n], in1=bc(epsc), op=ALU.add),
          reads=[tmpb, epsc], writes=[tmpb])
        return E("pool", lambda e: e.tensor_tensor(out=out_ap, in0=tmpb.ap[:tp, 0:n], in1=bc(mhalf), op=ALU.pow),
                 reads=[tmpb, mhalf], pwrites=[wbuf])

    wq = newbuf([128, 4, 2, 256], BF16); wk = newbuf([128, 4, 2, 256], BF16); wv = newbuf([128, 4, 2, 256], BF16)
    wif = newbuf([128, 24, 8], BF16)
    bifb = newbuf([128, 8], F32)
    gngT = newbuf([128, 8], F32); skipT = newbuf([128, 8], F32); cbT = newbuf([128, 8], F32)
    lgT = newbuf([128, 8], F32); lbT = newbuf([128, 8], F32)
    cwT = newbuf([128, 4, 8], F32)
    bsp0 = newbuf([128, 8], F32)
    W00 = newbuf([16, 8], F32)
    WT = newbuf([128, 8, 128], BF16)
    Wdiag = newbuf([16, 8, 16], BF16)

    def deferred_setup():
        for wb_, src in ((wq, w_q), (wk, w_k), (wv, w_v)):
            DMA("pool", wb_.ap, src.rearrange("h (kk p) e -> p h kk e", p=128), writes=[wb_])
        DMA("pool", wif.ap, w_if.rearrange("(k p) g -> p k g", p=128), writes=[wif])
        DMA("sp", bifb.ap, b_if.partition_broadcast(128), writes=[bifb])
        for b_, src in ((gngT, gn_g), (skipT, skip), (cbT, conv_b), (lgT, lnv_g), (lbT, lnv_b)):
            DMA("sp", b_.ap, src.rearrange("(c p) -> p c", p=128), writes=[b_], allow_slow_non_contiguous=True)
        DMA("sp", cwT.ap, conv_w.rearrange("j (c p) -> p j c", p=128), writes=[cwT], allow_slow_non_contiguous=True)
        DMA("sp", bsp0.ap, b_sp[:, 0].partition_broadcast(128), writes=[bsp0], allow_slow_non_contiguous=True)
        DMA("sp", W00.ap, w_sp[:, 0, 0].partition_broadcast(16), writes=[W00], allow_slow_non_contiguous=True)
        wspf = fs(0, 1024)
        DMA("sp", wspf.ap.rearrange("p (h s) -> p h s", h=8), w_sp.rearrange("h t s -> t h s"), writes=[wspf])
        wtf = fs(1024, 1024)
        for half in range(2):
            pw_ = pb(half, 0, 512)
            for hh in range(4):
                h = half * 4 + hh
                E("pe", lambda e, h=h, hh=hh, pw_=pw_: e.transpose(out=pw_.ap[:, hh * 128:(hh + 1) * 128],
                                                                   in_=wspf.ap[:, h * 128:(h + 1) * 128], identity=identf.ap),
                  reads=[wspf, identf], pwrites=[pw_])
            E("act", lambda e, half=half, pw_=pw_: e.copy(out=wtf.ap[:, half * 512:(half + 1) * 512], in_=pw_.ap),
              reads=[pw_], pwrites=[wtf])
        E("pool", lambda e: e.affine_select(out=WT.ap, in_=wtf.ap.rearrange("p (h t) -> p h t", h=8),
                                            pattern=[[0, 8], [1, 128]], compare_op=ALU.is_ge, fill=0.0, base=0,
                                            channel_multiplier=-1), reads=[wtf], writes=[WT])
        E("dve", lambda e: e.tensor_tensor(out=Wdiag.ap, in0=identf.ap[0:16, None, 0:16].broadcast_to([16, 8, 16]),
                                           in1=W00.ap[:, :, None].broadcast_to([16, 8, 16]), op=ALU.mult),
          reads=[identf, W00], writes=[Wdiag])

    CT = newbuf([128, 4, 2, 256], F32); nT = newbuf([128, 4, 2], F32)
    memset(CT, 0.0); memset(nT, 0.0)
    mcar = newbuf([1, 4], F32)
    memset(mcar, 0.0)
    tails = newbuf([128, 8, 3], BF16)
    memset(tails, 0.0)
    G_g = newbuf([128, 8, 9], F32)
    E1 = newbuf([128, 4, 8], F32); LFN = newbuf([128, 4, 8], F32); BNs = newbuf([128, 4, 8], F32)
    Csb = newbuf([128, 4, 8], F32); A1 = newbuf([128, 4, 8], F32)
    WS = newbuf([128, 4, 8], F32); FL = newbuf([128, 4, 8], F32); GSb = newbuf([128, 4, 8], F32)
    cmaxc = newbuf([32, 1], F32)
    rowA = newbuf([1, 32], F32); rowB = newbuf([1, 32], F32); MN = newbuf([1, 32], F32); MPV = newbuf([1, 32], F32)
    RG = newbuf([1, 64], F32); rtmp = newbuf([1, 32], F32)
    S12 = newbuf([16, 12], F32); sg = newbuf([16, 64], F32)
    m_s = newbuf([16, 4], F32)
    BD = newbuf([16, 16, 12], F32); bcs = newbuf([128, 16, 12], F32)
    rinv_s = newbuf([16, 4], F32)
    vTs = newbuf([128, 8, 16], F32); bufT = newbuf([128, 8, 3, 16], F32)
    qks = newbuf([16, 2, 1024], BF16)
    sgs = newbuf([16, 1024], BF16)
    vns = newbuf([16, 1024], BF16)
    CqT = newbuf([128, 4, 2, 16], F32); wvT = newbuf([128, 4, 2, 16], F32); numTs = newbuf([128, 4, 2, 16], F32)
    stat_g = newbuf([128, 64], F32)
    stat2_g = newbuf([128, 64], F32)
    ptmp = newbuf([128, 80], F32)
    ptmps = [newbuf([128, 2], F32), newbuf([128, 2], F32)]
    S1_g = newbuf([128, 9, 8], F32); S2_g = newbuf([128, 9, 8], F32); MEAN = newbuf([128, 9, 8], F32)
    RSTD = newbuf([128, 9, 8], F32)
    memset(S1_g, 0.0); memset(S2_g, 0.0)

    DMA("sp", m_s.ap, m_in, writes=[m_s])
    out_dmas.append(P.dma("sp", cvs[:, 0:2, :], cv_in[:, 1:3, :], "cvcp"))

    NWB = 2
    wblk = [newbuf([128, 8, 512], BF16) for _ in range(NWB)]
    wbi = [0]

    def load_wblk(blk):
        b = wblk[wbi[0] % NWB]
        DMA("pool", b.ap, w_in[:, blk * 512:(blk + 1) * 512].rearrange("(k p) n -> p k n", p=128), writes=[b],
            key=rkey("wb", wbi[0], NWB))
        wbi[0] += 1
        return b

    psrot = [0]

    def colgroups(pp):
        g = [(0, 512), (512, 512)]
        if pp == 0:
            g.append((1024, 16))
        return g

    def ntiles(pp):
        return 9 if pp == 0 else 8

    def tinfo(i):
        return (128, i * 128) if i < 8 else (NS, TH)

    evrot = [0]

    def evac(out_ap, in_ap, reads, pw=None, w=None, eng=None, func=None, scale=1.0, bias=None):
        if eng is None:
            eng = "act" if (func is not None or evrot[0] % 2 == 0) else "dve"
            evrot[0] += 1
        kw = dict(reads=list(reads), pwrites=[pw] if pw else [], writes=[w] if w else [])
        if eng == "act":
            f = func or AF.Copy
            if bias is not None:
                kw["reads"].append(bias[0])
                return E("act", lambda e: e.activation(out=out_ap, in_=in_ap, func=f, scale=scale, bias=bias[1]), **kw)
            return E("act", lambda e: e.activation(out=out_ap, in_=in_ap, func=f, scale=scale), **kw)
        return E(eng, lambda e: e.tensor_copy(out=out_ap, in_=in_ap), **kw)

    def pipeline(n, stages):
        S = len(stages)
        for t in range(n + S - 1):
            for s_ in reversed(range(S)):
                i = t - s_
                if 0 <= i < n:
                    stages[s_](i)

    def phase_hT(pp, hT, hook=None, hTg=None):
        t0 = pp * TH
        stat = rebuf(stat_g)
        gin = cs_(0)
        DMA("sp", gin.ap, norm_g.partition_broadcast(128), writes=[gin])
        NXB = 6
        xts = [ar(2 + i // 4, (i % 4) * 2048, 2048, F32) for i in range(NXB)]
        hbs = [fs(3072, 512, BF16), fs(3584, 512, BF16)]
        ptrs = [pb(0, 0, 512, BF16), pb(1, 0, 512, BF16)]
        pjunk = Buf(cst[:, 1, 0:512].bitcast(BF16), "cst", 4096, 4096 + 2048)
        hTv_ = hT.ap.rearrange("p (c t) -> p c t", c=8)

        def s0(i):
            tp, c0 = tinfo(i)
            xt = xts[i % NXB]
            src = xp[t0 + i * 128:t0 + (i + 1) * 128, :] if i < 8 else xs
            DMA("sp", xt.ap[:tp], src, writes=[xt], key=rkey("x", i, NXB))
            E("act", lambda e: e.activation(out=pjunk.ap[:tp], in_=xt.ap[:tp], func=AF.Square,
                                            accum_out=stat.ap[:tp, i:i + 1]), reads=[xt], pwrites=[stat], writes=[pjunk])
            rstd_pool(stat.ap[:tp, 16 + i:17 + i], stat.ap[:tp, i:i + 1], c1024, 1, tp, [stat], stat, ptmps[i % 2])
            if i == 2 and hook is not None:
                hook()

        def s1(i):
            tp, c0 = tinfo(i)
            xt = xts[i % NXB]; hb = hbs[i % 2]
            E("dve", lambda e: e.scalar_tensor_tensor(
                out=hb.ap[:tp], in0=xt.ap[:tp], scalar=stat.ap[:tp, 16 + i:17 + i], in1=gin.ap[:tp],
                op0=ALU.mult, op1=ALU.mult), reads=[xt, stat, gin], writes=[hb])

        def s2(i):
            tp, c0 = tinfo(i)
            hb = hbs[i % 2]; ptr = ptrs[i % 2]
            for k in range(8):
                E("pe", lambda e, k=k: e.transpose(
                    out=ptr.ap[:, k * 128:k * 128 + tp], in_=hb.ap[:tp, k * 128:(k + 1) * 128],
                    identity=ident.ap[:tp, :tp]), reads=[hb, ident], writes=[ptr] if k == 0 else [], pwrites=[] if k == 0 else [ptr])
            evac(hTv_[:, :, c0:c0 + tp], ptr.ap.rearrange("p (c t) -> p c t", c=8)[:, :, 0:tp], [ptr],
                 pw=hTg[i // 4 if i < 8 else 2])

        pipeline(ntiles(pp), [s0, s1, s2])

    def projB(pp, lhs_fn, nk, rhs_fn, out_fn, lbufs, rbufs, groups=None, rb_fn=None):
        for (c0, n) in (groups if groups is not None else colgroups(pp)):
            ps = pb(psrot[0] % 6, 0, 512); psrot[0] += 1
            rb = list(rbufs) if rb_fn is None else [rb_fn(c0)]
            for k in range(nk):
                E("pe", lambda e, k=k, c0=c0, n=n, ps=ps: e.matmul(ps.ap[:, 0:n], lhsT=lhs_fn(k), rhs=rhs_fn(k, c0, n),
                                                                   start=(k == 0), stop=(k == nk - 1)),
                  reads=list(lbufs) + rb, writes=[ps] if k == 0 else [], pwrites=[] if k == 0 else [ps])
            out_fn(ps, c0, n)

    def projA(lhs_fn, nk, rhs_fn, tp, lbufs, rbufs, bank=None):
        if bank is None:
            bank = psrot[0] % 6; psrot[0] += 1
        ps = pb(bank, 0, 512)
        for k in range(nk):
            E("pe", lambda e, k=k, ps=ps: e.matmul(ps.ap[:tp, :], lhsT=lhs_fn(k), rhs=rhs_fn(k),
                                                   start=(k == 0), stop=(k == nk - 1)),
              reads=list(lbufs) + list(rbufs), writes=[ps] if k == 0 else [], pwrites=[] if k == 0 else [ps])
        return ps

    V3 = lambda b, c: b.ap.rearrange("p (c t) -> p c t", c=c)

    for pp in range(2):
        t0 = pp * TH
        G = rebuf(G_g); stat2 = rebuf(stat2_g); S1 = rebuf(S1_g); S2 = rebuf(S2_g)
        NC_ = TC if pp == 0 else TH
        nt = ntiles(pp)
        wxb = []
        hT = ar(0, 0, 8 * TC)
        hTg = [Buf(hT.ap, "arena", 0, 8 * TC * 2) for _ in range(3)]
        hTl = hTg if pp == 0 else hTg[0:2]
        hgrp = lambda c0: hTg[min(c0 // 512, 2)]
        phase_hT(pp, hT, hook=lambda: wxb.extend([load_wblk(6), load_wblk(7)]), hTg=hTg)
        if pp == 0:
            deferred_setup()
        hTv = V3(hT, 8)
        hscrb = Buf(hscr, "hscr", 0, 1)
        DMA("sp", hscr.rearrange("p (c t) -> p c t", c=8)[:, :, 0:NC_], hTv[:, :, 0:NC_], reads=hTl, writes=[hscrb])
        xbT = ar(1, 0, 8 * XW); xbv = V3(xbT, 8)
        xcT = ar(2, 0, 8 * TC); xcv = V3(xcT, 8)
        qT = ar(3, 0, 8 * TC); qv = V3(qT, 8)
        kT = ar(4, 0, 8 * TC); kv = V3(kT, 8)
        vaug = ar(5, 0, 8 * 4 * 257); vav = vaug.ap.rearrange("p (i h e) -> p i h e", i=8, h=4)
        vT = ar(6, 0, 8 * TC); vv = V3(vT, 8)

        if pp == 0:
            cvtok = ar(6, 0, 6144, F32)
            DMA("sp", cvtok.ap[0:16, :], cv_in.rearrange("b j c -> b (j c)"), writes=[cvtok])
            pcv = pb(7, 0, 384)
            for j in range(3):
                for c in range(8):
                    idx = c * 3 + j
                    E("pe", lambda e, j=j, c=c, idx=idx: e.transpose(
                        out=pcv.ap[:, idx * 16:(idx + 1) * 16],
                        in_=cvtok.ap[0:16, j * 1024 + c * 128:j * 1024 + (c + 1) * 128], identity=identf.ap[0:16, 0:16]),
                      reads=[cvtok, identf], pwrites=[pcv])
            evac(bufT.ap.rearrange("p c j b -> p (c j b)"), pcv.ap, [pcv], w=bufT, eng="dve")
        E("dve", lambda e: e.tensor_copy(out=xbv[:, :, 0:3], in_=tails.ap), reads=[tails], pwrites=[xbT])
        xbtok = fs(3072, 1024)
        tok_tile = 8 if pp == 0 else 7
        for grp in colgroups(pp):
            for bi, blk in enumerate((6, 7)):
                wb = wxb[bi]
                for cc in range(4):
                    c = bi * 4 + cc

                    def outf(ps, c0, n, c=c):
                        evac(xbv[:, c, 3 + c0:3 + c0 + n], ps.ap[:, 0:n], [ps], pw=xbT)
                    projB(pp, lambda k, cc=cc, wb=wb: wb.ap[:, k, cc * 128:(cc + 1) * 128], 8,
                          lambda k, c0, n: hTv[:, k, c0:c0 + n], outf, [wb], [], groups=[grp], rb_fn=hgrp)
        for bi, blk in enumerate((6, 7)):
            wb = wxb[bi]
            tp, tc0 = tinfo(tok_tile)
            ps = projA(lambda k: hTv[:, k, tc0:tc0 + tp], 8, lambda k, wb=wb: wb.ap[:, k, :], tp, [hgrp(tc0)], [wb])
            evac(xbtok.ap[:tp, bi * 512:(bi + 1) * 512], ps.ap[:tp, :], [ps], pw=xbtok)
        if pp == 0:
            out_dmas.append(DMA("sp", cvs[:, 2, :], xbtok.ap[0:16, :], reads=[xbtok]))
        else:
            out_dmas.append(DMA("sp", cvp, xbtok.ap[125:128, :], reads=[xbtok]))
        if pp == 0:
            E("dve", lambda e: e.tensor_copy(out=tails.ap, in_=xbv[:, :, 3 + TH - 3:3 + TH]), reads=[xbT], writes=[tails])

        Dws = [fs(0, 256, BF16), fs(256, 256, BF16)]
        for c in range(8):
            Dw = Dws[c % 2]
            d3 = Dw.ap.rearrange("p (j m) -> p j m", j=4)
            E("dve", lambda e, c=c, d3=d3: e.tensor_tensor(out=d3, in0=identf.ap[:, None, :].broadcast_to([128, 4, 128]),
                                                          in1=cwT.ap[:, :, c:c + 1].broadcast_to([128, 4, 128]), op=ALU.mult),
              reads=[identf, cwT], writes=[Dw])
            for g_ in range(2):
                ps = pb(psrot[0] % 6, 0, 512); psrot[0] += 1
                for j in range(4):
                    E("pe", lambda e, c=c, j=j, g_=g_, ps=ps, d3=d3: e.matmul(
                        ps.ap, lhsT=d3[:, j, :], rhs=xbv[:, c, j + g_ * 512:j + g_ * 512 + 512], start=(j == 0), stop=(j == 3)),
                      reads=[Dw, xbT], writes=[ps] if j == 0 else [], pwrites=[] if j == 0 else [ps])
                evac(xcv[:, c, g_ * 512:(g_ + 1) * 512], ps.ap, [ps], pw=xcT, eng="act", func=AF.Silu, bias=(cbT, cbT.ap[:, c:c + 1]))
        if pp == 0:
            accS = fs(2048, 128)
            av = accS.ap.rearrange("p (c b) -> p c b", c=8)
            for c in range(8):
                E("dve", lambda e, c=c: e.tensor_scalar(out=av[:, c, :], in0=xbv[:, c, 3 + TH:3 + TC], scalar1=cwT.ap[:, 3, c:c + 1],
                                                       scalar2=None, op0=ALU.mult), reads=[xbT, cwT], pwrites=[accS])
                for j in range(3):
                    E("dve", lambda e, c=c, j=j: e.scalar_tensor_tensor(
                        out=av[:, c, :], in0=bufT.ap[:, c, j, :], scalar=cwT.ap[:, j, c:c + 1], in1=av[:, c, :],
                        op0=ALU.mult, op1=ALU.add), reads=[bufT, cwT, accS], pwrites=[accS])
                evac(xcv[:, c, TH:TC], av[:, c, :], [accS], pw=xcT, eng="act", func=AF.Silu, bias=(cbT, cbT.ap[:, c:c + 1]))

        E("pool", lambda e: e.memset(vav[:, :, :, 256:257], 1.0), pwrites=[vaug])
        for (dst, dv, wsrc, src, sv, soff) in ((qT, qv, wq, xcT, xcv, 0), (kT, kv, wk, xcT, xcv, 0), (vT, vv, wv, xbT, xbv, 3)):
            for h in range(4):
                for ec in range(2):
                    def outf(ps, c0, n, h=h, ec=ec, dst=dst, dv=dv):
                        evac(dv[:, 2 * h + ec, c0:c0 + n], ps.ap[:, 0:n], [ps], pw=dst)
                        if dst is vT and c0 == TH:
                            evac(vTs.ap[:, 2 * h + ec, :], ps.ap[:, 0:n], [ps], pw=vTs, eng="dve")
                    projB(pp, lambda k, h=h, ec=ec, wsrc=wsrc: wsrc.ap[:, h, k, ec * 128:(ec + 1) * 128], 2,
                          lambda k, c0, n, h=h, sv=sv, soff=soff: sv[:, 2 * h + k, soff + c0:soff + c0 + n], outf, [wsrc], [src])
        for i in range(8):
            for hb2 in range(2):
                ps = pb(psrot[0] % 6, 0, 512); psrot[0] += 1
                for hh in range(2):
                    h = hb2 * 2 + hh
                    for kk in range(2):
                        E("pe", lambda e, i=i, h=h, hh=hh, kk=kk, ps=ps: e.matmul(
                            ps.ap[:, hh * 256:(hh + 1) * 256], lhsT=xbv[:, 2 * h + kk, 3 + i * 128:3 + (i + 1) * 128],
                            rhs=wv.ap[:, h, kk, :], start=(kk == 0), stop=(kk == 1)),
                          reads=[xbT, wv], writes=[ps] if (hh == 0 and kk == 0) else [], pwrites=[] if (hh == 0 and kk == 0) else [ps])
                evac(vav[:, i, hb2 * 2:hb2 * 2 + 2, 0:256], ps.ap.rearrange("p (h e) -> p h e", h=2), [ps], pw=vaug)
        if pp == 0:
            q_sf = fs(0, 1024); k_sf = fs(1024, 1024); n_sf = fs(2048, 1024); tmp_s = fs(3072, 1024)
            DMA("sp", n_sf.ap[0:16, :], n_in, writes=[n_sf])
            for (wsrc, dstf, qi) in ((wq, q_sf, 0), (wk, k_sf, 1)):
                for hb2 in range(2):
                    ps = pb(psrot[0] % 6, 0, 512); psrot[0] += 1
                    for hh in range(2):
                        h = hb2 * 2 + hh
                        for kk in range(2):
                            E("pe", lambda e, h=h, hh=hh, kk=kk, ps=ps, wsrc=wsrc: e.matmul(
                                ps.ap[0:16, hh * 256:(hh + 1) * 256], lhsT=xcv[:, 2 * h + kk, TH:TC],
                                rhs=wsrc.ap[:, h, kk, :], start=(kk == 0), stop=(kk == 1)),
                              reads=[xcT, wsrc], writes=[ps] if (hh == 0 and kk == 0) else [],
                              pwrites=[] if (hh == 0 and kk == 0) else [ps])
                    evac(dstf.ap[0:16, hb2 * 512:(hb2 + 1) * 512], ps.ap[0:16, :], [ps], pw=dstf, eng="act")
                    evac(qks.ap[0:16, qi, hb2 * 512:(hb2 + 1) * 512], ps.ap[0:16, :], [ps], pw=qks, eng="dve")

        gps = pb(6, 0, 72)
        for i in range(nt):
            tp, c0 = tinfo(i)
            for kc in range(24):
                srcb, srcv = ((qT, qv), (kT, kv), (vT, vv))[kc // 8]
                E("pe", lambda e, i=i, kc=kc, tp=tp, c0=c0, srcv=srcv: e.matmul(
                    gps.ap[:tp, i * 8:(i + 1) * 8], lhsT=srcv[:, kc % 8, c0:c0 + tp], rhs=wif.ap[:, kc, :],
                    start=(kc == 0), stop=(kc == 23)), reads=[srcb, wif],
                  writes=[gps] if (i == 0 and kc == 0) else [], pwrites=[] if (i == 0 and kc == 0) else [gps])
        E("dve", lambda e: e.tensor_tensor(out=G.ap[:, :, 0:8], in0=gps.ap[:, 0:64].rearrange("p (i g) -> p g i", g=8),
                                           in1=bifb.ap[:, :, None].broadcast_to([128, 8, 8]), op=ALU.add),
          reads=[gps, bifb], pwrites=[G])
        if pp == 0:
            E("dve", lambda e: e.tensor_tensor(out=G.ap[0:16, :, 8], in0=gps.ap[0:16, 64:72], in1=bifb.ap[0:16, :], op=ALU.add),
              reads=[gps, bifb], pwrites=[G])
        def gate_math():
            f32v = lambda b: b.ap.rearrange("p h j -> p (h j)")
            E("act", lambda e: e.activation(out=E1.ap, in_=G.ap[:, 4:8, 0:8], func=AF.Exp, scale=-1.0), reads=[G], writes=[E1])
            E("act", lambda e: e.activation(out=LFN.ap, in_=E1.ap, func=AF.Ln, bias=onec.ap[:, 0:1]), reads=[E1, onec], writes=[LFN])
            pbn = pb(7, 0, 32); prB = pb(6, 64, 32)
            yield
            E("pe", lambda e: e.matmul(pbn.ap, lhsT=tri.ap, rhs=f32v(LFN), start=True, stop=True), reads=[tri, LFN], writes=[pbn])
            yield
            E("pe", lambda e: e.matmul(prB.ap[0:1, :], lhsT=onesf.ap[:, 0:1], rhs=f32v(LFN), start=True, stop=True),
              reads=[onesf, LFN], writes=[prB])
            E("act", lambda e: e.copy(out=f32v(BNs), in_=pbn.ap), reads=[pbn], writes=[BNs])
            E("act", lambda e: e.copy(out=rowB.ap, in_=prB.ap[0:1, :]), reads=[prB], writes=[rowB])
            E("dve", lambda e: e.tensor_tensor(out=Csb.ap, in0=G.ap[:, 0:4, 0:8], in1=pbn.ap.rearrange("p (h j) -> p h j", h=4), op=ALU.add),
              reads=[G, pbn], writes=[Csb])
            yield
            pct = pb(7, 128, 128)
            E("pe", lambda e: e.transpose(out=pct.ap[0:32, :], in_=f32v(Csb), identity=identf.ap), reads=[Csb, identf], writes=[pct])
            E("dve", lambda e: e.tensor_reduce(out=cmaxc.ap, in_=pct.ap[0:32, :], axis=AX.X, op=ALU.max), reads=[pct], writes=[cmaxc])
            yield
            prA = pb(6, 128, 32)
            E("pe", lambda e: e.transpose(out=prA.ap[0:1, :], in_=cmaxc.ap, identity=identf.ap[0:32, 0:32]),
              reads=[cmaxc, identf], writes=[prA])
            E("dve", lambda e: e.tensor_copy(out=rowA.ap, in_=prA.ap[0:1, :]), reads=[prA], writes=[rowA])
            for h in range(4):
                E("dve", lambda e, h=h: e.tensor_tensor_scan(out=MN.ap[0:1, h * 8:(h + 1) * 8], data0=rowA.ap[0:1, h * 8:(h + 1) * 8],
                                                             data1=rowB.ap[0:1, h * 8:(h + 1) * 8], initial=mcar.ap[0:1, h:h + 1],
                                                             op0=ALU.max, op1=ALU.subtract),
                  reads=[rowA, rowB, mcar], writes=[MN] if h == 0 else [], pwrites=[] if h == 0 else [MN])
            MN3 = MN.ap.rearrange("p (h j) -> p h j", h=4); MP3 = MPV.ap.rearrange("p (h j) -> p h j", h=4)
            E("dve", lambda e: e.tensor_copy(out=MP3[:, :, 0:1], in_=mcar.ap[:, :, None]), reads=[mcar], writes=[MPV])
            E("dve", lambda e: e.tensor_copy(out=MP3[:, :, 1:8], in_=MN3[:, :, 0:7]), reads=[MN], pwrites=[MPV])
            E("dve", lambda e: e.tensor_tensor(out=RG.ap[0:1, 0:32], in0=MN.ap, in1=rowB.ap, op=ALU.add), reads=[MN, rowB], writes=[RG])
            E("dve", lambda e: e.tensor_tensor(out=rtmp.ap, in0=MPV.ap, in1=RG.ap[0:1, 0:32], op=ALU.subtract),
              reads=[MPV, RG], writes=[rtmp])
            E("act", lambda e: e.activation(out=RG.ap[0:1, 32:64], in_=rtmp.ap, func=AF.Exp), reads=[rtmp], pwrites=[RG])
            E("dve", lambda e: e.tensor_copy(out=mcar.ap[:, :, None], in_=MN3[:, :, 7:8]), reads=[MN, MPV], writes=[mcar])
            if pp == 1:
                out_dmas.append(DMA("sp", mp_o, mcar.ap, reads=[mcar]))
            yield
            pbc = pb(7, 256, 64)
            E("pe", lambda e: e.matmul(pbc.ap, lhsT=onesf.ap[0:1, :], rhs=RG.ap[0:1, :], start=True, stop=True),
              reads=[onesf, RG], writes=[pbc])
            E("dve", lambda e: e.tensor_tensor(out=f32v(A1), in0=f32v(Csb), in1=pbc.ap[:, 0:32], op=ALU.subtract),
              reads=[Csb, pbc], writes=[A1])
            E("act", lambda e: e.activation(out=WS.ap, in_=A1.ap, func=AF.Exp, bias=nln16.ap[:, 0:1]), reads=[A1, nln16], writes=[WS])
            E("dve", lambda e: e.tensor_tensor(out=f32v(E1), in0=f32v(BNs), in1=pbc.ap[:, 0:32], op=ALU.subtract),
              reads=[BNs, pbc], writes=[E1])
            E("act", lambda e: e.activation(out=FL.ap, in_=E1.ap, func=AF.Exp), reads=[E1], writes=[FL])
            E("dve", lambda e: e.tensor_copy(out=f32v(GSb), in_=pbc.ap[:, 32:64]), reads=[pbc], writes=[GSb])

            if pp == 0:
                lis = G.ap[0:16, 0:4, 8]; lfs = G.ap[0:16, 4:8, 8]
                sgv = lambda a, b: sg.ap[0:16, a:b]
                E("act", lambda e: e.activation(out=sgv(0, 4), in_=lfs, func=AF.Exp, scale=-1.0), reads=[G], pwrites=[sg])
                E("act", lambda e: e.activation(out=sgv(4, 8), in_=sgv(0, 4), func=AF.Ln, bias=onec.ap[0:16, 0:1]),
                  reads=[sg, onec], pwrites=[sg])
                E("dve", lambda e: e.tensor_tensor(out=sgv(8, 12), in0=lis, in1=sgv(4, 8), op=ALU.add), reads=[G, sg], pwrites=[sg])
                E("dve", lambda e: e.tensor_tensor(out=sgv(12, 16), in0=sgv(8, 12), in1=m_s.ap, op=ALU.max), reads=[sg, m_s], pwrites=[sg])
                E("dve", lambda e: e.tensor_tensor(out=sgv(16, 20), in0=sgv(12, 16), in1=sgv(4, 8), op=ALU.subtract),
                  reads=[sg], pwrites=[sg])
                out_dmas.append(DMA("sp", ms_o, sgv(16, 20), reads=[sg]))
                E("dve", lambda e: e.tensor_tensor(out=sgv(20, 24), in0=sgv(8, 12), in1=sgv(12, 16), op=ALU.subtract),
                  reads=[sg], pwrites=[sg])
                E("act", lambda e: e.activation(out=S12.ap[:, 0:4], in_=sgv(20, 24), func=AF.Exp, bias=nln16.ap[0:16, 0:1]),
                  reads=[sg, nln16], pwrites=[S12])
                E("dve", lambda e: e.tensor_tensor(out=sgv(24, 28), in0=m_s.ap, in1=sgv(12, 16), op=ALU.subtract),
                  reads=[sg, m_s], pwrites=[sg])
                E("act", lambda e: e.activation(out=S12.ap[:, 4:8], in_=sgv(24, 28), func=AF.Exp), reads=[sg], pwrites=[S12])
                E("act", lambda e: e.activation(out=sgv(28, 32), in_=sgv(16, 20), func=AF.Exp, scale=-1.0), reads=[sg], pwrites=[sg])
                E("dve", lambda e: e.tensor_tensor(out=tmp_s.ap[0:16, :], in0=q_sf.ap[0:16, :], in1=k_sf.ap[0:16, :], op=ALU.mult),
                  reads=[q_sf, k_sf], writes=[tmp_s])
                E("dve", lambda e: e.tensor_reduce(out=sgv(32, 36), in_=tmp_s.ap[0:16, :].rearrange("p (h d) -> p h d", h=4),
                                                   axis=AX.X, op=ALU.add), reads=[tmp_s], pwrites=[sg])
                E("dve", lambda e: e.tensor_tensor(out=tmp_s.ap[0:16, :], in0=q_sf.ap[0:16, :], in1=n_sf.ap[0:16, :], op=ALU.mult),
                  reads=[q_sf, n_sf, sg], writes=[tmp_s])
                E("dve", lambda e: e.tensor_reduce(out=sgv(36, 40), in_=tmp_s.ap[0:16, :].rearrange("p (h d) -> p h d", h=4),
                                                   axis=AX.X, op=ALU.add), reads=[tmp_s], pwrites=[sg])
                E("dve", lambda e: e.tensor_tensor(out=S12.ap[:, 8:12], in0=S12.ap[:, 0:4], in1=sgv(32, 36), op=ALU.mult),
                  reads=[S12, sg], pwrites=[S12])
                E("dve", lambda e: e.tensor_tensor(out=sgv(40, 44), in0=S12.ap[:, 4:8], in1=sgv(36, 40), op=ALU.mult),
                  reads=[S12, sg], pwrites=[sg])
                E("dve", lambda e: e.tensor_tensor(out=sgv(44, 48), in0=sgv(40, 44), in1=S12.ap[:, 8:12], op=ALU.add),
                  reads=[S12, sg], pwrites=[sg])
                E("dve", lambda e: e.scalar_tensor_tensor(out=sgv(48, 52), in0=sgv(44, 48), scalar=-1.0, in1=sgv(44, 48),
                                                          op0=ALU.mult, op1=ALU.max), reads=[sg], pwrites=[sg])
                E("dve", lambda e: e.tensor_tensor(out=sgv(52, 56), in0=sgv(48, 52), in1=sgv(28, 32), op=ALU.max), reads=[sg], pwrites=[sg])
                E("dve", lambda e: e.reciprocal(out=rinv_s.ap, in_=sgv(52, 56)), reads=[sg], writes=[rinv_s])
                n3 = lambda b: b.ap[0:16, :].rearrange("p (h d) -> p h d", h=4)
                E("dve", lambda e: e.tensor_tensor(out=n3(n_sf), in0=n3(n_sf), in1=S12.ap[:, 4:8, None].broadcast_to([16, 4, 256]),
                                                   op=ALU.mult), reads=[n_sf, S12, tmp_s], writes=[n_sf])
                E("dve", lambda e: e.tensor_tensor(out=n3(tmp_s), in0=n3(k_sf), in1=S12.ap[:, 0:4, None].broadcast_to([16, 4, 256]),
                                                   op=ALU.mult), reads=[k_sf, S12], writes=[tmp_s])
                E("dve", lambda e: e.tensor_tensor(out=n_sf.ap[0:16, :], in0=n_sf.ap[0:16, :], in1=tmp_s.ap[0:16, :], op=ALU.add),
                  reads=[n_sf, tmp_s], writes=[n_sf])
                out_dmas.append(DMA("sp", ns_o, n_sf.ap[0:16, :], reads=[n_sf]))
                E("dve", lambda e: e.tensor_tensor(out=BD.ap, in0=S12.ap[:, None, :].broadcast_to([16, 16, 12]),
                                                   in1=identf.ap[0:16, 0:16, None].broadcast_to([16, 16, 12]), op=ALU.mult),
                  reads=[S12, identf], writes=[BD])
                pbd = pb(6, 192, 192)
                yield
                E("pe", lambda e: e.matmul(pbd.ap, lhsT=onesf.ap[0:16, :], rhs=BD.ap.rearrange("p b q -> p (b q)"),
                                           start=True, stop=True), reads=[onesf, BD], writes=[pbd])
                E("act", lambda e: e.copy(out=bcs.ap.rearrange("p b q -> p (b q)"), in_=pbd.ap), reads=[pbd], writes=[bcs])


            yield
        gm = gate_math()

        sigo = ar(1, 0, 8192); sgo = sigo.ap.rearrange("p (i f) -> p i f", i=8)
        for bi, blk in enumerate((8, 9)):
            wb = load_wblk(blk)
            for i in range(nt):
                tp, c0 = tinfo(i)
                ps = projA(lambda k, c0=c0, tp=tp: hTv[:, k, c0:c0 + tp], 8, lambda k, wb=wb: wb.ap[:, k, :], tp, [hgrp(c0)], [wb])
                if i < 8:
                    evac(sgo[:tp, i, bi * 512:(bi + 1) * 512], ps.ap[:tp, :], [ps], pw=sigo, eng="act", func=AF.Sigmoid)
                else:
                    evac(sgs.ap[:tp, bi * 512:(bi + 1) * 512], ps.ap[:tp, :], [ps], pw=sgs, eng="act", func=AF.Sigmoid)
                if i % 2 == 1:
                    next(gm, None)
        szb = ar(6, 0, 8 * TC); szv = V3(szb, 8)

        def sxs_szg(c):
            E("dve", lambda e: e.scalar_tensor_tensor(out=xcv[:, c, 0:NC_], in0=xcv[:, c, 0:NC_], scalar=skipT.ap[:, c:c + 1],
                                                      in1=szv[:, c, 0:NC_], op0=ALU.mult, op1=ALU.mult),
              reads=[xcT, skipT, szb], pwrites=[xcT], war=[xcT])
            E("dve", lambda e: e.tensor_scalar(out=szv[:, c, 0:NC_], in0=szv[:, c, 0:NC_], scalar1=gngT.ap[:, c:c + 1],
                                               scalar2=None, op0=ALU.mult), reads=[szb, gngT, xcT], pwrites=[szb])

        for bi, blk in enumerate((10, 11)):
            wb = load_wblk(blk)
            for cc in range(4):
                c = bi * 4 + cc

                def outf(ps, c0, n, c=c):
                    evac(szv[:, c, c0:c0 + n], ps.ap[:, 0:n], [ps], pw=szb, eng="act", func=AF.Silu)
                projB(pp, lambda k, cc=cc, wb=wb: wb.ap[:, k, cc * 128:(cc + 1) * 128], 8,
                      lambda k, c0, n: hTv[:, k, c0:c0 + n], outf, [wb], [], rb_fn=hgrp)
                if c >= 1:
                    sxs_szg(c - 1)
        sxs_szg(7)
        for _ in gm:
            pass

        ybT = ar(0, 0, 8 * TC); ybv = V3(ybT, 8)
        SKs = [pb(0, 0, 512), pb(1, 0, 512)]
        KTs = [pb(6, 0, 512)]
        NHs = [pb(2, 0, 512), pb(3, 0, 512)]
        HTs = [pb(7, 0, 512)]
        Us = [pb(4, 0, 512), pb(5, 0, 512)]
        o = 0
        PTbs = [fs(o + i * 64, 64, BF16) for i in range(2)]; o += 128
        KWbs = [fs(o + i * 128, 128, BF16) for i in range(2)]; o += 256
        ABbs = [fs(o + i * 257, 257, BF16) for i in range(3)]; o += 771
        HBfs = [fs(o + i * 256, 256) for i in range(3)]; o += 768
        HBNs = [fs(o + i * 128, 128, BF16) for i in range(2)]; o += 256
        Yts = [fs(o + i * 256, 256) for i in range(2)]; o += 512
        rvs = [fs(o + i * 16, 16) for i in range(3)]; o += 48

        def post1a(tp, num_b, den_ap, fl_ap, fl_b, rv):
            E("act", lambda e: e.activation(out=rv.ap[:tp, 0:1], in_=den_ap, func=AF.Abs), reads=[num_b], writes=[rv])
            E("dve", lambda e: e.tensor_tensor(out=rv.ap[:tp, 2:3], in0=rv.ap[:tp, 0:1], in1=fl_ap, op=ALU.max),
              reads=[rv, fl_b], pwrites=[rv])
            E("pool", lambda e: e.tensor_tensor(out=rv.ap[:tp, 1:2], in0=rv.ap[:tp, 2:3], in1=monec.ap[:tp, 0:1], op=ALU.pow),
              reads=[rv, monec], pwrites=[rv])

        def post(it, tp, num_ap, num_b, den_ap, fl_ap, fl_b, sig_ap, h, c0, rinv_ap=None, rinv_b=None, sig_b=None, part=0):
            sig_b = sig_b or sigo
            rv = rvs[it % 3]; HBf = HBfs[it % 3]; HBN = HBNs[it % 2]; Yt = Yts[it % 2]
            hbtb = HTs[0]; hbt_ap = hbtb.ap[:, 0:128].bitcast(BF16)
            if part == 4:
                post1a(tp, num_b, den_ap, fl_ap, fl_b, rv)
            if part == 1:
                rinv_ap = rv.ap[:tp, 1:2]; rinv_b = rv
            if part in (0, 1):
                post1(tp, num_ap, num_b, den_ap, fl_ap, fl_b, sig_ap, rinv_ap, rinv_b, sig_b, rv, HBf, part)
            if part in (0, 2):
                post2(tp, rv, HBf, HBN, hbtb, hbt_ap)
            if part in (0, 3):
                post3(tp, h, c0, hbtb, hbt_ap, Yt)

        def post1(tp, num_ap, num_b, den_ap, fl_ap, fl_b, sig_ap, rinv_ap, rinv_b, sig_b, rv, HBf, part):
            E("dve", lambda e: e.scalar_tensor_tensor(out=HBf.ap[:tp], in0=num_ap, scalar=rinv_ap, in1=sig_ap,
                                                      op0=ALU.mult, op1=ALU.mult), reads=[num_b, rinv_b, sig_b], writes=[HBf])
            if part == 0:
                E("dve", lambda e: e.bn_stats(out=rv.ap[:tp, 4:10], in_=HBf.ap[:tp]), reads=[HBf], writes=[rv])
            else:
                E("dve", lambda e: e.bn_stats(out=rv.ap[:tp, 4:10], in_=HBf.ap[:tp]), reads=[HBf], pwrites=[rv])
            E("dve", lambda e: e.bn_aggr(out=rv.ap[:tp, 10:12], in_=rv.ap[:tp, 4:10]), reads=[rv], pwrites=[rv])
            E("pool", lambda e: e.tensor_tensor(out=rv.ap[:tp, 12:13], in0=rv.ap[:tp, 11:12], in1=epsc.ap[:tp, 0:1], op=ALU.add),
              reads=[rv, epsc], pwrites=[rv])
            E("pool", lambda e: e.tensor_tensor(out=rv.ap[:tp, 13:14], in0=rv.ap[:tp, 12:13], in1=mhalf.ap[:tp, 0:1], op=ALU.pow),
              reads=[rv, mhalf], pwrites=[rv])

        def post2(tp, rv, HBf, HBN, hbtb, hbt_ap):
            E("dve", lambda e: e.tensor_scalar(out=HBN.ap[:tp], in0=HBf.ap[:tp], scalar1=rv.ap[:tp, 10:11], scalar2=rv.ap[:tp, 13:14],
                                               op0=ALU.subtract, op1=ALU.mult), reads=[HBf, rv], writes=[HBN])
            for ec in range(2):
                E("pe", lambda e, ec=ec: e.transpose(out=hbt_ap[:, ec * 128:ec * 128 + tp], in_=HBN.ap[:tp, ec * 128:(ec + 1) * 128],
                                                     identity=ident.ap[:tp, :tp]), reads=[HBN, ident],
                  writes=[hbtb] if ec == 0 else [], pwrites=[] if ec == 0 else [hbtb])

        def post3(tp, h, c0, hbtb, hbt_ap, Yt):
            E("dve", lambda e: e.tensor_tensor(out=Yt.ap.rearrange("p (a t) -> p a t", a=2)[:, :, 0:tp],
                                               in0=hbt_ap.rearrange("p (a t) -> p a t", a=2)[:, :, 0:tp],
                                               in1=szv[:, 2 * h:2 * h + 2, c0:c0 + tp], op=ALU.mult), reads=[hbtb, szb], writes=[Yt])
            E("pool", lambda e: e.tensor_tensor(out=ybv[:, 2 * h:2 * h + 2, c0:c0 + tp],
                                                in0=Yt.ap.rearrange("p (a t) -> p a t", a=2)[:, :, 0:tp],
                                                in1=xcv[:, 2 * h:2 * h + 2, c0:c0 + tp], op=ALU.add), reads=[Yt, xcT], pwrites=[ybT])

        def stageA(it, j, h, part):
            jc = j * 128
            SK = SKs[it % 2]; PTb = PTbs[it % 2]; KWb = KWbs[it % 2]; ABb = ABbs[it % 3]
            KT = KTs[0]
            st_ap = SK.ap[:, 0:128]; ktr_ap = KT.ap[:, 0:128].bitcast(BF16); NU_ap = SK.ap[:, 256:258]
            NH = NHs[it % 2]; num_ap = NH.ap[:, 0:257]; U = Us[it % 2]
            wcol = WS.ap[:, h, j:j + 1]; gcol = GSb.ap[:, h, j:j + 1]
            if part == 2:
                return stageA2(it, j, h, jc, SK, PTb, KWb, ABb, KT, st_ap, ktr_ap, NU_ap, NH, num_ap, U, wcol, gcol)
            for ec in range(2):
                E("pe", lambda e, ec=ec: e.matmul(st_ap, lhsT=kv[:, 2 * h + ec, jc:jc + 128], rhs=qv[:, 2 * h + ec, jc:jc + 128],
                                                  start=(ec == 0), stop=(ec == 1)), reads=[kT, qT],
                  writes=[SK] if ec == 0 else [], pwrites=[] if ec == 0 else [SK])
            for dc in range(2):
                E("pe", lambda e, dc=dc: e.transpose(out=ktr_ap[:, dc * 128:(dc + 1) * 128], in_=kv[:, 2 * h + dc, jc:jc + 128],
                                                     identity=ident.ap), reads=[kT, ident],
                  writes=[KT] if dc == 0 else [], pwrites=[] if dc == 0 else [KT])

        def stageA2(it, j, h, jc, SK, PTb, KWb, ABb, KT, st_ap, ktr_ap, NU_ap, NH, num_ap, U, wcol, gcol):
            E("dve", lambda e: e.scalar_tensor_tensor(out=PTb.ap, in0=st_ap, scalar=wcol, in1=maskT.ap, op0=ALU.mult, op1=ALU.mult),
              reads=[SK, WS, maskT], writes=[PTb])
            E("act", lambda e: e.activation(out=KWb.ap, in_=ktr_ap, func=AF.Copy, scale=wcol), reads=[KT, WS], writes=[KWb])
            ab3 = ABb.ap.rearrange("p (a e) -> p a e", a=2)
            E("pe", lambda e: e.matmul(num_ap, lhsT=PTb.ap, rhs=vav[:, j, h, :], start=True, stop=False),
              reads=[PTb, vaug], writes=[NH])
            for dc in range(2):
                E("pe", lambda e, dc=dc: e.matmul(num_ap, lhsT=qv[:, 2 * h + dc, jc:jc + 128], rhs=ab3[:, dc, :],
                                                  start=False, stop=(dc == 1)), reads=[qT, ABb], pwrites=[NH])
            for dc in range(2):
                E("pe", lambda e, dc=dc: e.matmul(U.ap[:, dc * 256:(dc + 1) * 256], lhsT=KWb.ap[:, dc * 128:(dc + 1) * 128],
                                                  rhs=vav[:, j, h, 0:256], start=True, stop=True), reads=[KWb, vaug],
                  writes=[U] if dc == 0 else [], pwrites=[] if dc == 0 else [U])
            for dc in range(2):
                E("pe", lambda e, dc=dc: e.matmul(NU_ap[:, dc:dc + 1], lhsT=KWb.ap[:, dc * 128:(dc + 1) * 128],
                                                  rhs=onesb.ap[:, 0:1], start=True, stop=True), reads=[KWb, onesb], pwrites=[SK])

        def stageAb(it, j, h):
            ABb = ABbs[it % 3]
            gcol = GSb.ap[:, h, j:j + 1]
            ab3 = ABb.ap.rearrange("p (a e) -> p a e", a=2)
            E("act", lambda e: e.activation(out=ab3[:, :, 0:256], in_=CT.ap[:, h, :, :], func=AF.Copy, scale=gcol),
              reads=[CT, GSb], writes=[ABb])
            E("act", lambda e: e.activation(out=ab3[:, :, 256], in_=nT.ap[:, h, :], func=AF.Copy, scale=gcol),
              reads=[nT, GSb], pwrites=[ABb])

        def stageU(it, j, h):
            SK = SKs[it % 2]; ABb = ABbs[it % 3]; U = Us[it % 2]
            NU_ap = SK.ap[:, 256:258]
            gcol = GSb.ap[:, h, j:j + 1]
            E("dve", lambda e: e.scalar_tensor_tensor(out=CT.ap[:, h, :, :].rearrange("p a e -> p (a e)"),
                                                      in0=CT.ap[:, h, :, :].rearrange("p a e -> p (a e)"), scalar=gcol, in1=U.ap,
                                                      op0=ALU.mult, op1=ALU.add), reads=[CT, GSb, U, ABb], pwrites=[CT])
            E("dve", lambda e: e.scalar_tensor_tensor(out=nT.ap[:, h, :], in0=nT.ap[:, h, :], scalar=gcol, in1=NU_ap,
                                                      op0=ALU.mult, op1=ALU.add), reads=[nT, GSb, SK, ABb], pwrites=[nT])

        def stageB(it, j, h, part):
            num = NHs[it % 2]
            post(it, 128, num.ap[:, 0:256], num, num.ap[:, 256:257], FL.ap[:, h, j:j + 1], FL,
                 sgo[:, j, h * 256:(h + 1) * 256], h, j * 128, part=part)

        wvb = [load_wblk(2), load_wblk(3)]
        its = [(j, h) for j in range(8) for h in range(4)]
        nit = len(its)
        stageAb(0, *its[0])
        for t in range(nit + 3):
            if t < nit:
                stageA(t, *its[t], part=1)
            if 0 <= t - 3 < nit:
                stageB(t - 3, *its[t - 3], part=3)
            if 0 <= t - 2 < nit:
                stageB(t - 2, *its[t - 2], part=2)
            if t < nit:
                stageA(t, *its[t], part=2)
            if 0 <= t - 1 < nit:
                stageB(t - 1, *its[t - 1], part=1)
            if t < nit:
                stageU(t, *its[t])
                stageB(t, *its[t], part=4)
            if t + 1 < nit:
                stageAb(t + 1, *its[t + 1])

        hT2 = ar(1, 0, 8 * TC)
        h2v = V3(hT2, 8)
        DMA("sp", h2v[:, :, 0:NC_], hscr.rearrange("p (c t) -> p c t", c=8)[:, :, 0:NC_], reads=[hscrb], writes=[hT2])
        if pp == 1:
            Cst = ar(3, 0, 4096, F32)
            csv = Cst.ap.rearrange("p (h a d) -> p h a d", h=4, a=2)
            for h in range(4):
                for eh in range(2):
                    pt_ = pb((h * 2 + eh) % 2 + 2, 0, 256)
                    for dc in range(2):
                        E("pe", lambda e, h=h, eh=eh, dc=dc, pt_=pt_: e.transpose(
                            out=pt_.ap[:, dc * 128:(dc + 1) * 128], in_=CT.ap[:, h, dc, eh * 128:(eh + 1) * 128],
                            identity=identf.ap), reads=[CT, identf], writes=[pt_] if dc == 0 else [], pwrites=[] if dc == 0 else [pt_])
                    evac(csv[:, h, eh, :], pt_.ap, [pt_], pw=Cst)
            out_dmas.append(DMA("sp", Cp.rearrange("h (a p) d -> p h a d", p=128), csv, reads=[Cst]))
            out_dmas.append(DMA("sp", np_o.rearrange("h (a p) -> p h a", p=128), nT.ap, reads=[nT], allow_slow_non_contiguous=True))

        if pp == 0:
            NCB = 12
            Cins = [ar(3 + i // 8, (i % 8) * 1024, 1024, F32) for i in range(NCB)]
            qkb = fs(2752, 1024, BF16)
            junkc = fs(3776, 128, BF16)
            E("dve", lambda e: e.tensor_tensor(
                out=wvT.ap, in0=vTs.ap.rearrange("p (h a) b -> p h a b", h=4),
                in1=bcs.ap[:, :, 0:4].rearrange("p b h -> p h b")[:, :, None, :].broadcast_to([128, 4, 2, 16]),
                op=ALU.mult), reads=[vTs, bcs], writes=[wvT])
            units = [(b, h) for b in range(NS) for h in range(4)]
            pqs = {}

            def c_in(u):
                b, h = units[u]
                Cin = Cins[u % NCB]
                c3 = Cin.ap.rearrange("p (a d) -> p a d", a=2)
                DMA("sp", c3, C_in[b, h].rearrange("(a p) d -> p a d", p=128), writes=[Cin], key=rkey("ci", u, NCB))

            def c_nop(u):
                pass

            def c_s0(u):
                b, h = units[u]
                if h == 0:
                    E("pool", lambda e: e.tensor_tensor(out=qkb.ap[0:16, :], in0=qks.ap.rearrange("p a f -> p (a f)"),
                                                        in1=identf.ap[0:16, b:b + 1].broadcast_to([16, 2048]), op=ALU.mult),
                      reads=[qks, identf], writes=[qkb])
                pq = pb(u % 8, 0, 512)
                pqs[u] = pq
                E("pe", lambda e: e.matmul(
                    pq.ap.rearrange("p (a d) -> p a d", a=2), lhsT=onesb.ap[0:16, :],
                    rhs=qkb.ap[0:16, :].rearrange("p (a f) -> p a f", a=2)[:, :, h * 256:(h + 1) * 256], start=True, stop=True),
                  reads=[qkb, onesb], writes=[pq])

            def c_s1(u):
                b, h = units[u]
                Cin = Cins[u % NCB]; pq = pqs[u]
                c3 = Cin.ap.rearrange("p (a d) -> p a d", a=2)
                for a in range(2):
                    E("dve", lambda e, a=a: e.scalar_tensor_tensor(
                        out=junkc.ap, in0=c3[:, a, :], scalar=1.0, in1=pq.ap[:, 0:256], op0=ALU.mult, op1=ALU.mult,
                        accum_out=CqT.ap[:, h, a, b:b + 1]), reads=[Cin, pq], writes=[junkc], pwrites=[CqT])

            def c_s2(u):
                b, h = units[u]
                Cin = Cins[u % NCB]
                E("act", lambda e: e.activation(out=Cin.ap, in_=Cin.ap, func=AF.Copy, scale=bcs.ap[:, b, 4 + h:5 + h]),
                  reads=[Cin, bcs], writes=[Cin])

            def c_s3(u):
                b, h = units[u]
                Cin = Cins[u % NCB]; pq = pqs[u]
                c3 = Cin.ap.rearrange("p (a d) -> p a d", a=2)
                for a in range(2):
                    E("dve", lambda e, a=a: e.scalar_tensor_tensor(
                        out=c3[:, a, :], in0=pq.ap[:, 256:512], scalar=wvT.ap[:, h, a, b:b + 1], in1=c3[:, a, :],
                        op0=ALU.mult, op1=ALU.add), reads=[Cin, pq, wvT], writes=[Cin])
                out_dmas.append(DMA("sp", Cs[b, h].rearrange("(a p) d -> p a d", p=128), c3, reads=[Cin], key=rkey("co", u, NCB)))

            pipeline(len(units), [c_in, c_nop, c_nop, c_nop, c_nop, c_nop, c_s0, c_s1, c_s2, c_s3])
            bq = lambda q: bcs.ap[:, :, q * 4:(q + 1) * 4].rearrange("p b h -> p h b")[:, :, None, :].broadcast_to([128, 4, 2, 16])
            E("dve", lambda e: e.tensor_tensor(out=numTs.ap, in0=vTs.ap.rearrange("p (h a) b -> p h a b", h=4), in1=bq(2), op=ALU.mult),
              reads=[vTs, bcs], writes=[numTs])
            E("dve", lambda e: e.tensor_tensor(out=CqT.ap, in0=CqT.ap, in1=bq(1), op=ALU.mult), reads=[CqT, bcs], writes=[CqT])
            E("dve", lambda e: e.tensor_tensor(out=numTs.ap, in0=numTs.ap, in1=CqT.ap, op=ALU.add), reads=[numTs, CqT], writes=[numTs])
            for hb2 in range(2):
                pn = pb(4 + hb2, 0, 512)
                for hh in range(2):
                    h = hb2 * 2 + hh
                    for a in range(2):
                        E("pe", lambda e, h=h, hh=hh, a=a, pn=pn: e.transpose(
                            out=pn.ap[0:16, hh * 256 + a * 128:hh * 256 + (a + 1) * 128], in_=numTs.ap[:, h, a, :], identity=identf.ap),
                          reads=[numTs, identf], writes=[pn] if (hh == 0 and a == 0) else [], pwrites=[] if (hh == 0 and a == 0) else [pn])
                for hh in range(2):
                    h = hb2 * 2 + hh
                    post(h, 16, pn.ap[0:16, hh * 256:(hh + 1) * 256], pn, None, None, None,
                         sgs.ap[0:16, h * 256:(h + 1) * 256], h, TH, rinv_ap=rinv_s.ap[:, h:h + 1], rinv_b=rinv_s, sig_b=sgs)

        lg = cs_(1); lb = cs_(2); bspb = cs_(0)
        DMA("sp", lg.ap, lnv_g.partition_broadcast(128), writes=[lg])
        DMA("sp", lb.ap, lnv_b.partition_broadcast(128), writes=[lb])
        VG = ar(4, 0, 16384, F32); vgv = VG.ap.rearrange("p (i f) -> p i f", i=8)
        vn = ar(2, 0, 8192); vnv = vn.ap.rearrange("p (i f) -> p i f", i=8)
        vsb = fs(2048, 1024)
        yaT = ar(3, 0, 8 * TC); yav = V3(yaT, 8)
        SQs = [fs(3072, 512), fs(3584, 512)]
        for bi, blk in enumerate((2, 3)):
            wb = wvb[bi]
            for i in range(nt):
                tp, c0 = tinfo(i)
                ps = projA(lambda k, c0=c0, tp=tp: h2v[:, k, c0:c0 + tp], 8, lambda k, wb=wb: wb.ap[:, k, :], tp, [hT2], [wb])
                vg_ap = vgv[:tp, i, bi * 512:(bi + 1) * 512] if i < 8 else vsb.ap[:tp, bi * 512:(bi + 1) * 512]
                VGb = VG if i < 8 else vsb
                evac(vg_ap, ps.ap[:tp, :], [ps], pw=VGb, eng="act", func=AF.Gelu)
                SQ = SQs[i % 2]
                E("dve", lambda e, vg_ap=vg_ap, SQ=SQ, tp=tp: e.tensor_tensor(out=SQ.ap[:tp], in0=vg_ap, in1=vg_ap, op=ALU.mult),
                  reads=[VGb], writes=[SQ])
                E("dve", lambda e, vg_ap=vg_ap, tp=tp, i=i, bi=bi: e.tensor_reduce(
                    out=S1.ap[:tp, i, bi * 4:(bi + 1) * 4], in_=vg_ap.rearrange("p (h d) -> p h d", h=4), axis=AX.X, op=ALU.add),
                  reads=[VGb], pwrites=[S1])
                E("dve", lambda e, SQ=SQ, tp=tp, i=i, bi=bi: e.tensor_reduce(
                    out=S2.ap[:tp, i, bi * 4:(bi + 1) * 4], in_=SQ.ap[:tp].rearrange("p (h d) -> p h d", h=4), axis=AX.X, op=ALU.add),
                  reads=[SQ], pwrites=[S2])
        for bi, blk in enumerate((0, 1)):
            wb = load_wblk(blk)
            for cc in range(4):
                c = bi * 4 + cc

                def outf(ps, c0, n, c=c):
                    evac(yav[:, c, c0:c0 + n], ps.ap[:, 0:n], [ps], pw=yaT, eng="act", func=AF.Gelu)
                projB(pp, lambda k, cc=cc, wb=wb: wb.ap[:, k, cc * 128:(cc + 1) * 128], 8,
                      lambda k, c0, n: h2v[:, k, c0:c0 + n], outf, [wb], [hT2])
        nst = nt * 8
        fl2 = lambda b: b.ap.rearrange("p i g -> p (i g)")[:, 0:nst]
        cb128 = c128.ap[:, 0:1].broadcast_to([128, nst])
        E("pool", lambda e: e.tensor_tensor(out=fl2(MEAN), in0=fl2(S1), in1=cb128, op=ALU.mult), reads=[S1, c128], writes=[MEAN])
        E("pool", lambda e: e.tensor_tensor(out=fl2(S1), in0=fl2(MEAN), in1=fl2(MEAN), op=ALU.mult), reads=[MEAN], writes=[S1])
        E("pool", lambda e: e.tensor_tensor(out=fl2(S2), in0=fl2(S2), in1=cb128, op=ALU.mult), reads=[S2, c128], writes=[S2])
        E("pool", lambda e: e.tensor_tensor(out=fl2(S2), in0=fl2(S2), in1=fl2(S1), op=ALU.subtract), reads=[S2, S1], writes=[S2])
        E("pool", lambda e: e.tensor_tensor(out=fl2(S2), in0=fl2(S2), in1=epsc.ap[:, 0:1].broadcast_to([128, nst]), op=ALU.add),
          reads=[S2, epsc], writes=[S2])
        E("pool", lambda e: e.tensor_tensor(out=fl2(RSTD), in0=fl2(S2), in1=mhalf.ap[:, 0:1].broadcast_to([128, nst]), op=ALU.pow),
          reads=[S2, mhalf], writes=[RSTD])
        wza = [load_wblk(4), load_wblk(5)]
        NTs = [fs(0, 1024), fs(1024, 1024)]
        vs_f = vsb
        for i in range(nt):
            tp, c0 = tinfo(i)
            NT_ = NTs[i % 2]
            n3_ = NT_.ap[:tp].rearrange("p (h d) -> p h d", h=8)
            vsrc = vgv[:tp, i, :] if i < 8 else vsb.ap[:tp, :]
            E("dve", lambda e, i=i, tp=tp, n3_=n3_, vsrc=vsrc: e.tensor_tensor(
                out=n3_, in0=vsrc.rearrange("p (h d) -> p h d", h=8),
                in1=MEAN.ap[:tp, i, :, None].broadcast_to([tp, 8, 128]), op=ALU.subtract), reads=[VG if i < 8 else vsb, MEAN], writes=[NT_])
            if i < 8:
                E("dve", lambda e, i=i, tp=tp, n3_=n3_: e.tensor_tensor(
                    out=vnv[:tp, i, :].rearrange("p (h d) -> p h d", h=8), in0=n3_,
                    in1=RSTD.ap[:tp, i, :, None].broadcast_to([tp, 8, 128]), op=ALU.mult), reads=[NT_, RSTD], pwrites=[vn])
            else:
                E("dve", lambda e, i=i, tp=tp, n3_=n3_: e.tensor_tensor(
                    out=n3_, in0=n3_, in1=RSTD.ap[:tp, i, :, None].broadcast_to([tp, 8, 128]), op=ALU.mult), reads=[NT_, RSTD], writes=[NT_])
                E("pool", lambda e, tp=tp, NT_=NT_: e.tensor_tensor(out=NT_.ap[:tp], in0=NT_.ap[:tp], in1=lg.ap[:tp], op=ALU.mult),
                  reads=[NT_, lg], writes=[NT_])
                E("pool", lambda e, tp=tp, NT_=NT_: e.tensor_tensor(out=vs_f.ap[:tp], in0=NT_.ap[:tp], in1=lb.ap[:tp], op=ALU.add),
                  reads=[NT_, lb], writes=[vs_f])
                E("pool", lambda e, i=i, tp=tp: e.tensor_copy(out=vns.ap[:tp, :], in_=vs_f.ap[:tp]), reads=[vs_f], writes=[vns])
                out_dmas.append(DMA("sp", vs_o, vs_f.ap[0:16, :], reads=[vs_f]))
        wo = ar(4, 0, 16 * 1024); wov = wo.ap.rearrange("p (k n) -> p k n", k=16)
        DMA("pool", wov, w_out.rearrange("(k p) n -> p k n", p=128), writes=[wo])
        wg = ar(6, 0, 8 * 1024); wgv = wg.ap.rearrange("p (k n) -> p k n", k=8)
        DMA("pool", wgv, w_pg.rearrange("(k p) n -> p k n", p=128), writes=[wg])
        DMA("sp", bspb.ap, b_sp.rearrange("h t -> (h t)").partition_broadcast(128), writes=[bspb])
        SZs = [fs(3072, 512), fs(3584, 512)]
        T1s = [fs(0, 512), fs(512, 512)]
        for hf in range(2):
            prs = pb(4 + hf, 0, 512)
            E("pe", lambda e, hf=hf, prs=prs: e.matmul(prs.ap, lhsT=onesb.ap, rhs=WT.ap.rearrange("p h t -> p (h t)")[:, hf * 512:(hf + 1) * 512],
                                                   start=True, stop=True), reads=[onesb, WT], writes=[prs])
            for cc in range(4):
                c = hf * 4 + cc
                E("dve", lambda e, c=c, cc=cc, prs=prs: e.scalar_tensor_tensor(
                    out=bspb.ap[:, c * 128:(c + 1) * 128], in0=prs.ap[:, cc * 128:(cc + 1) * 128], scalar=lbT.ap[:, c:c + 1],
                    in1=bspb.ap[:, c * 128:(c + 1) * 128], op0=ALU.mult, op1=ALU.add), reads=[prs, lbT, bspb], writes=[bspb])
        gi = 0
        for bi, blk in enumerate((4, 5)):
            wb = wza[bi]
            for cc in range(4):
                c = bi * 4 + cc
                for (c0, n) in colgroups(pp):
                    SZ = SZs[gi % 2]; T1 = T1s[gi % 2]; gi += 1
                    zps = pb(psrot[0] % 4, 0, 512); psrot[0] += 1
                    for k in range(8):
                        E("pe", lambda e, k=k, c0=c0, n=n, zps=zps, cc=cc, wb=wb: e.matmul(
                            zps.ap[:, 0:n], lhsT=wb.ap[:, k, cc * 128:(cc + 1) * 128], rhs=h2v[:, k, c0:c0 + n],
                            start=(k == 0), stop=(k == 7)), reads=[wb, hT2], writes=[zps] if k == 0 else [], pwrites=[] if k == 0 else [zps])
                    evac(SZ.ap[:, 0:n], zps.ap[:, 0:n], [zps], w=SZ, eng="act", func=AF.Silu)
                    sps = pb(4 + gi % 2, 0, 512)
                    if n == 512:
                        for jj in range(4):
                            j = c0 // 128 + jj
                            E("pe", lambda e, jj=jj, j=j, c=c, sps=sps: e.matmul(
                                sps.ap[:, jj * 128:(jj + 1) * 128], lhsT=vnv[:, j, c * 128:(c + 1) * 128], rhs=WT.ap[:, c, :],
                                start=True, stop=True), reads=[vn, WT], writes=[sps] if jj == 0 else [], pwrites=[] if jj == 0 else [sps])
                        E("dve", lambda e, sps=sps, T1=T1, c=c: e.scalar_tensor_tensor(
                            out=T1.ap.rearrange("p (j t) -> p j t", j=4), in0=sps.ap.rearrange("p (j t) -> p j t", j=4),
                            scalar=lgT.ap[:, c:c + 1],
                            in1=bspb.ap[:, None, c * 128:(c + 1) * 128].broadcast_to([128, 4, 128]), op0=ALU.mult, op1=ALU.add),
                          reads=[sps, bspb, lgT], writes=[T1])
                    else:
                        E("pe", lambda e, c=c, sps=sps: e.matmul(sps.ap[:, 0:16], lhsT=vns.ap[0:16, c * 128:(c + 1) * 128],
                                                                 rhs=Wdiag.ap[0:16, c, :], start=True, stop=True),
                          reads=[vns, Wdiag], writes=[sps])
                        E("dve", lambda e, sps=sps, T1=T1, c=c: e.tensor_scalar(out=T1.ap[:, 0:16], in0=sps.ap[:, 0:16],
                                                                                scalar1=bsp0.ap[:, c:c + 1], scalar2=None, op0=ALU.add),
                          reads=[sps, bsp0], writes=[T1])
                    E("dve", lambda e, c=c, c0=c0, n=n, SZ=SZ: e.tensor_tensor(out=yav[:, c, c0:c0 + n], in0=yav[:, c, c0:c0 + n],
                                                                               in1=SZ.ap[:, 0:n], op=ALU.mult),
                      reads=[yaT, SZ], pwrites=[yaT])
                    E("dve", lambda e, c=c, c0=c0, n=n, T1=T1: e.tensor_tensor(out=yav[:, c, c0:c0 + n], in0=yav[:, c, c0:c0 + n],
                                                                               in1=T1.ap[:, 0:n], op=ALU.mult),
                      reads=[yaT, T1], pwrites=[yaT])

        wp = ar(2, 0, 2 * 1024); wpv = wp.ap.rearrange("p (k n) -> p k n", k=2)
        DMA("pool", wpv, w_pp.rearrange("(k p) n -> p k n", p=128), writes=[wp])
        bpg = cs_(0); plg = cs_(1); fng = cs_(2)
        DMA("sp", bpg.ap, b_pg.partition_broadcast(128), writes=[bpg])
        DMA("sp", plg.ap, ple_g.partition_broadcast(128), writes=[plg])
        DMA("sp", fng.ap, fin_g.partition_broadcast(128), writes=[fng])
        xts = [fs(0, 1024), fs(1024, 1024)]
        X1s = [fs(2048, 1024), fs(3072, 1024)]
        pts = [ar(1, i * 512, 512, F32) for i in range(2)]
        X1b = ar(1, 1024, 1024); X1T = ar(1, 2048, 1024); ptb = ar(1, 3072, 256); PT2 = ar(1, 3328, 256)
        Gs = ar(1, 3584, 2048, F32); Ef = ar(1, 5632, 2048, F32)
        Yf = ar(2, 2048, 2048, F32); X2 = ar(2, 4096, 2048, F32); Tf = ar(2, 6144, 2048, F32)
        ptmp5 = [ptmps[0], ptmps[1]]
        PS = {}

        def f_L(i):
            tp, c0 = tinfo(i)
            xt = xts[i % 2]; pt = pts[i % 2]
            DMA("sp", xt.ap[:tp], xp[t0 + i * 128:t0 + (i + 1) * 128, :] if i < 8 else xs, writes=[xt], key=rkey("x5", i, 2))
            DMA("sp", pt.ap[:tp], pp_[t0 + i * 128:t0 + (i + 1) * 128, :] if i < 8 else psm, writes=[pt], key=rkey("p5", i, 2))

        def f_O(i, nbs=(0, 1)):
            tp, c0 = tinfo(i)
            if ("o", i) not in PS:
                PS[("o", i)] = [pb(0, 0, 512), pb(1, 0, 512)]
            pso = PS[("o", i)]
            for nb in nbs:
                for kc in range(16):
                    ysrc, yb_ = (yav, yaT) if kc < 8 else (ybv, ybT)
                    E("pe", lambda e, nb=nb, kc=kc, ysrc=ysrc: e.matmul(
                        pso[nb].ap[:tp, :], lhsT=ysrc[:, kc % 8, c0:c0 + tp], rhs=wov[:, kc, nb * 512:(nb + 1) * 512],
                        start=(kc == 0), stop=(kc == 15)), reads=[yb_, wo], writes=[pso[nb]] if kc == 0 else [],
                      pwrites=[] if kc == 0 else [pso[nb]])

        def f_X(i, nb):
            tp, c0 = tinfo(i)
            xt = xts[i % 2]; X1 = X1s[i % 2]; pso = PS[("o", i)]
            E("dve", lambda e: e.tensor_tensor(
                out=X1b.ap[:tp, nb * 512:(nb + 1) * 512], in0=pso[nb].ap[:tp, :], in1=xt.ap[:tp, nb * 512:(nb + 1) * 512], op=ALU.add),
              reads=[pso[nb], xt], writes=[X1b] if nb == 0 else [], pwrites=[] if nb == 0 else [X1b])
            E("dve", lambda e: e.tensor_tensor(
                out=X1.ap[:tp, nb * 512:(nb + 1) * 512], in0=pso[nb].ap[:tp, :], in1=xt.ap[:tp, nb * 512:(nb + 1) * 512], op=ALU.add),
              reads=[pso[nb], xt], writes=[X1] if nb == 0 else [], pwrites=[] if nb == 0 else [X1])

        def f_P(i):
            tp, c0 = tinfo(i)
            pt = pts[i % 2]
            E("dve", lambda e: e.tensor_copy(out=ptb.ap[:tp], in_=pt.ap[:tp]), reads=[pt], writes=[ptb])
            ppt = pb(7, 0, 128, BF16)
            for k in range(2):
                E("pe", lambda e, k=k: e.transpose(out=ppt.ap[:, k * 128:k * 128 + tp], in_=ptb.ap[:tp, k * 128:(k + 1) * 128],
                                                   identity=ident.ap[:tp, :tp]), reads=[ptb, ident],
                  writes=[ppt] if k == 0 else [], pwrites=[] if k == 0 else [ppt])
            evac(PT2.ap, ppt.ap, [ppt], w=PT2, eng="dve")
            pse = [pb(4, 0, 512), pb(5, 0, 512)]
            PS[("e", i)] = pse
            for nb in range(2):
                for k in range(2):
                    E("pe", lambda e, nb=nb, k=k: e.matmul(
                        pse[nb].ap[:tp, :], lhsT=PT2.ap[:, k * 128:k * 128 + tp], rhs=wpv[:, k, nb * 512:(nb + 1) * 512],
                        start=(k == 0), stop=(k == 1)), reads=[PT2, wp], writes=[pse[nb]] if k == 0 else [], pwrites=[] if k == 0 else [pse[nb]])

            for nb in range(2):
                E("act", lambda e, nb=nb: e.activation(
                    out=X2.ap[:tp, nb * 512:(nb + 1) * 512], in_=pse[nb].ap[:tp, :], func=AF.Square,
                    accum_out=stat2.ap[:tp, 2 * i + nb:2 * i + nb + 1]), reads=[pse[nb]],
                  pwrites=[stat2] if nb == 0 else [stat2, X2], writes=[X2] if nb == 0 else [])
            E("pool", lambda e: e.tensor_tensor(out=stat2.ap[:tp, 40 + i:41 + i], in0=stat2.ap[:tp, 2 * i:2 * i + 1],
                                                in1=stat2.ap[:tp, 2 * i + 1:2 * i + 2], op=ALU.add), reads=[stat2], pwrites=[stat2])
            rstd_pool(stat2.ap[:tp, 50 + i:51 + i], stat2.ap[:tp, 40 + i:41 + i], c1024, 1, tp, [stat2], stat2, ptmp5[0])

        def f_T(i, hf):
            tp, c0 = tinfo(i)
            if hf == 0:
                PS[("xt", i)] = pb(6, 0, 512, BF16)
                PS[("g", i)] = [pb(2, 0, 512), pb(3, 0, 512)]
            pxt = PS[("xt", i)]; psg = PS[("g", i)]
            ks = range(hf * 4, hf * 4 + 4)
            for k in ks:
                E("pe", lambda e, k=k: e.transpose(out=pxt.ap[:, k * 128:k * 128 + tp], in_=X1b.ap[:tp, k * 128:(k + 1) * 128],
                                                   identity=ident.ap[:tp, :tp]), reads=[X1b, ident],
                  writes=[pxt] if k == 0 else [], pwrites=[] if k == 0 else [pxt])
            E("dve", lambda e: e.tensor_copy(out=X1T.ap[:, hf * 512:(hf + 1) * 512], in_=pxt.ap[:, hf * 512:(hf + 1) * 512]),
              reads=[pxt], writes=[X1T] if hf == 0 else [], pwrites=[] if hf == 0 else [X1T])
            for nb in range(2):
                for k in ks:
                    E("pe", lambda e, nb=nb, k=k: e.matmul(
                        psg[nb].ap[:tp, :], lhsT=X1T.ap[:, k * 128:k * 128 + tp], rhs=wgv[:, k, nb * 512:(nb + 1) * 512],
                        start=(k == 0), stop=(k == 7)), reads=[X1T, wg], writes=[psg[nb]] if k == 0 else [], pwrites=[] if k == 0 else [psg[nb]])

        def f_G(i):
            tp, c0 = tinfo(i)
            psg = PS[("g", i)]
            for nb in range(2):
                E("dve", lambda e, nb=nb: e.tensor_tensor(
                    out=Tf.ap[:tp, nb * 512:(nb + 1) * 512], in0=psg[nb].ap[:tp, :], in1=bpg.ap[:tp, nb * 512:(nb + 1) * 512], op=ALU.add),
                  reads=[psg[nb], bpg], writes=[Tf] if nb == 0 else [], pwrites=[] if nb == 0 else [Tf])

        def f_E(i):
            tp, c0 = tinfo(i)
            pse = PS[("e", i)]
            for nb in range(2):
                E("dve", lambda e, nb=nb: e.scalar_tensor_tensor(
                    out=Ef.ap[:tp, nb * 512:(nb + 1) * 512], in0=pse[nb].ap[:tp, :], scalar=stat2.ap[:tp, 50 + i:51 + i],
                    in1=plg.ap[:tp, nb * 512:(nb + 1) * 512], op0=ALU.mult, op1=ALU.mult), reads=[pse[nb], stat2, plg],
                  writes=[Ef] if nb == 0 else [], pwrites=[] if nb == 0 else [Ef])
            E("act", lambda e: e.activation(out=Gs.ap[:tp], in_=Tf.ap[:tp], func=AF.Sigmoid), reads=[Tf], writes=[Gs])

        def f_R(i):
            tp, c0 = tinfo(i)
            X1 = X1s[i % 2]
            E("dve", lambda e: e.tensor_tensor(out=Gs.ap[:tp], in0=Gs.ap[:tp], in1=Ef.ap[:tp], op=ALU.mult), reads=[Gs, Ef], writes=[Gs])
            E("dve", lambda e: e.tensor_tensor(out=X2.ap[:tp], in0=X1.ap[:tp], in1=Gs.ap[:tp], op=ALU.add),
              reads=[X1, Gs], writes=[X2])
            E("act", lambda e: e.activation(out=Ef.ap[:tp], in_=X2.ap[:tp], func=AF.Square,
                                            accum_out=stat2.ap[:tp, 20 + i:21 + i]), reads=[X2], pwrites=[stat2], writes=[Ef])
            rstd_pool(stat2.ap[:tp, 30 + i:31 + i], stat2.ap[:tp, 20 + i:21 + i], c1024, 1, tp, [stat2], stat2, ptmp5[1])

        def f_Rb(i):
            tp, c0 = tinfo(i)
            E("dve", lambda e: e.scalar_tensor_tensor(out=Yf.ap[:tp], in0=X2.ap[:tp], scalar=stat2.ap[:tp, 30 + i:31 + i],
                                                      in1=fng.ap[:tp], op0=ALU.mult, op1=ALU.mult),
              reads=[X2, stat2, fng], writes=[Yf])
            dst = y_p[t0 + i * 128:t0 + (i + 1) * 128, :] if i < 8 else y_s
            out_dmas.append(DMA("sp", dst, Yf.ap[:tp], reads=[Yf], key=rkey("yo", i, 2)))

        f_L(0)
        if nt > 1:
            f_L(1)
        for t in range(nt):
            f_O(t)
            f_X(t, 0)
            f_T(t, 0)
            f_X(t, 1)
            f_T(t, 1)
            if t >= 2:
                f_Rb(t - 2)
            if t >= 1:
                f_E(t - 1)
            f_P(t)
            if t + 2 < nt:
                f_L(t + 2)
            f_G(t)
            if t >= 1:
                f_R(t - 1)
        if nt >= 2:
            f_Rb(nt - 2)
        f_E(nt - 1)
        f_R(nt - 1)
        f_Rb(nt - 1)

    if limit is not None:
        while True:
            pe_kept = [o for o in P.ops["pe"] if o.seq <= limit]
            if pe_kept and pe_kept[-1].open_group:
                limit += 1
            else:
                break
        for e_ in P.ENGS:
            P.ops[e_] = [o for o in P.ops[e_] if o.seq <= limit]
        out_dmas = [o for o in out_dmas if o.seq <= limit]
        for e_ in ("pe", "act", "dve", "pool"):
            if P.ops[e_]:
                out_dmas.append(P.ops[e_][-1])
        print("limit", limit, "of", Op._seq[0], {e_: len(P.ops[e_]) for e_ in P.ENGS})
    P.add("sp", lambda e: e.nop(), deps=out_dmas)
    P.emit()
    es.close()
    return nc


_NC_CACHE = {}


def kernel(**inp):
    f = lambda a: np.ascontiguousarray(np.asarray(a, dtype=np.float32))
    if "nc" not in _NC_CACHE:
        _NC_CACHE["nc"] = build()
    nc = _NC_CACHE["nc"]
    x_prompt = f(inp["x_prompt"]); x_sample = f(inp["x_sample"]).reshape(128, 1024)
    C = f(inp["state_mlstm_C"])[0]; n = f(inp["state_mlstm_n"])[0].reshape(128, 1024)
    m = f(inp["state_mlstm_m"])[0]; cv = f(inp["state_conv"])[0]
    p_prompt = f(inp["p_prompt"])[0]; p_sample = f(inp["p_sample"])[0].reshape(128, 256)
    shared = {
        "norm_g": f(inp["norm_in_g"])[0], "w_in": f(inp["w_in"])[0],
        "lnv_g": f(inp["ln_v_g"])[0].reshape(1024), "lnv_b": f(inp["ln_v_b"])[0].reshape(1024),
        "w_sp": f(inp["w_spatial"])[0], "b_sp": f(inp["b_spatial"])[0],
        "conv_w": f(inp["conv_w"])[0], "conv_b": f(inp["conv_b"])[0],
        "w_q": f(inp["w_q"])[0], "w_k": f(inp["w_k"])[0], "w_v": f(inp["w_v"])[0],
        "w_if": f(inp["w_if"])[0], "b_if": f(inp["b_if"])[0],
        "gn_g": f(inp["gn_g"])[0].reshape(1024), "skip": f(inp["skip"])[0],
        "w_out": f(inp["w_out"])[0], "w_pg": f(inp["w_ple_gate"])[0], "b_pg": f(inp["b_ple_gate"])[0],
        "w_pp": f(inp["w_ple_proj"])[0], "ple_g": f(inp["ple_norm_g"])[0], "fin_g": f(inp["final_norm_g"]),
    }
    in_maps = []
    for c in range(8):
        s = slice(c * 16, (c + 1) * 16)
        d = dict(shared)
        d.update({"xp": x_prompt[c], "xs": x_sample[s], "pp": p_prompt[c], "psm": p_sample[s],
                  "C_in": C[s], "n_in": n[s], "m_in": m[s], "cv_in": cv[s]})
        in_maps.append(d)
    res = run_bass_kernel_spmd(nc, in_maps, core_ids=list(range(8)))
    R = res.results
    cat = lambda k: np.concatenate([np.asarray(r[k]) for r in R], axis=0)
    stk = lambda k: np.stack([np.asarray(r[k]) for r in R], axis=0)
    y_prompt = stk("y_p")
    y_sample = cat("y_s").reshape(128, 1, 1024)
    Cp_ = stk("Cp")[None]
    np__ = stk("np_o")[None]
    mp_ = stk("mp_o").reshape(8, 4)[None]
    cvp_ = stk("cvp")[None]
    Cs_ = cat("Cs")[None]
    ns_ = cat("ns_o").reshape(128, 4, 256)[None]
    ms_ = cat("ms_o")[None]
    cvs_ = cat("cvs")[None]
    vs_ = cat("vs_o").reshape(128, 1, 1024)[None]
    return (y_prompt.astype(np.float32), y_sample.astype(np.float32), Cp_.astype(np.float32), np__.astype(np.float32),
            mp_.astype(np.float32), cvp_.astype(np.float32), Cs_.astype(np.float32), ns_.astype(np.float32),
            ms_.astype(np.float32), cvs_.astype(np.float32), vs_.astype(np.float32))
```

```python
import numpy as np
from contextlib import ExitStack
import concourse.bass as bass
import concourse.mybir as mybir
from concourse.bass_utils import run_bass_kernel_spmd

F32 = mybir.dt.float32
BF16 = mybir.dt.bfloat16
AF = mybir.ActivationFunctionType
ALU = mybir.AluOpType
AX = mybir.AxisListType

T = 2048
TH = 1024
NS = 16
TC = TH + NS
XW = TC + 3
SLOT = 8352
NSLOT = 7
LN16 = 2.772588722239781
EPS = 1e-6
DEBUG = {}


class Op:
    __slots__ = ("eng", "fn", "deps", "ticket", "needed", "dkey", "dval", "seq", "open_group")
    _seq = [0]

    def __init__(self, eng, fn, deps):
        Op._seq[0] += 1
        self.seq = Op._seq[0]
        self.eng = eng
        self.fn = fn
        self.deps = deps
        self.ticket = None
        self.needed = False
        self.dkey = None
        self.dval = None


class _Rec:
    def __getattr__(self, name):
        def f(*a, **k):
            self.__dict__["call"] = (name, a, k)
            return None
        return f


class Prog:
    ENGS = ("pe", "act", "dve", "pool", "sp")

    def __init__(self, nc):
        self.nc = nc
        self.ops = {e: [] for e in self.ENGS}
        self.dcount = {}

    def add(self, eng, fn, deps=()):
        dl = [d for d in deps if d is not None]
        rec = _Rec()
        fn(rec)
        name, a, k = rec.call
        fn = (lambda e, name=name, a=a, k=k: getattr(e, name)(*a, **k))
        op = Op(eng, fn, dl)
        op.open_group = (name == "matmul" and k.get("stop") is False)
        for d in dl:
            d.needed = True
        self.ops[eng].append(op)
        return op

    def dma(self, queue, out, in_, key, deps=(), **kw):
        def fn(e):
            return e.dma_start(out=out, in_=in_, **kw)

        op = self.add(queue, fn, deps)
        self.dcount[key] = self.dcount.get(key, 0) + 16
        op.dkey = key
        op.dval = self.dcount[key]
        return op

    def emit(self):
        nc = self.nc
        with ExitStack() as es:
            esem = {e: es.enter_context(nc.semaphore("s_" + e)) for e in self.ENGS}
            dsem = {k: es.enter_context(nc.semaphore("d_%s" % (k,))) for k in self.dcount}
            for e in self.ENGS:
                c = 0
                for op in self.ops[e]:
                    if op.dkey is None and op.needed:
                        c += 1
                        op.ticket = c
            block = es.enter_context(nc.Block())

            def run(ename, eng):
                waited = {}
                for op in self.ops[ename]:
                    for d in op.deps:
                        if d.dkey is not None:
                            s, v = dsem[d.dkey], d.dval
                        else:
                            s, v = esem[d.eng], d.ticket
                        if waited.get(s.name, 0) < v:
                            eng.wait_ge(s, v)
                            waited[s.name] = v
                    ins = op.fn(eng)
                    if op.dkey is not None:
                        ins.then_inc(dsem[op.dkey], 16)
                    elif op.needed:
                        ins.then_inc(esem[ename], 1)

            @block.tensor
            def _(e):
                run("pe", e)

            @block.scalar
            def _(e):
                run("act", e)

            @block.vector
            def _(e):
                run("dve", e)

            @block.gpsimd
            def _(e):
                run("pool", e)

            @block.sync
            def _(e):
                run("sp", e)


def _key(op):
    return ("d", op.dkey) if op.dkey is not None else ("e", op.eng)


class Buf:
    registry = []

    def __init__(self, ap, tname=None, lo=0, hi=0, psum=False):
        self.psum = psum
        self.ap = ap
        self.wr = {}
        self.rd = {}
        self.old = {}
        if tname is not None:
            for (tn, l, h, b) in Buf.registry:
                if tn == tname and l < hi and lo < h:
                    for d in (b.old, b.wr, b.rd):
                        for k, v in d.items():
                            self._put(self.old, k, v)
            Buf.registry.append((tname, lo, hi, self))

    @staticmethod
    def _put(d, k, op):
        cur = d.get(k)
        if cur is None:
            d[k] = op
        elif op.dkey is not None:
            if op.dval > cur.dval:
                d[k] = op
        elif op.seq > cur.seq:
            d[k] = op

    def __getitem__(self, idx):
        return self.ap[idx]

    def rdeps(self, eng):
        out = [op for k, op in self.wr.items() if not (eng == "pe" and k == ("e", "pe"))]
        if self.psum:
            out += [op for k, op in self.rd.items() if k != ("e", eng)]
        return out

    def wdeps(self, eng, partial=False):
        skip = ("e", "pe") if eng == "pe" else None
        out = [op for k, op in self.old.items() if k != skip]
        if not partial:
            out += [op for k, op in self.rd.items() if k != skip]
            out += [op for k, op in self.wr.items() if k != skip]
        return out

    def read(self, op):
        self._put(self.rd, _key(op), op)

    def wrote(self, op, partial=False):
        if not partial:
            self.wr = {}
            self.rd = {}
            self.old = {}
        self._put(self.wr, _key(op), op)


def build(limit=None):
    Buf.registry = []
    Op._seq[0] = 0
    nc = bass.Bass("TRN2", target_bir_lowering=False)
    P = Prog(nc)
    es = ExitStack()

    def din(name, shape):
        return nc.dram_tensor(name, list(shape), F32, kind="ExternalInput").ap()

    def dout(name, shape):
        return nc.dram_tensor(name, list(shape), F32, kind="ExternalOutput").ap()

    xp = din("xp", [T, 1024]); xs = din("xs", [NS, 1024])
    pp_ = din("pp", [T, 256]); psm = din("psm", [NS, 256])
    C_in = din("C_in", [NS, 4, 256, 256]); n_in = din("n_in", [NS, 1024]); m_in = din("m_in", [NS, 4])
    cv_in = din("cv_in", [NS, 3, 1024])
    norm_g = din("norm_g", [1024]); w_in = din("w_in", [1024, 6144])
    lnv_g = din("lnv_g", [1024]); lnv_b = din("lnv_b", [1024])
    w_sp = din("w_sp", [8, 128, 128]); b_sp = din("b_sp", [8, 128])
    conv_w = din("conv_w", [4, 1024]); conv_b = din("conv_b", [1024])
    w_q = din("w_q", [4, 256, 256]); w_k = din("w_k", [4, 256, 256]); w_v = din("w_v", [4, 256, 256])
    w_if = din("w_if", [3072, 8]); b_if = din("b_if", [8])
    gn_g = din("gn_g", [1024]); skip = din("skip", [1024])
    w_out = din("w_out", [2048, 1024]); w_pg = din("w_pg", [1024, 1024]); b_pg = din("b_pg", [1024])
    w_pp = din("w_pp", [256, 1024]); ple_g = din("ple_g", [1024]); fin_g = din("fin_g", [1024])

    y_p = dout("y_p", [T, 1024]); y_s = dout("y_s", [NS, 1024])
    Cp = dout("Cp", [4, 256, 256]); np_o = dout("np_o", [4, 256]); mp_o = dout("mp_o", [1, 4])
    cvp = dout("cvp", [3, 1024])
    Cs = dout("Cs", [NS, 4, 256, 256]); ns_o = dout("ns_o", [NS, 1024]); ms_o = dout("ms_o", [NS, 4])
    cvs = dout("cvs", [NS, 3, 1024]); vs_o = dout("vs_o", [NS, 1024])

    hscr = nc.dram_tensor("hscr", [128, 8 * TC], BF16, kind="Internal").ap()
    hscrb = Buf(hscr, "hscr", 0, 1)
    out_dmas = []
    cnt = [0]

    def sbt(shape, dt, name=None):
        cnt[0] += 1
        return es.enter_context(nc.sbuf_tensor(name or ("t%d" % cnt[0]), list(shape), dt))

    def newbuf(shape, dt):
        t = sbt(shape, dt)
        ap = t[:] if len(shape) == 2 else t[tuple(slice(None) for _ in shape)]
        b = Buf(ap, "nb%d" % cnt[0], 0, 1)
        b.tname = "nb%d" % cnt[0]
        return b

    def rebuf(b):
        nb_ = Buf(b.ap, b.tname, 0, 1)
        nb_.tname = b.tname
        return nb_

    def E(eng, fn, reads=(), writes=(), pwrites=(), extra=(), war=()):
        deps = list(extra)
        for b in war:
            deps += [op for k, op in b.rd.items()]
        for b in reads:
            deps += b.rdeps(eng)
        for b in writes:
            deps += b.wdeps(eng)
        for b in pwrites:
            deps += b.wdeps(eng, partial=True)
        op = P.add(eng, fn, deps)
        for b in reads:
            b.read(op)
        for b in writes:
            b.wrote(op)
        for b in pwrites:
            b.wrote(op, partial=True)
        return op

    dkc = [0]

    def DMA(queue, out, in_, reads=(), writes=(), pwrites=(), key=None, **kw):
        deps = []
        for b in reads:
            deps += b.rdeps(queue)
        for b in writes:
            deps += b.wdeps(queue)
        for b in pwrites:
            deps += b.wdeps(queue, partial=True)
        if key is None:
            dkc[0] += 1
            key = "k%d" % dkc[0]
        op = P.dma(queue, out, in_, key, deps, **kw)
        for b in reads:
            b.read(op)
        for b in writes:
            b.wrote(op)
        for b in pwrites:
            b.wrote(op, partial=True)
        return op

    def rkey(prefix, i, n):
        return "%s%d" % (prefix, i % n)

    arena = sbt([128, NSLOT * SLOT], BF16, "arena")
    FS = 4096
    fscr = sbt([128, FS], F32, "fscr")
    cst = sbt([128, 3, 1024], F32, "cst")

    def ar(slot, off, n, dt=BF16):
        lo = slot * SLOT + off
        ap = arena[:, lo:lo + n]
        if dt == F32:
            ap = ap.bitcast(F32)
        return Buf(ap, "arena", lo * 2, (lo + n) * 2)

    def fs(off, n, dt=F32):
        ap = fscr[:, off:off + n]
        if dt == BF16:
            ap = ap.bitcast(BF16)
        return Buf(ap, "fscr", off * 4, (off + n) * 4)

    def cs_(i):
        return Buf(cst[:, i, :], "cst", i * 4096, (i + 1) * 4096)

    banks = [es.enter_context(nc.psum_tensor("bank%d" % i, [128, 512], F32)) for i in range(8)]

    def pb(bank, off, n, dt=F32):
        ap = banks[bank][:, off:off + n]
        if dt == BF16:
            ap = ap.bitcast(BF16)
        return Buf(ap, "bank%d" % bank, 0, 2048, psum=True)

    ident = newbuf([128, 128], BF16); identf = newbuf([128, 128], F32)
    maskT = newbuf([128, 128], BF16); tri = newbuf([128, 128], F32)
    onesf = newbuf([128, 128], F32); onesb = newbuf([128, 128], BF16)
    epsc = newbuf([128, 1], F32); mhalf = newbuf([128, 1], F32); onec = newbuf([128, 1], F32)
    nln16 = newbuf([128, 1], F32)
    monec = newbuf([128, 1], F32)
    c1024 = newbuf([128, 1], F32); c128 = newbuf([128, 1], F32); c256 = newbuf([128, 1], F32)

    def memset(b, val, eng="pool"):
        return E(eng, lambda e: e.memset(b.ap, val), writes=[b])

    memset(identf, 1.0)
    E("pool", lambda e: e.affine_select(out=identf.ap, in_=identf.ap, pattern=[[-1, 128]], compare_op=ALU.is_equal,
                                        fill=0.0, base=0, channel_multiplier=1), reads=[identf], writes=[identf])
    E("pool", lambda e: e.tensor_copy(out=ident.ap, in_=identf.ap), reads=[identf], writes=[ident])
    memset(tri, 1.0)
    E("pool", lambda e: e.affine_select(out=tri.ap, in_=tri.ap, pattern=[[1, 128]], compare_op=ALU.is_ge,
                                        fill=0.0, base=0, channel_multiplier=-1), reads=[tri], writes=[tri])
    E("pool", lambda e: e.tensor_copy(out=maskT.ap, in_=tri.ap), reads=[tri], writes=[maskT])
    memset(onesf, 1.0); memset(onesb, 1.0)
    memset(epsc, EPS); memset(mhalf, -0.5); memset(onec, 1.0); memset(nln16, -LN16); memset(monec, -1.0)
    memset(c1024, 1.0 / 1024); memset(c128, 1.0 / 128); memset(c256, 1.0 / 256)

    def rstd_pool(out_ap, in_ap, cinv, n, tp, rbufs, wbuf, tmpb):
        def bc(c):
            return c.ap[:tp, 0:1] if n == 1 else c.ap[:tp, 0:1].broadcast_to([tp, n])
        E("pool", lambda e: e.tensor_tensor(out=tmpb.ap[:tp, 0:n], in0=in_ap, in1=bc(cinv), op=ALU.mult),
          reads=list(rbufs) + [cinv], writes=[tmpb])
        E("pool", lambda e: e.tensor_tensor(out=tmpb.ap[:tp, 0:n], in0=tmpb.ap[:tp, 0:n], in1=bc(epsc), op=ALU.add),
          reads=[tmpb, epsc], writes=[tmpb])
        return E("pool", lambda e: e.tensor_tensor(out=out_ap, in0=tmpb.ap[:tp, 0:n], in1=bc(mhalf), op=ALU.pow),
                 reads=[tmpb, mhalf], pwrites=[wbuf])

    wq = newbuf([128, 4, 2, 256], BF16); wk = newbuf([128, 4, 2, 256], BF16); wv = newbuf([128, 4, 2, 256], BF16)
    wif = newbuf([128, 24, 8], BF16)
    bifb = newbuf([128, 8], F32)
    gngT = newbuf([128, 8], F32); skipT = newbuf([128, 8], F32); cbT = newbuf([128, 8], F32)
    lgT = newbuf([128, 8], F32); lbT = newbuf([128, 8], F32)
    cwT = newbuf([128, 4, 8], F32)
    bsp0 = newbuf([128, 8], F32)
    W00 = newbuf([16, 8], F32)
    WT = newbuf([128, 8, 128], BF16)
    Wdiag = newbuf([16, 8, 16], BF16)

    def deferred_setup():
        for wb_, src in ((wq, w_q), (wk, w_k), (wv, w_v)):
            DMA("pool", wb_.ap, src.rearrange("h (kk p) e -> p h kk e", p=128), writes=[wb_])
        DMA("pool", wif.ap, w_if.rearrange("(k p) g -> p k g", p=128), writes=[wif])
        DMA("sp", bifb.ap, b_if.partition_broadcast(128), writes=[bifb])
        for b_, src in ((gngT, gn_g), (skipT, skip), (cbT, conv_b), (lgT, lnv_g), (lbT, lnv_b)):
            DMA("sp", b_.ap, src.rearrange("(c p) -> p c", p=128), writes=[b_], allow_slow_non_contiguous=True)
        DMA("sp", cwT.ap, conv_w.rearrange("j (c p) -> p j c", p=128), writes=[cwT], allow_slow_non_contiguous=True)
        DMA("sp", bsp0.ap, b_sp[:, 0].partition_broadcast(128), writes=[bsp0], allow_slow_non_contiguous=True)
        DMA("sp", W00.ap, w_sp[:, 0, 0].partition_broadcast(16), writes=[W00], allow_slow_non_contiguous=True)
        wspf = fs(0, 1024)
        DMA("sp", wspf.ap.rearrange("p (h s) -> p h s", h=8), w_sp.rearrange("h t s -> t h s"), writes=[wspf])
        wtf = fs(1024, 1024)
        for half in range(2):
            pw_ = pb(half, 0, 512)
            for hh in range(4):
                h = half * 4 + hh
                E("pe", lambda e, h=h, hh=hh, pw_=pw_: e.transpose(out=pw_.ap[:, hh * 128:(hh + 1) * 128],
                                                                   in_=wspf.ap[:, h * 128:(h + 1) * 128], identity=identf.ap),
                  reads=[wspf, identf], pwrites=[pw_])
            E("act", lambda e, half=half, pw_=pw_: e.copy(out=wtf.ap[:, half * 512:(half + 1) * 512], in_=pw_.ap),
              reads=[pw_], pwrites=[wtf])
        E("pool", lambda e: e.affine_select(out=WT.ap, in_=wtf.ap.rearrange("p (h t) -> p h t", h=8),
                                            pattern=[[0, 8], [1, 128]], compare_op=ALU.is_ge, fill=0.0, base=0,
                                            channel_multiplier=-1), reads=[wtf], writes=[WT])
        E("dve", lambda e: e.tensor_tensor(out=Wdiag.ap, in0=identf.ap[0:16, None, 0:16].broadcast_to([16, 8, 16]),
                                           in1=W00.ap[:, :, None].broadcast_to([16, 8, 16]), op=ALU.mult),
          reads=[identf, W00], writes=[Wdiag])

    CT = newbuf([128, 4, 2, 256], F32); nT = newbuf([128, 4, 2], F32)
    memset(CT, 0.0); memset(nT, 0.0)
    mcar = newbuf([1, 4], F32)
    memset(mcar, 0.0)
    tails = newbuf([128, 8, 3], BF16)
    memset(tails, 0.0)
    G_g = newbuf([128, 8, 9], F32)
    E1 = newbuf([128, 4, 8], F32); LFN = newbuf([128, 4, 8], F32); BNs = newbuf([128, 4, 8], F32)
    Csb = newbuf([128, 4, 8], F32); A1 = newbuf([128, 4, 8], F32)
    WS = newbuf([128, 4, 8], F32); FL = newbuf([128, 4, 8], F32); GSb = newbuf([128, 4, 8], F32)
    cmaxc = newbuf([32, 1], F32)
    rowA = newbuf([1, 32], F32); rowB = newbuf([1, 32], F32); MN = newbuf([1, 32], F32); MPV = newbuf([1, 32], F32)
    RG = newbuf([1, 64], F32); rtmp = newbuf([1, 32], F32)
    S12 = newbuf([16, 12], F32); sg = newbuf([16, 64], F32)
    m_s = newbuf([16, 4], F32)
    BD = newbuf([16, 16, 12], F32); bcs = newbuf([128, 16, 12], F32)
    rinv_s = newbuf([16, 4], F32)
    vTs = newbuf([128, 8, 16], F32); bufT = newbuf([128, 8, 3, 16], F32)
    qks = newbuf([16, 2, 1024], BF16)
    sgs = newbuf([16, 1024], BF16)
    vns = newbuf([16, 1024], BF16)
    CqT = newbuf([128, 4, 2, 16], F32); wvT = newbuf([128, 4, 2, 16], F32); numTs = newbuf([128, 4, 2, 16], F32)
    stat_g = newbuf([128, 64], F32)
    stat2_g = newbuf([128, 64], F32)
    ptmp = newbuf([128, 80], F32)
    ptmps = [newbuf([128, 2], F32), newbuf([128, 2], F32)]
    S1_g = newbuf([128, 9, 8], F32); S2_g = newbuf([128, 9, 8], F32); MEAN = newbuf([128, 9, 8], F32)
    RSTD = newbuf([128, 9, 8], F32)
    memset(S1_g, 0.0); memset(S2_g, 0.0)

    DMA("sp", m_s.ap, m_in, writes=[m_s])
    out_dmas.append(P.dma("sp", cvs[:, 0:2, :], cv_in[:, 1:3, :], "cvcp"))

    NWB = 2
    wblk = [newbuf([128, 8, 512], BF16) for _ in range(NWB)]
    wbi = [0]

    def load_wblk(blk):
        b = wblk[wbi[0] % NWB]
        DMA("pool", b.ap, w_in[:, blk * 512:(blk + 1) * 512].rearrange("(k p) n -> p k n", p=128), writes=[b],
            key=rkey("wb", wbi[0], NWB))
        wbi[0] += 1
        return b

    psrot = [0]

    def colgroups(pp):
        g = [(0, 512), (512, 512)]
        if pp == 0:
            g.append((1024, 16))
        return g

    def ntiles(pp):
        return 9 if pp == 0 else 8

    def tinfo(i):
        return (128, i * 128) if i < 8 else (NS, TH)

    evrot = [0]

    def evac(out_ap, in_ap, reads, pw=None, w=None, eng=None, func=None, scale=1.0, bias=None):
        if eng is None:
            eng = "act" if (func is not None or evrot[0] % 2 == 0) else "dve"
            evrot[0] += 1
        kw = dict(reads=list(reads), pwrites=[pw] if pw else [], writes=[w] if w else [])
        if eng == "act":
            f = func or AF.Copy
            if bias is not None:
                kw["reads"].append(bias[0])
                return E("act", lambda e: e.activation(out=out_ap, in_=in_ap, func=f, scale=scale, bias=bias[1]), **kw)
            return E("act", lambda e: e.activation(out=out_ap, in_=in_ap, func=f, scale=scale), **kw)
        return E(eng, lambda e: e.tensor_copy(out=out_ap, in_=in_ap), **kw)

    def pipeline(n, stages):
        S = len(stages)
        for t in range(n + S - 1):
            for s_ in reversed(range(S)):
                i = t - s_
                if 0 <= i < n:
                    stages[s_](i)

    def phase_hT(pp, hT, hook=None, hTg=None):
        t0 = pp * TH
        stat = rebuf(stat_g)
        gin = cs_(0)
        DMA("sp", gin.ap, norm_g.partition_broadcast(128), writes=[gin])
        NXB = 6
        xts = [ar(2 + i // 4, (i % 4) * 2048, 2048, F32) for i in range(NXB)]
        hbs = [fs(3072, 512, BF16), fs(3584, 512, BF16)]
        ptrs = [pb(0, 0, 512, BF16), pb(1, 0, 512, BF16)]
        pjunk = Buf(cst[:, 1, 0:512].bitcast(BF16), "cst", 4096, 4096 + 2048)
        hTv_ = hT.ap.rearrange("p (c t) -> p c t", c=8)

        def s0(i):
            tp, c0 = tinfo(i)
            xt = xts[i % NXB]
            src = xp[t0 + i * 128:t0 + (i + 1) * 128, :] if i < 8 else xs
            DMA("sp", xt.ap[:tp], src, writes=[xt], key=rkey("x", i, NXB))
            E("act", lambda e: e.activation(out=pjunk.ap[:tp], in_=xt.ap[:tp], func=AF.Square,
                                            accum_out=stat.ap[:tp, i:i + 1]), reads=[xt], pwrites=[stat], writes=[pjunk])
            rstd_pool(stat.ap[:tp, 16 + i:17 + i], stat.ap[:tp, i:i + 1], c1024, 1, tp, [stat], stat, ptmps[i % 2])
            if i == 2 and hook is not None:
                hook()

        def s1(i):
            tp, c0 = tinfo(i)
            xt = xts[i % NXB]; hb = hbs[i % 2]
            E("dve", lambda e: e.scalar_tensor_tensor(
                out=hb.ap[:tp], in0=xt.ap[:tp], scalar=stat.ap[:tp, 16 + i:17 + i], in1=gin.ap[:tp],
                op0=ALU.mult, op1=ALU.mult), reads=[xt, stat, gin], writes=[hb])

        def s2(i):
            tp, c0 = tinfo(i)
            hb = hbs[i % 2]; ptr = ptrs[i % 2]
            for k in range(8):
                E("pe", lambda e, k=k: e.transpose(
                    out=ptr.ap[:, k * 128:k * 128 + tp], in_=hb.ap[:tp, k * 128:(k + 1) * 128],
                    identity=ident.ap[:tp, :tp]), reads=[hb, ident], writes=[ptr] if k == 0 else [], pwrites=[] if k == 0 else [ptr])
            evac(hTv_[:, :, c0:c0 + tp], ptr.ap.rearrange("p (c t) -> p c t", c=8)[:, :, 0:tp], [ptr],
                 pw=hTg[i // 4 if i < 8 else 2])

        pipeline(ntiles(pp), [s0, s1, s2])

    def projB(pp, lhs_fn, nk, rhs_fn, out_fn, lbufs, rbufs, groups=None, rb_fn=None):
        for (c0, n) in (groups if groups is not None else colgroups(pp)):
            ps = pb(psrot[0] % 6, 0, 512); psrot[0] += 1
            rb = list(rbufs) if rb_fn is None else [rb_fn(c0)]
            for k in range(nk):
                E("pe", lambda e, k=k, c0=c0, n=n, ps=ps: e.matmul(ps.ap[:, 0:n], lhsT=lhs_fn(k), rhs=rhs_fn(k, c0, n),
                                                                   start=(k == 0), stop=(k == nk - 1)),
                  reads=list(lbufs) + rb, writes=[ps] if k == 0 else [], pwrites=[] if k == 0 else [ps])
            out_fn(ps, c0, n)

    def projA(lhs_fn, nk, rhs_fn, tp, lbufs, rbufs, bank=None):
        if bank is None:
            bank = psrot[0] % 6; psrot[0] += 1
        ps = pb(bank, 0, 512)
        for k in range(nk):
            E("pe", lambda e, k=k, ps=ps: e.matmul(ps.ap[:tp, :], lhsT=lhs_fn(k), rhs=rhs_fn(k),
                                                   start=(k == 0), stop=(k == nk - 1)),
              reads=list(lbufs) + list(rbufs), writes=[ps] if k == 0 else [], pwrites=[] if k == 0 else [ps])
        return ps

    V3 = lambda b, c: b.ap.rearrange("p (c t) -> p c t", c=c)

    for pp in range(2):
        t0 = pp * TH
        G = rebuf(G_g); stat2 = rebuf(stat2_g); S1 = rebuf(S1_g); S2 = rebuf(S2_g)
        NC_ = TC if pp == 0 else TH
        nt = ntiles(pp)
        wxb = []
        hT = ar(0, 0, 8 * TC)
        hTg = [Buf(hT.ap, "arena", 0, 8 * TC * 2) for _ in range(3)]
        hTl = hTg if pp == 0 else hTg[0:2]
        hgrp = lambda c0: hTg[min(c0 // 512, 2)]
        phase_hT(pp, hT, hook=lambda: wxb.extend([load_wblk(6), load_wblk(7)]), hTg=hTg)
        if pp == 0:
            deferred_setup()
        hTv = V3(hT, 8)
        hscrb = Buf(hscr, "hscr", 0, 1)
        DMA("sp", hscr.rearrange("p (c t) -> p c t", c=8)[:, :, 0:NC_], hTv[:, :, 0:NC_], reads=hTl, writes=[hscrb])
        xbT = ar(1, 0, 8 * XW); xbv = V3(xbT, 8)
        xcT = ar(2, 0, 8 * TC); xcv = V3(xcT, 8)
        qT = ar(3, 0, 8 * TC); qv = V3(qT, 8)
        kT = ar(4, 0, 8 * TC); kv = V3(kT, 8)
        vaug = ar(5, 0, 8 * 4 * 257); vav = vaug.ap.rearrange("p (i h e) -> p i h e", i=8, h=4)
        vT = ar(6, 0, 8 * TC); vv = V3(vT, 8)

        if pp == 0:
            cvtok = ar(6, 0, 6144, F32)
            DMA("sp", cvtok.ap[0:16, :], cv_in.rearrange("b j c -> b (j c)"), writes=[cvtok])
            pcv = pb(7, 0, 384)
            for j in range(3):
                for c in range(8):
                    idx = c * 3 + j
                    E("pe", lambda e, j=j, c=c, idx=idx: e.transpose(
                        out=pcv.ap[:, idx * 16:(idx + 1) * 16],
                        in_=cvtok.ap[0:16, j * 1024 + c * 128:j * 1024 + (c + 1) * 128], identity=identf.ap[0:16, 0:16]),
                      reads=[cvtok, identf], pwrites=[pcv])
            evac(bufT.ap.rearrange("p c j b -> p (c j b)"), pcv.ap, [pcv], w=bufT, eng="dve")
        E("dve", lambda e: e.tensor_copy(out=xbv[:, :, 0:3], in_=tails.ap), reads=[tails], pwrites=[xbT])
        xbtok = fs(3072, 1024)
        tok_tile = 8 if pp == 0 else 7
        for grp in colgroups(pp):
            for bi, blk in enumerate((6, 7)):
                wb = wxb[bi]
                for cc in range(4):
                    c = bi * 4 + cc

                    def outf(ps, c0, n, c=c):
                        evac(xbv[:, c, 3 + c0:3 + c0 + n], ps.ap[:, 0:n], [ps], pw=xbT)
                    projB(pp, lambda k, cc=cc, wb=wb: wb.ap[:, k, cc * 128:(cc + 1) * 128], 8,
                          lambda k, c0, n: hTv[:, k, c0:c0 + n], outf, [wb], [], groups=[grp], rb_fn=hgrp)
        for bi, blk in enumerate((6, 7)):
            wb = wxb[bi]
            tp, tc0 = tinfo(tok_tile)
            ps = projA(lambda k: hTv[:, k, tc0:tc0 + tp], 8, lambda k, wb=wb: wb.ap[:, k, :], tp, [hgrp(tc0)], [wb])
            evac(xbtok.ap[:tp, bi * 512:(bi + 1) * 512], ps.ap[:tp, :], [ps], pw=xbtok)
        if pp == 0:
            out_dmas.append(DMA("sp", cvs[:, 2, :], xbtok.ap[0:16, :], reads=[xbtok]))
        else:
            out_dmas.append(DMA("sp", cvp, xbtok.ap[125:128, :], reads=[xbtok]))
        if pp == 0:
            E("dve", lambda e: e.tensor_copy(out=tails.ap, in_=xbv[:, :, 3 + TH - 3:3 + TH]), reads=[xbT], writes=[tails])

        Dws = [fs(0, 256, BF16), fs(256, 256, BF16)]
        for c in range(8):
            Dw = Dws[c % 2]
            d3 = Dw.ap.rearrange("p (j m) -> p j m", j=4)
            E("dve", lambda e, c=c, d3=d3: e.tensor_tensor(out=d3, in0=identf.ap[:, None, :].broadcast_to([128, 4, 128]),
                                                          in1=cwT.ap[:, :, c:c + 1].broadcast_to([128, 4, 128]), op=ALU.mult),
              reads=[identf, cwT], writes=[Dw])
            for g_ in range(2):
                ps = pb(psrot[0] % 6, 0, 512); psrot[0] += 1
                for j in range(4):
                    E("pe", lambda e, c=c, j=j, g_=g_, ps=ps, d3=d3: e.matmul(
                        ps.ap, lhsT=d3[:, j, :], rhs=xbv[:, c, j + g_ * 512:j + g_ * 512 + 512], start=(j == 0), stop=(j == 3)),
                      reads=[Dw, xbT], writes=[ps] if j == 0 else [], pwrites=[] if j == 0 else [ps])
                evac(xcv[:, c, g_ * 512:(g_ + 1) * 512], ps.ap, [ps], pw=xcT, eng="act", func=AF.Silu, bias=(cbT, cbT.ap[:, c:c + 1]))
        if pp == 0:
            accS = fs(2048, 128)
            av = accS.ap.rearrange("p (c b) -> p c b", c=8)
            for c in range(8):
                E("dve", lambda e, c=c: e.tensor_scalar(out=av[:, c, :], in0=xbv[:, c, 3 + TH:3 + TC], scalar1=cwT.ap[:, 3, c:c + 1],
                                                       scalar2=None, op0=ALU.mult), reads=[xbT, cwT], pwrites=[accS])
                for j in range(3):
                    E("dve", lambda e, c=c, j=j: e.scalar_tensor_tensor(
                        out=av[:, c, :], in0=bufT.ap[:, c, j, :], scalar=cwT.ap[:, j, c:c + 1], in1=av[:, c, :],
                        op0=ALU.mult, op1=ALU.add), reads=[bufT, cwT, accS], pwrites=[accS])
                evac(xcv[:, c, TH:TC], av[:, c, :], [accS], pw=xcT, eng="act", func=AF.Silu, bias=(cbT, cbT.ap[:, c:c + 1]))

        E("pool", lambda e: e.memset(vav[:, :, :, 256:257], 1.0), pwrites=[vaug])
        for (dst, dv, wsrc, src, sv, soff) in ((qT, qv, wq, xcT, xcv, 0), (kT, kv, wk, xcT, xcv, 0), (vT, vv, wv, xbT, xbv, 3)):
            for h in range(4):
                for ec in range(2):
                    def outf(ps, c0, n, h=h, ec=ec, dst=dst, dv=dv):
                        evac(dv[:, 2 * h + ec, c0:c0 + n], ps.ap[:, 0:n], [ps], pw=dst)
                        if dst is vT and c0 == TH:
                            evac(vTs.ap[:, 2 * h + ec, :], ps.ap[:, 0:n], [ps], pw=vTs, eng="dve")
                    projB(pp, lambda k, h=h, ec=ec, wsrc=wsrc: wsrc.ap[:, h, k, ec * 128:(ec + 1) * 128], 2,
                          lambda k, c0, n, h=h, sv=sv, soff=soff: sv[:, 2 * h + k, soff + c0:soff + c0 + n], outf, [wsrc], [src])
        for i in range(8):
            for hb2 in range(2):
                ps = pb(psrot[0] % 6, 0, 512); psrot[0] += 1
                for hh in range(2):
                    h = hb2 * 2 + hh
                    for kk in range(2):
                        E("pe", lambda e, i=i, h=h, hh=hh, kk=kk, ps=ps: e.matmul(
                            ps.ap[:, hh * 256:(hh + 1) * 256], lhsT=xbv[:, 2 * h + kk, 3 + i * 128:3 + (i + 1) * 128],
                            rhs=wv.ap[:, h, kk, :], start=(kk == 0), stop=(kk == 1)),
                          reads=[xbT, wv], writes=[ps] if (hh == 0 and kk == 0) else [], pwrites=[] if (hh == 0 and kk == 0) else [ps])
                evac(vav[:, i, hb2 * 2:hb2 * 2 + 2, 0:256], ps.ap.rearrange("p (h e) -> p h e", h=2), [ps], pw=vaug)
        if pp == 0:
            q_sf = fs(0, 1024); k_sf = fs(1024, 1024); n_sf = fs(2048, 1024); tmp_s = fs(3072, 1024)
            DMA("sp", n_sf.ap[0:16, :], n_in, writes=[n_sf])
            for (wsrc, dstf, qi) in ((wq, q_sf, 0), (wk, k_sf, 1)):
                for hb2 in range(2):
                    ps = pb(psrot[0] % 6, 0, 512); psrot[0] += 1
                    for hh in range(2):
                        h = hb2 * 2 + hh
                        for kk in range(2):
                            E("pe", lambda e, h=h, hh=hh, kk=kk, ps=ps, wsrc=wsrc: e.matmul(
                                ps.ap[0:16, hh * 256:(hh + 1) * 256], lhsT=xcv[:, 2 * h + kk, TH:TC],
                                rhs=wsrc.ap[:, h, kk, :], start=(kk == 0), stop=(kk == 1)),
                              reads=[xcT, wsrc], writes=[ps] if (hh == 0 and kk == 0) else [],
                              pwrites=[] if (hh == 0 and kk == 0) else [ps])
                    evac(dstf.ap[0:16, hb2 * 512:(hb2 + 1) * 512], ps.ap[0:16, :], [ps], pw=dstf, eng="act")
                    evac(qks.ap[0:16, qi, hb2 * 512:(hb2 + 1) * 512], ps.ap[0:16, :], [ps], pw=qks, eng="dve")

        gps = pb(6, 0, 72)
        for i in range(nt):
            tp, c0 = tinfo(i)
            for kc in range(24):
                srcb, srcv = ((qT, qv), (kT, kv), (vT, vv))[kc // 8]
                E("pe", lambda e, i=i, kc=kc, tp=tp, c0=c0, srcv=srcv: e.matmul(
                    gps.ap[:tp, i * 8:(i + 1) * 8], lhsT=srcv[:, kc % 8, c0:c0 + tp], rhs=wif.ap[:, kc, :],
                    start=(kc == 0), stop=(kc == 23)), reads=[srcb, wif],
                  writes=[gps] if (i == 0 and kc == 0) else [], pwrites=[] if (i == 0 and kc == 0) else [gps])
        E("dve", lambda e: e.tensor_tensor(out=G.ap[:, :, 0:8], in0=gps.ap[:, 0:64].rearrange("p (i g) -> p g i", g=8),
                                           in1=bifb.ap[:, :, None].broadcast_to([128, 8, 8]), op=ALU.add),
          reads=[gps, bifb], pwrites=[G])
        if pp == 0:
            E("dve", lambda e: e.tensor_tensor(out=G.ap[0:16, :, 8], in0=gps.ap[0:16, 64:72], in1=bifb.ap[0:16, :], op=ALU.add),
              reads=[gps, bifb], pwrites=[G])
        def gate_math():
            f32v = lambda b: b.ap.rearrange("p h j -> p (h j)")
            E("act", lambda e: e.activation(out=E1.ap, in_=G.ap[:, 4:8, 0:8], func=AF.Exp, scale=-1.0), reads=[G], writes=[E1])
            E("act", lambda e: e.activation(out=LFN.ap, in_=E1.ap, func=AF.Ln, bias=onec.ap[:, 0:1]), reads=[E1, onec], writes=[LFN])
            pbn = pb(7, 0, 32); prB = pb(6, 64, 32)
            yield
            E("pe", lambda e: e.matmul(pbn.ap, lhsT=tri.ap, rhs=f32v(LFN), start=True, stop=True), reads=[tri, LFN], writes=[pbn])
            yield
            E("pe", lambda e: e.matmul(prB.ap[0:1, :], lhsT=onesf.ap[:, 0:1], rhs=f32v(LFN), start=True, stop=True),
              reads=[onesf, LFN], writes=[prB])
            E("act", lambda e: e.copy(out=f32v(BNs), in_=pbn.ap), reads=[pbn], writes=[BNs])
            E("act", lambda e: e.copy(out=rowB.ap, in_=prB.ap[0:1, :]), reads=[prB], writes=[rowB])
            E("dve", lambda e: e.tensor_tensor(out=Csb.ap, in0=G.ap[:, 0:4, 0:8], in1=pbn.ap.rearrange("p (h j) -> p h j", h=4), op=ALU.add),
              reads=[G, pbn], writes=[Csb])
            yield
            pct = pb(7, 128, 128)
            E("pe", lambda e: e.transpose(out=pct.ap[0:32, :], in_=f32v(Csb), identity=identf.ap), reads=[Csb, identf], writes=[pct])
            E("dve", lambda e: e.tensor_reduce(out=cmaxc.ap, in_=pct.ap[0:32, :], axis=AX.X, op=ALU.max), reads=[pct], writes=[cmaxc])
            yield
            prA = pb(6, 128, 32)
            E("pe", lambda e: e.transpose(out=prA.ap[0:1, :], in_=cmaxc.ap, identity=identf.ap[0:32, 0:32]),
              reads=[cmaxc, identf], writes=[prA])
            E("dve", lambda e: e.tensor_copy(out=rowA.ap, in_=prA.ap[0:1, :]), reads=[prA], writes=[rowA])
            for h in range(4):
                E("dve", lambda e, h=h: e.tensor_tensor_scan(out=MN.ap[0:1, h * 8:(h + 1) * 8], data0=rowA.ap[0:1, h * 8:(h + 1) * 8],
                                                             data1=rowB.ap[0:1, h * 8:(h + 1) * 8], initial=mcar.ap[0:1, h:h + 1],
                                                             op0=ALU.max, op1=ALU.subtract),
                  reads=[rowA, rowB, mcar], writes=[MN] if h == 0 else [], pwrites=[] if h == 0 else [MN])
            MN3 = MN.ap.rearrange("p (h j) -> p h j", h=4); MP3 = MPV.ap.rearrange("p (h j) -> p h j", h=4)
            E("dve", lambda e: e.tensor_copy(out=MP3[:, :, 0:1], in_=mcar.ap[:, :, None]), reads=[mcar], writes=[MPV])
            E("dve", lambda e: e.tensor_copy(out=MP3[:, :, 1:8], in_=MN3[:, :, 0:7]), reads=[MN], pwrites=[MPV])
            E("dve", lambda e: e.tensor_tensor(out=RG.ap[0:1, 0:32], in0=MN.ap, in1=rowB.ap, op=ALU.add), reads=[MN, rowB], writes=[RG])
            E("dve", lambda e: e.tensor_tensor(out=rtmp.ap, in0=MPV.ap, in1=RG.ap[0:1, 0:32], op=ALU.subtract),
              reads=[MPV, RG], writes=[rtmp])
            E("act", lambda e: e.activation(out=RG.ap[0:1, 32:64], in_=rtmp.ap, func=AF.Exp), reads=[rtmp], pwrites=[RG])
            E("dve", lambda e: e.tensor_copy(out=mcar.ap[:, :, None], in_=MN3[:, :, 7:8]), reads=[MN, MPV], writes=[mcar])
            if pp == 1:
                out_dmas.append(DMA("sp", mp_o, mcar.ap, reads=[mcar]))
            yield
            pbc = pb(7, 256, 64)
            E("pe", lambda e: e.matmul(pbc.ap, lhsT=onesf.ap[0:1, :], rhs=RG.ap[0:1, :], start=True, stop=True),
              reads=[onesf, RG], writes=[pbc])
            E("dve", lambda e: e.tensor_tensor(out=f32v(A1), in0=f32v(Csb), in1=pbc.ap[:, 0:32], op=ALU.subtract),
              reads=[Csb, pbc], writes=[A1])
            E("act", lambda e: e.activation(out=WS.ap, in_=A1.ap, func=AF.Exp, bias=nln16.ap[:, 0:1]), reads=[A1, nln16], writes=[WS])
            E("dve", lambda e: e.tensor_tensor(out=f32v(E1), in0=f32v(BNs), in1=pbc.ap[:, 0:32], op=ALU.subtract),
              reads=[BNs, pbc], writes=[E1])
            E("act", lambda e: e.activation(out=FL.ap, in_=E1.ap, func=AF.Exp), reads=[E1], writes=[FL])
            E("dve", lambda e: e.tensor_copy(out=f32v(GSb), in_=pbc.ap[:, 32:64]), reads=[pbc], writes=[GSb])

            if pp == 0:
                lis = G.ap[0:16, 0:4, 8]; lfs = G.ap[0:16, 4:8, 8]
                sgv = lambda a, b: sg.ap[0:16, a:b]
                E("act", lambda e: e.activation(out=sgv(0, 4), in_=lfs, func=AF.Exp, scale=-1.0), reads=[G], pwrites=[sg])
                E("act", lambda e: e.activation(out=sgv(4, 8), in_=sgv(0, 4), func=AF.Ln, bias=onec.ap[0:16, 0:1]),
                  reads=[sg, onec], pwrites=[sg])
                E("dve", lambda e: e.tensor_tensor(out=sgv(8, 12), in0=lis, in1=sgv(4, 8), op=ALU.add), reads=[G, sg], pwrites=[sg])
                E("dve", lambda e: e.tensor_tensor(out=sgv(12, 16), in0=sgv(8, 12), in1=m_s.ap, op=ALU.max), reads=[sg, m_s], pwrites=[sg])
                E("dve", lambda e: e.tensor_tensor(out=sgv(16, 20), in0=sgv(12, 16), in1=sgv(4, 8), op=ALU.subtract),
                  reads=[sg], pwrites=[sg])
                out_dmas.append(DMA("sp", ms_o, sgv(16, 20), reads=[sg]))
                E("dve", lambda e: e.tensor_tensor(out=sgv(20, 24), in0=sgv(8, 12), in1=sgv(12, 16), op=ALU.subtract),
                  reads=[sg], pwrites=[sg])
                E("act", lambda e: e.activation(out=S12.ap[:, 0:4], in_=sgv(20, 24), func=AF.Exp, bias=nln16.ap[0:16, 0:1]),
                  reads=[sg, nln16], pwrites=[S12])
                E("dve", lambda e: e.tensor_tensor(out=sgv(24, 28), in0=m_s.ap, in1=sgv(12, 16), op=ALU.subtract),
                  reads=[sg, m_s], pwrites=[sg])
                E("act", lambda e: e.activation(out=S12.ap[:, 4:8], in_=sgv(24, 28), func=AF.Exp), reads=[sg], pwrites=[S12])
                E("act", lambda e: e.activation(out=sgv(28, 32), in_=sgv(16, 20), func=AF.Exp, scale=-1.0), reads=[sg], pwrites=[sg])
                E("dve", lambda e: e.tensor_tensor(out=tmp_s.ap[0:16, :], in0=q_sf.ap[0:16, :], in1=k_sf.ap[0:16, :], op=ALU.mult),
                  reads=[q_sf, k_sf], writes=[tmp_s])
                E("dve", lambda e: e.tensor_reduce(out=sgv(32, 36), in_=tmp_s.ap[0:16, :].rearrange("p (h d) -> p h d", h=4),
                                                   axis=AX.X, op=ALU.add), reads=[tmp_s], pwrites=[sg])
                E("dve", lambda e: e.tensor_tensor(out=tmp_s.ap[0:16, :], in0=q_sf.ap[0:16, :], in1=n_sf.ap[0:16, :], op=ALU.mult),
                  reads=[q_sf, n_sf, sg], writes=[tmp_s])
                E("dve", lambda e: e.tensor_reduce(out=sgv(36, 40), in_=tmp_s.ap[0:16, :].rearrange("p (h d) -> p h d", h=4),
                                                   axis=AX.X, op=ALU.add), reads=[tmp_s], pwrites=[sg])
                E("dve", lambda e: e.tensor_tensor(out=S12.ap[:, 8:12], in0=S12.ap[:, 0:4], in1=sgv(32, 36), op=ALU.mult),
                  reads=[S12, sg], pwrites=[S12])
                E("dve", lambda e: e.tensor_tensor(out=sgv(40, 44), in0=S12.ap[:, 4:8], in1=sgv(36, 40), op=ALU.mult),
                  reads=[S12, sg], pwrites=[sg])
                E("dve", lambda e: e.tensor_tensor(out=sgv(44, 48), in0=sgv(40, 44), in1=S12.ap[:, 8:12], op=ALU.add),
                  reads=[S12, sg], pwrites=[sg])
                E("dve", lambda e: e.scalar_tensor_tensor(out=sgv(48, 52), in0=sgv(44, 48), scalar=-1.0, in1=sgv(44, 48),
                                                          op0=ALU.mult, op1=ALU.max), reads=[sg], pwrites=[sg])
                E("dve", lambda e: e.tensor_tensor(out=sgv(52, 56), in0=sgv(48, 52), in1=sgv(28, 32), op=ALU.max), reads=[sg], pwrites=[sg])
                E("dve", lambda e: e.reciprocal(out=rinv_s.ap, in_=sgv(52, 56)), reads=[sg], writes=[rinv_s])
                n3 = lambda b: b.ap[0:16, :].rearrange("p (h d) -> p h d", h=4)
                E("dve", lambda e: e.tensor_tensor(out=n3(n_sf), in0=n3(n_sf), in1=S12.ap[:, 4:8, None].broadcast_to([16, 4, 256]),
                                                   op=ALU.mult), reads=[n_sf, S12, tmp_s], writes=[n_sf])
                E("dve", lambda e: e.tensor_tensor(out=n3(tmp_s), in0=n3(k_sf), in1=S12.ap[:, 0:4, None].broadcast_to([16, 4, 256]),
                                                   op=ALU.mult), reads=[k_sf, S12], writes=[tmp_s])
                E("dve", lambda e: e.tensor_tensor(out=n_sf.ap[0:16, :], in0=n_sf.ap[0:16, :], in1=tmp_s.ap[0:16, :], op=ALU.add),
                  reads=[n_sf, tmp_s], writes=[n_sf])
                out_dmas.append(DMA("sp", ns_o, n_sf.ap[0:16, :], reads=[n_sf]))
                E("dve", lambda e: e.tensor_tensor(out=BD.ap, in0=S12.ap[:, None, :].broadcast_to([16, 16, 12]),
                                                   in1=identf.ap[0:16, 0:16, None].broadcast_to([16, 16, 12]), op=ALU.mult),
                  reads=[S12, identf], writes=[BD])
                pbd = pb(6, 192, 192)
                yield
                E("pe", lambda e: e.matmul(pbd.ap, lhsT=onesf.ap[0:16, :], rhs=BD.ap.rearrange("p b q -> p (b q)"),
                                           start=True, stop=True), reads=[onesf, BD], writes=[pbd])
                E("act", lambda e: e.copy(out=bcs.ap.rearrange("p b q -> p (b q)"), in_=pbd.ap), reads=[pbd], writes=[bcs])


            yield
        gm = gate_math()

        sigo = ar(1, 0, 8192); sgo = sigo.ap.rearrange("p (i f) -> p i f", i=8)
        for bi, blk in enumerate((8, 9)):
            wb = load_wblk(blk)
            for i in range(nt):
                tp, c0 = tinfo(i)
                ps = projA(lambda k, c0=c0, tp=tp: hTv[:, k, c0:c0 + tp], 8, lambda k, wb=wb: wb.ap[:, k, :], tp, [hgrp(c0)], [wb])
                if i < 8:
                    evac(sgo[:tp, i, bi * 512:(bi + 1) * 512], ps.ap[:tp, :], [ps], pw=sigo, eng="act", func=AF.Sigmoid)
                else:
                    evac(sgs.ap[:tp, bi * 512:(bi + 1) * 512], ps.ap[:tp, :], [ps], pw=sgs, eng="act", func=AF.Sigmoid)
                if i % 2 == 1:
                    next(gm, None)
        szb = ar(6, 0, 8 * TC); szv = V3(szb, 8)

        def sxs_szg(c):
            E("dve", lambda e: e.scalar_tensor_tensor(out=xcv[:, c, 0:NC_], in0=xcv[:, c, 0:NC_], scalar=skipT.ap[:, c:c + 1],
                                                      in1=szv[:, c, 0:NC_], op0=ALU.mult, op1=ALU.mult),
              reads=[xcT, skipT, szb], pwrites=[xcT], war=[xcT])
            E("dve", lambda e: e.tensor_scalar(out=szv[:, c, 0:NC_], in0=szv[:, c, 0:NC_], scalar1=gngT.ap[:, c:c + 1],
                                               scalar2=None, op0=ALU.mult), reads=[szb, gngT, xcT], pwrites=[szb])

        for bi, blk in enumerate((10, 11)):
            wb = load_wblk(blk)
            for cc in range(4):
                c = bi * 4 + cc

                def outf(ps, c0, n, c=c):
                    evac(szv[:, c, c0:c0 + n], ps.ap[:, 0:n], [ps], pw=szb, eng="act", func=AF.Silu)
                projB(pp, lambda k, cc=cc, wb=wb: wb.ap[:, k, cc * 128:(cc + 1) * 128], 8,
                      lambda k, c0, n: hTv[:, k, c0:c0 + n], outf, [wb], [], rb_fn=hgrp)
                if c >= 1:
                    sxs_szg(c - 1)
        sxs_szg(7)
        for _ in gm:
            pass

        ybT = ar(0, 0, 8 * TC); ybv = V3(ybT, 8)
        SKs = [pb(0, 0, 512), pb(1, 0, 512)]
        KTs = [pb(6, 0, 512)]
        NHs = [pb(2, 0, 512), pb(3, 0, 512)]
        HTs = [pb(7, 0, 512)]
        Us = [pb(4, 0, 512), pb(5, 0, 512)]
        o = 0
        PTbs = [fs(o + i * 64, 64, BF16) for i in range(2)]; o += 128
        KWbs = [fs(o + i * 128, 128, BF16) for i in range(2)]; o += 256
        ABbs = [fs(o + i * 257, 257, BF16) for i in range(3)]; o += 771
        HBfs = [fs(o + i * 256, 256) for i in range(3)]; o += 768
        HBNs = [fs(o + i * 128, 128, BF16) for i in range(2)]; o += 256
        Yts = [fs(o + i * 256, 256) for i in range(2)]; o += 512
        rvs = [fs(o + i * 16, 16) for i in range(3)]; o += 48

        def post1a(tp, num_b, den_ap, fl_ap, fl_b, rv):
            E("act", lambda e: e.activation(out=rv.ap[:tp, 0:1], in_=den_ap, func=AF.Abs), reads=[num_b], writes=[rv])
            E("dve", lambda e: e.tensor_tensor(out=rv.ap[:tp, 2:3], in0=rv.ap[:tp, 0:1], in1=fl_ap, op=ALU.max),
              reads=[rv, fl_b], pwrites=[rv])
            E("pool", lambda e: e.tensor_tensor(out=rv.ap[:tp, 1:2], in0=rv.ap[:tp, 2:3], in1=monec.ap[:tp, 0:1], op=ALU.pow),
              reads=[rv, monec], pwrites=[rv])

        def post(it, tp, num_ap, num_b, den_ap, fl_ap, fl_b, sig_ap, h, c0, rinv_ap=None, rinv_b=None, sig_b=None, part=0):
            sig_b = sig_b or sigo
            rv = rvs[it % 3]; HBf = HBfs[it % 3]; HBN = HBNs[it % 2]; Yt = Yts[it % 2]
            hbtb = HTs[0]; hbt_ap = hbtb.ap[:, 0:128].bitcast(BF16)
            if part == 4:
                post1a(tp, num_b, den_ap, fl_ap, fl_b, rv)
            if part == 1:
                rinv_ap = rv.ap[:tp, 1:2]; rinv_b = rv
            if part in (0, 1):
                post1(tp, num_ap, num_b, den_ap, fl_ap, fl_b, sig_ap, rinv_ap, rinv_b, sig_b, rv, HBf, part)
            if part in (0, 2):
                post2(tp, rv, HBf, HBN, hbtb, hbt_ap)
            if part in (0, 3):
                post3(tp, h, c0, hbtb, hbt_ap, Yt)

        def post1(tp, num_ap, num_b, den_ap, fl_ap, fl_b, sig_ap, rinv_ap, rinv_b, sig_b, rv, HBf, part):
            E("dve", lambda e: e.scalar_tensor_tensor(out=HBf.ap[:tp], in0=num_ap, scalar=rinv_ap, in1=sig_ap,
                                                      op0=ALU.mult, op1=ALU.mult), reads=[num_b, rinv_b, sig_b], writes=[HBf])
            if part == 0:
                E("dve", lambda e: e.bn_stats(out=rv.ap[:tp, 4:10], in_=HBf.ap[:tp]), reads=[HBf], writes=[rv])
            else:
                E("dve", lambda e: e.bn_stats(out=rv.ap[:tp, 4:10], in_=HBf.ap[:tp]), reads=[HBf], pwrites=[rv])
            E("dve", lambda e: e.bn_aggr(out=rv.ap[:tp, 10:12], in_=rv.ap[:tp, 4:10]), reads=[rv], pwrites=[rv])
            E("pool", lambda e: e.tensor_tensor(out=rv.ap[:tp, 12:13], in0=rv.ap[:tp, 11:12], in1=epsc.ap[:tp, 0:1], op=ALU.add),
              reads=[rv, epsc], pwrites=[rv])
            E("pool", lambda e: e.tensor_tensor(out=rv.ap[:tp, 13:14], in0=rv.ap[:tp, 12:13], in1=mhalf.ap[:tp, 0:1], op=ALU.pow),
              reads=[rv, mhalf], pwrites=[rv])

        def post2(tp, rv, HBf, HBN, hbtb, hbt_ap):
            E("dve", lambda e: e.tensor_scalar(out=HBN.ap[:tp], in0=HBf.ap[:tp], scalar1=rv.ap[:tp, 10:11], scalar2=rv.ap[:tp, 13:14],
                                               op0=ALU.subtract, op1=ALU.mult), reads=[HBf, rv], writes=[HBN])
            for ec in range(2):
                E("pe", lambda e, ec=ec: e.transpose(out=hbt_ap[:, ec * 128:ec * 128 + tp], in_=HBN.ap[:tp, ec * 128:(ec + 1) * 128],
                                                     identity=ident.ap[:tp, :tp]), reads=[HBN, ident],
                  writes=[hbtb] if ec == 0 else [], pwrites=[] if ec == 0 else [hbtb])

        def post3(tp, h, c0, hbtb, hbt_ap, Yt):
            E("dve", lambda e: e.tensor_tensor(out=Yt.ap.rearrange("p (a t) -> p a t", a=2)[:, :, 0:tp],
                                               in0=hbt_ap.rearrange("p (a t) -> p a t", a=2)[:, :, 0:tp],
                                               in1=szv[:, 2 * h:2 * h + 2, c0:c0 + tp], op=ALU.mult), reads=[hbtb, szb], writes=[Yt])
            E("pool", lambda e: e.tensor_tensor(out=ybv[:, 2 * h:2 * h + 2, c0:c0 + tp],
                                                in0=Yt.ap.rearrange("p (a t) -> p a t", a=2)[:, :, 0:tp],
                                                in1=xcv[:, 2 * h:2 * h + 2, c0:c0 + tp], op=ALU.add), reads=[Yt, xcT], pwrites=[ybT])

        def stageA(it, j, h, part):
            jc = j * 128
            SK = SKs[it % 2]; PTb = PTbs[it % 2]; KWb = KWbs[it % 2]; ABb = ABbs[it % 3]
            KT = KTs[0]
            st_ap = SK.ap[:, 0:128]; ktr_ap = KT.ap[:, 0:128].bitcast(BF16); NU_ap = SK.ap[:, 256:258]
            NH = NHs[it % 2]; num_ap = NH.ap[:, 0:257]; U = Us[it % 2]
            wcol = WS.ap[:, h, j:j + 1]; gcol = GSb.ap[:, h, j:j + 1]
            if part == 2:
                return stageA2(it, j, h, jc, SK, PTb, KWb, ABb, KT, st_ap, ktr_ap, NU_ap, NH, num_ap, U, wcol, gcol)
            for ec in range(2):
                E("pe", lambda e, ec=ec: e.matmul(st_ap, lhsT=kv[:, 2 * h + ec, jc:jc + 128], rhs=qv[:, 2 * h + ec, jc:jc + 128],
                                                  start=(ec == 0), stop=(ec == 1)), reads=[kT, qT],
                  writes=[SK] if ec == 0 else [], pwrites=[] if ec == 0 else [SK])
            for dc in range(2):
                E("pe", lambda e, dc=dc: e.transpose(out=ktr_ap[:, dc * 128:(dc + 1) * 128], in_=kv[:, 2 * h + dc, jc:jc + 128],
                                                     identity=ident.ap), reads=[kT, ident],
                  writes=[KT] if dc == 0 else [], pwrites=[] if dc == 0 else [KT])

        def stageA2(it, j, h, jc, SK, PTb, KWb, ABb, KT, st_ap, ktr_ap, NU_ap, NH, num_ap, U, wcol, gcol):
            E("dve", lambda e: e.scalar_tensor_tensor(out=PTb.ap, in0=st_ap, scalar=wcol, in1=maskT.ap, op0=ALU.mult, op1=ALU.mult),
              reads=[SK, WS, maskT], writes=[PTb])
            E("act", lambda e: e.activation(out=KWb.ap, in_=ktr_ap, func=AF.Copy, scale=wcol), reads=[KT, WS], writes=[KWb])
            ab3 = ABb.ap.rearrange("p (a e) -> p a e", a=2)
            E("pe", lambda e: e.matmul(num_ap, lhsT=PTb.ap, rhs=vav[:, j, h, :], start=True, stop=False),
              reads=[PTb, vaug], writes=[NH])
            for dc in range(2):
                E("pe", lambda e, dc=dc: e.matmul(num_ap, lhsT=qv[:, 2 * h + dc, jc:jc + 128], rhs=ab3[:, dc, :],
                                                  start=False, stop=(dc == 1)), reads=[qT, ABb], pwrites=[NH])
            for dc in range(2):
                E("pe", lambda e, dc=dc: e.matmul(U.ap[:, dc * 256:(dc + 1) * 256], lhsT=KWb.ap[:, dc * 128:(dc + 1) * 128],
                                                  rhs=vav[:, j, h, 0:256], start=True, stop=True), reads=[KWb, vaug],
                  writes=[U] if dc == 0 else [], pwrites=[] if dc == 0 else [U])
            for dc in range(2):
                E("pe", lambda e, dc=dc: e.matmul(NU_ap[:, dc:dc + 1], lhsT=KWb.ap[:, dc * 128:(dc + 1) * 128],
                                                  rhs=onesb.ap[:, 0:1], start=True, stop=True), reads=[KWb, onesb], pwrites=[SK])

        def stageAb(it, j, h):
            ABb = ABbs[it % 3]
            gcol = GSb.ap[:, h, j:j + 1]
            ab3 = ABb.ap.rearrange("p (a e) -> p a e", a=2)
            E("act", lambda e: e.activation(out=ab3[:, :, 0:256], in_=CT.ap[:, h, :, :], func=AF.Copy, scale=gcol),
              reads=[CT, GSb], writes=[ABb])
            E("act", lambda e: e.activation(out=ab3[:, :, 256], in_=nT.ap[:, h, :], func=AF.Copy, scale=gcol),
              reads=[nT, GSb], pwrites=[ABb])

        def stageU(it, j, h):
            SK = SKs[it % 2]; ABb = ABbs[it % 3]; U = Us[it % 2]
            NU_ap = SK.ap[:, 256:258]
            gcol = GSb.ap[:, h, j:j + 1]
            E("dve", lambda e: e.scalar_tensor_tensor(out=CT.ap[:, h, :, :].rearrange("p a e -> p (a e)"),
                                                      in0=CT.ap[:, h, :, :].rearrange("p a e -> p (a e)"), scalar=gcol, in1=U.ap,
                                                      op0=ALU.mult, op1=ALU.add), reads=[CT, GSb, U, ABb], pwrites=[CT])
            E("dve", lambda e: e.scalar_tensor_tensor(out=nT.ap[:, h, :], in0=nT.ap[:, h, :], scalar=gcol, in1=NU_ap,
                                                      op0=ALU.mult, op1=ALU.add), reads=[nT, GSb, SK, ABb], pwrites=[nT])

        def stageB(it, j, h, part):
            num = NHs[it % 2]
            post(it, 128, num.ap[:, 0:256], num, num.ap[:, 256:257], FL.ap[:, h, j:j + 1], FL,
                 sgo[:, j, h * 256:(h + 1) * 256], h, j * 128, part=part)

        wvb = [load_wblk(2), load_wblk(3)]
        its = [(j, h) for j in range(8) for h in range(4)]
        nit = len(its)
        stageAb(0, *its[0])
        for t in range(nit + 3):
            if t < nit:
                stageA(t, *its[t], part=1)
            if 0 <= t - 3 < nit:
                stageB(t - 3, *its[t - 3], part=3)
            if 0 <= t - 2 < nit:
                stageB(t - 2, *its[t - 2], part=2)
            if t < nit:
                stageA(t, *its[t], part=2)
            if 0 <= t - 1 < nit:
                stageB(t - 1, *its[t - 1], part=1)
            if t < nit:
                stageU(t, *its[t])
                stageB(t, *its[t], part=4)
            if t + 1 < nit:
                stageAb(t + 1, *its[t + 1])

        hT2 = ar(1, 0, 8 * TC)
        h2v = V3(hT2, 8)
        DMA("sp", h2v[:, :, 0:NC_], hscr.rearrange("p (c t) -> p c t", c=8)[:, :, 0:NC_], reads=[hscrb], writes=[hT2])
        if pp == 1:
            Cst = ar(3, 0, 4096, F32)
            csv = Cst.ap.rearrange("p (h a d) -> p h a d", h=4, a=2)
            for h in range(4):
                for eh in range(2):
                    pt_ = pb((h * 2 + eh) % 2 + 2, 0, 256)
                    for dc in range(2):
                        E("pe", lambda e, h=h, eh=eh, dc=dc, pt_=pt_: e.transpose(
                            out=pt_.ap[:, dc * 128:(dc + 1) * 128], in_=CT.ap[:, h, dc, eh * 128:(eh + 1) * 128],
                            identity=identf.ap), reads=[CT, identf], writes=[pt_] if dc == 0 else [], pwrites=[] if dc == 0 else [pt_])
                    evac(csv[:, h, eh, :], pt_.ap, [pt_], pw=Cst)
            out_dmas.append(DMA("sp", Cp.rearrange("h (a p) d -> p h a d", p=128), csv, reads=[Cst]))
            out_dmas.append(DMA("sp", np_o.rearrange("h (a p) -> p h a", p=128), nT.ap, reads=[nT], allow_slow_non_contiguous=True))

        if pp == 0:
            NCB = 12
            Cins = [ar(3 + i // 8, (i % 8) * 1024, 1024, F32) for i in range(NCB)]
            qkb = fs(2752, 1024, BF16)
            junkc = fs(3776, 128, BF16)
            E("dve", lambda e: e.tensor_tensor(
                out=wvT.ap, in0=vTs.ap.rearrange("p (h a) b -> p h a b", h=4),
                in1=bcs.ap[:, :, 0:4].rearrange("p b h -> p h b")[:, :, None, :].broadcast_to([128, 4, 2, 16]),
                op=ALU.mult), reads=[vTs, bcs], writes=[wvT])
            units = [(b, h) for b in range(NS) for h in range(4)]
            pqs = {}

            def c_in(u):
                b, h = units[u]
                Cin = Cins[u % NCB]
                c3 = Cin.ap.rearrange("p (a d) -> p a d", a=2)
                DMA("sp", c3, C_in[b, h].rearrange("(a p) d -> p a d", p=128), writes=[Cin], key=rkey("ci", u, NCB))

            def c_nop(u):
                pass

            def c_s0(u):
                b, h = units[u]
                if h == 0:
                    E("pool", lambda e: e.tensor_tensor(out=qkb.ap[0:16, :], in0=qks.ap.rearrange("p a f -> p (a f)"),
                                                        in1=identf.ap[0:16, b:b + 1].broadcast_to([16, 2048]), op=ALU.mult),
                      reads=[qks, identf], writes=[qkb])
                pq = pb(u % 8, 0, 512)
                pqs[u] = pq
                E("pe", lambda e: e.matmul(
                    pq.ap.rearrange("p (a d) -> p a d", a=2), lhsT=onesb.ap[0:16, :],
                    rhs=qkb.ap[0:16, :].rearrange("p (a f) -> p a f", a=2)[:, :, h * 256:(h + 1) * 256], start=True, stop=True),
                  reads=[qkb, onesb], writes=[pq])

            def c_s1(u):
                b, h = units[u]
                Cin = Cins[u % NCB]; pq = pqs[u]
                c3 = Cin.ap.rearrange("p (a d) -> p a d", a=2)
                for a in range(2):
                    E("dve", lambda e, a=a: e.scalar_tensor_tensor(
                        out=junkc.ap, in0=c3[:, a, :], scalar=1.0, in1=pq.ap[:, 0:256], op0=ALU.mult, op1=ALU.mult,
                        accum_out=CqT.ap[:, h, a, b:b + 1]), reads=[Cin, pq], writes=[junkc], pwrites=[CqT])

            def c_s2(u):
                b, h = units[u]
                Cin = Cins[u % NCB]
                E("act", lambda e: e.activation(out=Cin.ap, in_=Cin.ap, func=AF.Copy, scale=bcs.ap[:, b, 4 + h:5 + h]),
                  reads=[Cin, bcs], writes=[Cin])

            def c_s3(u):
                b, h = units[u]
                Cin = Cins[u % NCB]; pq = pqs[u]
                c3 = Cin.ap.rearrange("p (a d) -> p a d", a=2)
                for a in range(2):
                    E("dve", lambda e, a=a: e.scalar_tensor_tensor(
                        out=c3[:, a, :], in0=pq.ap[:, 256:512], scalar=wvT.ap[:, h, a, b:b + 1], in1=c3[:, a, :],
                        op0=ALU.mult, op1=ALU.add), reads=[Cin, pq, wvT], writes=[Cin])
                out_dmas.append(DMA("sp", Cs[b, h].rearrange("(a p) d -> p a d", p=128), c3, reads=[Cin], key=rkey("co", u, NCB)))

            pipeline(len(units), [c_in, c_nop, c_nop, c_nop, c_nop, c_nop, c_s0, c_s1, c_s2, c_s3])
            bq = lambda q: bcs.ap[:, :, q * 4:(q + 1) * 4].rearrange("p b h -> p h b")[:, :, None, :].broadcast_to([128, 4, 2, 16])
            E("dve", lambda e: e.tensor_tensor(out=numTs.ap, in0=vTs.ap.rearrange("p (h a) b -> p h a b", h=4), in1=bq(2), op=ALU.mult),
              reads=[vTs, bcs], writes=[numTs])
            E("dve", lambda e: e.tensor_tensor(out=CqT.ap, in0=CqT.ap, in1=bq(1), op=ALU.mult), reads=[CqT, bcs], writes=[CqT])
            E("dve", lambda e: e.tensor_tensor(out=numTs.ap, in0=numTs.ap, in1=CqT.ap, op=ALU.add), reads=[numTs, CqT], writes=[numTs])
            for hb2 in range(2):
                pn = pb(4 + hb2, 0, 512)
                for hh in range(2):
                    h = hb2 * 2 + hh
                    for a in range(2):
                        E("pe", lambda e, h=h, hh=hh, a=a, pn=pn: e.transpose(
                            out=pn.ap[0:16, hh * 256 + a * 128:hh * 256 + (a + 1) * 128], in_=numTs.ap[:, h, a, :], identity=identf.ap),
                          reads=[numTs, identf], writes=[pn] if (hh == 0 and a == 0) else [], pwrites=[] if (hh == 0 and a == 0) else [pn])
                for hh in range(2):
                    h = hb2 * 2 + hh
                    post(h, 16, pn.ap[0:16, hh * 256:(hh + 1) * 256], pn, None, None, None,
                         sgs.ap[0:16, h * 256:(h + 1) * 256], h, TH, rinv_ap=rinv_s.ap[:, h:h + 1], rinv_b=rinv_s, sig_b=sgs)

        lg = cs_(1); lb = cs_(2); bspb = cs_(0)
        DMA("sp", lg.ap, lnv_g.partition_broadcast(128), writes=[lg])
        DMA("sp", lb.ap, lnv_b.partition_broadcast(128), writes=[lb])
        VG = ar(4, 0, 16384, F32); vgv = VG.ap.rearrange("p (i f) -> p i f", i=8)
        vn = ar(2, 0, 8192); vnv = vn.ap.rearrange("p (i f) -> p i f", i=8)
        vsb = fs(2048, 1024)
        yaT = ar(3, 0, 8 * TC); yav = V3(yaT, 8)
        SQs = [fs(3072, 512), fs(3584, 512)]
        for bi, blk in enumerate((2, 3)):
            wb = wvb[bi]
            for i in range(nt):
                tp, c0 = tinfo(i)
                ps = projA(lambda k, c0=c0, tp=tp: h2v[:, k, c0:c0 + tp], 8, lambda k, wb=wb: wb.ap[:, k, :], tp, [hT2], [wb])
                vg_ap = vgv[:tp, i, bi * 512:(bi + 1) * 512] if i < 8 else vsb.ap[:tp, bi * 512:(bi + 1) * 512]
                VGb = VG if i < 8 else vsb
                evac(vg_ap, ps.ap[:tp, :], [ps], pw=VGb, eng="act", func=AF.Gelu)
                SQ = SQs[i % 2]
                E("act", lambda e, vg_ap=vg_ap, SQ=SQ, tp=tp: e.activation(out=SQ.ap[:tp], in_=vg_ap, func=AF.Square),
                  reads=[VGb], writes=[SQ])
                E("dve", lambda e, vg_ap=vg_ap, tp=tp, i=i, bi=bi: e.tensor_reduce(
                    out=S1.ap[:tp, i, bi * 4:(bi + 1) * 4], in_=vg_ap.rearrange("p (h d) -> p h d", h=4), axis=AX.X, op=ALU.add),
                  reads=[VGb], pwrites=[S1])
                E("dve", lambda e, SQ=SQ, tp=tp, i=i, bi=bi: e.tensor_reduce(
                    out=S2.ap[:tp, i, bi * 4:(bi + 1) * 4], in_=SQ.ap[:tp].rearrange("p (h d) -> p h d", h=4), axis=AX.X, op=ALU.add),
                  reads=[SQ], pwrites=[S2])
        for bi, blk in enumerate((0, 1)):
            wb = load_wblk(blk)
            for cc in range(4):
                c = bi * 4 + cc

                def outf(ps, c0, n, c=c):
                    evac(yav[:, c, c0:c0 + n], ps.ap[:, 0:n], [ps], pw=yaT, eng="act", func=AF.Gelu)
                projB(pp, lambda k, cc=cc, wb=wb: wb.ap[:, k, cc * 128:(cc + 1) * 128], 8,
                      lambda k, c0, n: h2v[:, k, c0:c0 + n], outf, [wb], [hT2])
        nst = nt * 8
        fl2 = lambda b: b.ap.rearrange("p i g -> p (i g)")[:, 0:nst]
        cb128 = c128.ap[:, 0:1].broadcast_to([128, nst])
        E("pool", lambda e: e.tensor_tensor(out=fl2(MEAN), in0=fl2(S1), in1=cb128, op=ALU.mult), reads=[S1, c128], writes=[MEAN])
        E("pool", lambda e: e.tensor_tensor(out=fl2(S1), in0=fl2(MEAN), in1=fl2(MEAN), op=ALU.mult), reads=[MEAN], writes=[S1])
        E("pool", lambda e: e.tensor_tensor(out=fl2(S2), in0=fl2(S2), in1=cb128, op=ALU.mult), reads=[S2, c128], writes=[S2])
        E("pool", lambda e: e.tensor_tensor(out=fl2(S2), in0=fl2(S2), in1=fl2(S1), op=ALU.subtract), reads=[S2, S1], writes=[S2])
        E("pool", lambda e: e.tensor_tensor(out=fl2(S2), in0=fl2(S2), in1=epsc.ap[:, 0:1].broadcast_to([128, nst]), op=ALU.add),
          reads=[S2, epsc], writes=[S2])
        E("pool", lambda e: e.tensor_tensor(out=fl2(RSTD), in0=fl2(S2), in1=mhalf.ap[:, 0:1].broadcast_to([128, nst]), op=ALU.pow),
          reads=[S2, mhalf], writes=[RSTD])
        wza = [load_wblk(4), load_wblk(5)]
        NTs = [fs(0, 1024), fs(1024, 1024)]
        vs_f = vsb
        for i in range(nt):
            tp, c0 = tinfo(i)
            NT_ = NTs[i % 2]
            n3_ = NT_.ap[:tp].rearrange("p (h d) -> p h d", h=8)
            vsrc = vgv[:tp, i, :] if i < 8 else vsb.ap[:tp, :]
            E("dve", lambda e, i=i, tp=tp, n3_=n3_, vsrc=vsrc: e.tensor_tensor(
                out=n3_, in0=vsrc.rearrange("p (h d) -> p h d", h=8),
                in1=MEAN.ap[:tp, i, :, None].broadcast_to([tp, 8, 128]), op=ALU.subtract), reads=[VG if i < 8 else vsb, MEAN], writes=[NT_])
            if i < 8:
                E("dve", lambda e, i=i, tp=tp, n3_=n3_: e.tensor_tensor(
                    out=vnv[:tp, i, :].rearrange("p (h d) -> p h d", h=8), in0=n3_,
                    in1=RSTD.ap[:tp, i, :, None].broadcast_to([tp, 8, 128]), op=ALU.mult), reads=[NT_, RSTD], pwrites=[vn])
            else:
                E("dve", lambda e, i=i, tp=tp, n3_=n3_: e.tensor_tensor(
                    out=n3_, in0=n3_, in1=RSTD.ap[:tp, i, :, None].broadcast_to([tp, 8, 128]), op=ALU.mult), reads=[NT_, RSTD], writes=[NT_])
                E("pool", lambda e, tp=tp, NT_=NT_: e.tensor_tensor(out=NT_.ap[:tp], in0=NT_.ap[:tp], in1=lg.ap[:tp], op=ALU.mult),
                  reads=[NT_, lg], writes=[NT_])
                E("pool", lambda e, tp=tp, NT_=NT_: e.tensor_tensor(out=vs_f.ap[:tp], in0=NT_.ap[:tp], in1=lb.ap[:tp], op=ALU.add),
                  reads=[NT_, lb], writes=[vs_f])
                E("pool", lambda e, i=i, tp=tp: e.tensor_copy(out=vns.ap[:tp, :], in_=vs_f.ap[:tp]), reads=[vs_f], writes=[vns])
                out_dmas.append(DMA("sp", vs_o, vs_f.ap[0:16, :], reads=[vs_f]))
        wo = ar(4, 0, 16 * 1024); wov = wo.ap.rearrange("p (k n) -> p k n", k=16)
        DMA("pool", wov, w_out.rearrange("(k p) n -> p k n", p=128), writes=[wo])
        wg = ar(6, 0, 8 * 1024); wgv = wg.ap.rearrange("p (k n) -> p k n", k=8)
        DMA("pool", wgv, w_pg.rearrange("(k p) n -> p k n", p=128), writes=[wg])
        DMA("sp", bspb.ap, b_sp.rearrange("h t -> (h t)").partition_broadcast(128), writes=[bspb])
        SZs = [fs(3072, 512), fs(3584, 512)]
        T1s = [fs(0, 512), fs(512, 512)]
        for hf in range(2):
            prs = pb(4 + hf, 0, 512)
            E("pe", lambda e, hf=hf, prs=prs: e.matmul(prs.ap, lhsT=onesb.ap, rhs=WT.ap.rearrange("p h t -> p (h t)")[:, hf * 512:(hf + 1) * 512],
                                                   start=True, stop=True), reads=[onesb, WT], writes=[prs])
            for cc in range(4):
                c = hf * 4 + cc
                E("dve", lambda e, c=c, cc=cc, prs=prs: e.scalar_tensor_tensor(
                    out=bspb.ap[:, c * 128:(c + 1) * 128], in0=prs.ap[:, cc * 128:(cc + 1) * 128], scalar=lbT.ap[:, c:c + 1],
                    in1=bspb.ap[:, c * 128:(c + 1) * 128], op0=ALU.mult, op1=ALU.add), reads=[prs, lbT, bspb], writes=[bspb])
        gi = 0
        for bi, blk in enumerate((4, 5)):
            wb = wza[bi]
            for cc in range(4):
                c = bi * 4 + cc
                for (c0, n) in colgroups(pp):
                    SZ = SZs[gi % 2]; T1 = T1s[gi % 2]; gi += 1
                    zps = pb(psrot[0] % 4, 0, 512); psrot[0] += 1
                    for k in range(8):
                        E("pe", lambda e, k=k, c0=c0, n=n, zps=zps, cc=cc, wb=wb: e.matmul(
                            zps.ap[:, 0:n], lhsT=wb.ap[:, k, cc * 128:(cc + 1) * 128], rhs=h2v[:, k, c0:c0 + n],
                            start=(k == 0), stop=(k == 7)), reads=[wb, hT2], writes=[zps] if k == 0 else [], pwrites=[] if k == 0 else [zps])
                    evac(SZ.ap[:, 0:n], zps.ap[:, 0:n], [zps], w=SZ, eng="act", func=AF.Silu)
                    sps = pb(4 + gi % 2, 0, 512)
                    if n == 512:
                        for jj in range(4):
                            j = c0 // 128 + jj
                            E("pe", lambda e, jj=jj, j=j, c=c, sps=sps: e.matmul(
                                sps.ap[:, jj * 128:(jj + 1) * 128], lhsT=vnv[:, j, c * 128:(c + 1) * 128], rhs=WT.ap[:, c, :],
                                start=True, stop=True), reads=[vn, WT], writes=[sps] if jj == 0 else [], pwrites=[] if jj == 0 else [sps])
                        E("dve", lambda e, sps=sps, T1=T1, c=c: e.scalar_tensor_tensor(
                            out=T1.ap.rearrange("p (j t) -> p j t", j=4), in0=sps.ap.rearrange("p (j t) -> p j t", j=4),
                            scalar=lgT.ap[:, c:c + 1],
                            in1=bspb.ap[:, None, c * 128:(c + 1) * 128].broadcast_to([128, 4, 128]), op0=ALU.mult, op1=ALU.add),
                          reads=[sps, bspb, lgT], writes=[T1])
                    else:
                        E("pe", lambda e, c=c, sps=sps: e.matmul(sps.ap[:, 0:16], lhsT=vns.ap[0:16, c * 128:(c + 1) * 128],
                                                                 rhs=Wdiag.ap[0:16, c, :], start=True, stop=True),
                          reads=[vns, Wdiag], writes=[sps])
                        E("dve", lambda e, sps=sps, T1=T1, c=c: e.tensor_scalar(out=T1.ap[:, 0:16], in0=sps.ap[:, 0:16],
                                                                                scalar1=bsp0.ap[:, c:c + 1], scalar2=None, op0=ALU.add),
                          reads=[sps, bsp0], writes=[T1])
                    E("dve", lambda e, c=c, c0=c0, n=n, SZ=SZ: e.tensor_tensor(out=yav[:, c, c0:c0 + n], in0=yav[:, c, c0:c0 + n],
                                                                               in1=SZ.ap[:, 0:n], op=ALU.mult),
                      reads=[yaT, SZ], pwrites=[yaT])
                    E("dve", lambda e, c=c, c0=c0, n=n, T1=T1: e.tensor_tensor(out=yav[:, c, c0:c0 + n], in0=yav[:, c, c0:c0 + n],
                                                                               in1=T1.ap[:, 0:n], op=ALU.mult),
                      reads=[yaT, T1], pwrites=[yaT])

        wp = ar(2, 0, 2 * 1024); wpv = wp.ap.rearrange("p (k n) -> p k n", k=2)
        DMA("pool", wpv, w_pp.rearrange("(k p) n -> p k n", p=128), writes=[wp])
        bpg = cs_(0); plg = cs_(1); fng = cs_(2)
        DMA("sp", bpg.ap, b_pg.partition_broadcast(128), writes=[bpg])
        DMA("sp", plg.ap, ple_g.partition_broadcast(128), writes=[plg])
        DMA("sp", fng.ap, fin_g.partition_broadcast(128), writes=[fng])
        xts = [fs(0, 1024), fs(1024, 1024)]
        X1s = [fs(2048, 1024), fs(3072, 1024)]
        pts = [ar(1, i * 512, 512, F32) for i in range(2)]
        X1b = ar(1, 1024, 1024); X1T = ar(1, 2048, 1024); ptb = ar(1, 3072, 256); PT2 = ar(1, 3328, 256)
        Gs = ar(1, 3584, 2048, F32); Ef = ar(1, 5632, 2048, F32)
        Yf = ar(2, 2048, 2048, F32); X2 = ar(2, 4096, 2048, F32); Tf = ar(2, 6144, 2048, F32)
        ptmp5 = [ptmps[0], ptmps[1]]
        PS = {}

        def f_L(i):
            tp, c0 = tinfo(i)
            xt = xts[i % 2]; pt = pts[i % 2]
            DMA("sp", xt.ap[:tp], xp[t0 + i * 128:t0 + (i + 1) * 128, :] if i < 8 else xs, writes=[xt], key=rkey("x5", i, 2))
            DMA("sp", pt.ap[:tp], pp_[t0 + i * 128:t0 + (i + 1) * 128, :] if i < 8 else psm, writes=[pt], key=rkey("p5", i, 2))

        def f_O(i, nbs=(0, 1)):
            tp, c0 = tinfo(i)
            if ("o", i) not in PS:
                PS[("o", i)] = [pb(0, 0, 512), pb(1, 0, 512)]
            pso = PS[("o", i)]
            for nb in nbs:
                for kc in range(16):
                    ysrc, yb_ = (yav, yaT) if kc < 8 else (ybv, ybT)
                    E("pe", lambda e, nb=nb, kc=kc, ysrc=ysrc: e.matmul(
                        pso[nb].ap[:tp, :], lhsT=ysrc[:, kc % 8, c0:c0 + tp], rhs=wov[:, kc, nb * 512:(nb + 1) * 512],
                        start=(kc == 0), stop=(kc == 15)), reads=[yb_, wo], writes=[pso[nb]] if kc == 0 else [],
                      pwrites=[] if kc == 0 else [pso[nb]])

        def f_X(i, nb):
            tp, c0 = tinfo(i)
            xt = xts[i % 2]; X1 = X1s[i % 2]; pso = PS[("o", i)]
            E("dve", lambda e: e.tensor_tensor(
                out=X1b.ap[:tp, nb * 512:(nb + 1) * 512], in0=pso[nb].ap[:tp, :], in1=xt.ap[:tp, nb * 512:(nb + 1) * 512], op=ALU.add),
              reads=[pso[nb], xt], writes=[X1b] if nb == 0 else [], pwrites=[] if nb == 0 else [X1b])
            E("dve", lambda e: e.tensor_tensor(
                out=X1.ap[:tp, nb * 512:(nb + 1) * 512], in0=pso[nb].ap[:tp, :], in1=xt.ap[:tp, nb * 512:(nb + 1) * 512], op=ALU.add),
              reads=[pso[nb], xt], writes=[X1] if nb == 0 else [], pwrites=[] if nb == 0 else [X1])

        def f_P(i):
            tp, c0 = tinfo(i)
            pt = pts[i % 2]
            E("dve", lambda e: e.tensor_copy(out=ptb.ap[:tp], in_=pt.ap[:tp]), reads=[pt], writes=[ptb])
            ppt = pb(7, 0, 128, BF16)
            for k in range(2):
                E("pe", lambda e, k=k: e.transpose(out=ppt.ap[:, k * 128:k * 128 + tp], in_=ptb.ap[:tp, k * 128:(k + 1) * 128],
                                                   identity=ident.ap[:tp, :tp]), reads=[ptb, ident],
                  writes=[ppt] if k == 0 else [], pwrites=[] if k == 0 else [ppt])
            evac(PT2.ap, ppt.ap, [ppt], w=PT2, eng="dve")
            pse = [pb(4, 0, 512), pb(5, 0, 512)]
            PS[("e", i)] = pse
            for nb in range(2):
                for k in range(2):
                    E("pe", lambda e, nb=nb, k=k: e.matmul(
                        pse[nb].ap[:tp, :], lhsT=PT2.ap[:, k * 128:k * 128 + tp], rhs=wpv[:, k, nb * 512:(nb + 1) * 512],
                        start=(k == 0), stop=(k == 1)), reads=[PT2, wp], writes=[pse[nb]] if k == 0 else [], pwrites=[] if k == 0 else [pse[nb]])

            for nb in range(2):
                E("act", lambda e, nb=nb: e.activation(
                    out=X2.ap[:tp, nb * 512:(nb + 1) * 512], in_=pse[nb].ap[:tp, :], func=AF.Square,
                    accum_out=stat2.ap[:tp, 2 * i + nb:2 * i + nb + 1]), reads=[pse[nb]],
                  pwrites=[stat2] if nb == 0 else [stat2, X2], writes=[X2] if nb == 0 else [])
            E("pool", lambda e: e.tensor_tensor(out=stat2.ap[:tp, 40 + i:41 + i], in0=stat2.ap[:tp, 2 * i:2 * i + 1],
                                                in1=stat2.ap[:tp, 2 * i + 1:2 * i + 2], op=ALU.add), reads=[stat2], pwrites=[stat2])
            rstd_pool(stat2.ap[:tp, 50 + i:51 + i], stat2.ap[:tp, 40 + i:41 + i], c1024, 1, tp, [stat2], stat2, ptmp5[0])

        def f_T(i, hf):
            tp, c0 = tinfo(i)
            if hf == 0:
                PS[("xt", i)] = pb(6, 0, 512, BF16)
                PS[("g", i)] = [pb(2, 0, 512), pb(3, 0, 512)]
            pxt = PS[("xt", i)]; psg = PS[("g", i)]
            ks = range(hf * 4, hf * 4 + 4)
            for k in ks:
                E("pe", lambda e, k=k: e.transpose(out=pxt.ap[:, k * 128:k * 128 + tp], in_=X1b.ap[:tp, k * 128:(k + 1) * 128],
                                                   identity=ident.ap[:tp, :tp]), reads=[X1b, ident],
                  writes=[pxt] if k == 0 else [], pwrites=[] if k == 0 else [pxt])
            E("dve", lambda e: e.tensor_copy(out=X1T.ap[:, hf * 512:(hf + 1) * 512], in_=pxt.ap[:, hf * 512:(hf + 1) * 512]),
              reads=[pxt], writes=[X1T] if hf == 0 else [], pwrites=[] if hf == 0 else [X1T])
            for nb in range(2):
                for k in ks:
                    E("pe", lambda e, nb=nb, k=k: e.matmul(
                        psg[nb].ap[:tp, :], lhsT=X1T.ap[:, k * 128:k * 128 + tp], rhs=wgv[:, k, nb * 512:(nb + 1) * 512],
                        start=(k == 0), stop=(k == 7)), reads=[X1T, wg], writes=[psg[nb]] if k == 0 else [], pwrites=[] if k == 0 else [psg[nb]])

        def f_G(i):
            tp, c0 = tinfo(i)
            psg = PS[("g", i)]
            for nb in range(2):
                E("dve", lambda e, nb=nb: e.tensor_tensor(
                    out=Tf.ap[:tp, nb * 512:(nb + 1) * 512], in0=psg[nb].ap[:tp, :], in1=bpg.ap[:tp, nb * 512:(nb + 1) * 512], op=ALU.add),
                  reads=[psg[nb], bpg], writes=[Tf] if nb == 0 else [], pwrites=[] if nb == 0 else [Tf])

        def f_E(i):
            tp, c0 = tinfo(i)
            pse = PS[("e", i)]
            for nb in range(2):
                E("dve", lambda e, nb=nb: e.scalar_tensor_tensor(
                    out=Ef.ap[:tp, nb * 512:(nb + 1) * 512], in0=pse[nb].ap[:tp, :], scalar=stat2.ap[:tp, 50 + i:51 + i],
                    in1=plg.ap[:tp, nb * 512:(nb + 1) * 512], op0=ALU.mult, op1=ALU.mult), reads=[pse[nb], stat2, plg],
                  writes=[Ef] if nb == 0 else [], pwrites=[] if nb == 0 else [Ef])
            E("act", lambda e: e.activation(out=Gs.ap[:tp], in_=Tf.ap[:tp], func=AF.Sigmoid), reads=[Tf], writes=[Gs])

        def f_R(i):
            tp, c0 = tinfo(i)
            X1 = X1s[i % 2]
            E("dve", lambda e: e.tensor_tensor(out=Gs.ap[:tp], in0=Gs.ap[:tp], in1=Ef.ap[:tp], op=ALU.mult), reads=[Gs, Ef], writes=[Gs])
            E("dve", lambda e: e.tensor_tensor(out=X2.ap[:tp], in0=X1.ap[:tp], in1=Gs.ap[:tp], op=ALU.add),
              reads=[X1, Gs], writes=[X2])
            E("act", lambda e: e.activation(out=Ef.ap[:tp], in_=X2.ap[:tp], func=AF.Square,
                                            accum_out=stat2.ap[:tp, 20 + i:21 + i]), reads=[X2], pwrites=[stat2], writes=[Ef])
            rstd_pool(stat2.ap[:tp, 30 + i:31 + i], stat2.ap[:tp, 20 + i:21 + i], c1024, 1, tp, [stat2], stat2, ptmp5[1])

        def f_Rb(i):
            tp, c0 = tinfo(i)
            E("dve", lambda e: e.scalar_tensor_tensor(out=Yf.ap[:tp], in0=X2.ap[:tp], scalar=stat2.ap[:tp, 30 + i:31 + i],
                                                      in1=fng.ap[:tp], op0=ALU.mult, op1=ALU.mult),
              reads=[X2, stat2, fng], writes=[Yf])
            dst = y_p[t0 + i * 128:t0 + (i + 1) * 128, :] if i < 8 else y_s
            out_dmas.append(DMA("sp", dst, Yf.ap[:tp], reads=[Yf], key=rkey("yo", i, 2)))

        f_L(0)
        if nt > 1:
            f_L(1)
        for t in range(nt):
            f_O(t)
            f_X(t, 0)
            f_T(t, 0)
            f_X(t, 1)
            f_T(t, 1)
            if t >= 2:
                f_Rb(t - 2)
            if t >= 1:
                f_E(t - 1)
            f_P(t)
            if t + 2 < nt:
                f_L(t + 2)
            f_G(t)
            if t >= 1:
                f_R(t - 1)
        if nt >= 2:
            f_Rb(nt - 2)
        f_E(nt - 1)
        f_R(nt - 1)
        f_Rb(nt - 1)

    if limit is not None:
        while True:
            pe_kept = [o for o in P.ops["pe"] if o.seq <= limit]
            if pe_kept and pe_kept[-1].open_group:
                limit += 1
            else:
                break
        for e_ in P.ENGS:
            P.ops[e_] = [o for o in P.ops[e_] if o.seq <= limit]
        out_dmas = [o for o in out_dmas if o.seq <= limit]
        for e_ in ("pe", "act", "dve", "pool"):
            if P.ops[e_]:
                out_dmas.append(P.ops[e_][-1])
        print("limit", limit, "of", Op._seq[0], {e_: len(P.ops[e_]) for e_ in P.ENGS})
    P.add("sp", lambda e: e.nop(), deps=out_dmas)
    P.emit()
    es.close()
    return nc


_NC_CACHE = {}


def kernel(**inp):
    f = lambda a: np.ascontiguousarray(np.asarray(a, dtype=np.float32))
    if "nc" not in _NC_CACHE:
        _NC_CACHE["nc"] = build()
    nc = _NC_CACHE["nc"]
    x_prompt = f(inp["x_prompt"]); x_sample = f(inp["x_sample"]).reshape(128, 1024)
    C = f(inp["state_mlstm_C"])[0]; n = f(inp["state_mlstm_n"])[0].reshape(128, 1024)
    m = f(inp["state_mlstm_m"])[0]; cv = f(inp["state_conv"])[0]
    p_prompt = f(inp["p_prompt"])[0]; p_sample = f(inp["p_sample"])[0].reshape(128, 256)
    shared = {
        "norm_g": f(inp["norm_in_g"])[0], "w_in": f(inp["w_in"])[0],
        "lnv_g": f(inp["ln_v_g"])[0].reshape(1024), "lnv_b": f(inp["ln_v_b"])[0].reshape(1024),
        "w_sp": f(inp["w_spatial"])[0], "b_sp": f(inp["b_spatial"])[0],
        "conv_w": f(inp["conv_w"])[0], "conv_b": f(inp["conv_b"])[0],
        "w_q": f(inp["w_q"])[0], "w_k": f(inp["w_k"])[0], "w_v": f(inp["w_v"])[0],
        "w_if": f(inp["w_if"])[0], "b_if": f(inp["b_if"])[0],
        "gn_g": f(inp["gn_g"])[0].reshape(1024), "skip": f(inp["skip"])[0],
        "w_out": f(inp["w_out"])[0], "w_pg": f(inp["w_ple_gate"])[0], "b_pg": f(inp["b_ple_gate"])[0],
        "w_pp": f(inp["w_ple_proj"])[0], "ple_g": f(inp["ple_norm_g"])[0], "fin_g": f(inp["final_norm_g"]),
    }
    in_maps = []
    for c in range(8):
        s = slice(c * 16, (c + 1) * 16)
        d = dict(shared)
        d.update({"xp": x_prompt[c], "xs": x_sample[s], "pp": p_prompt[c], "psm": p_sample[s],
                  "C_in": C[s], "n_in": n[s], "m_in": m[s], "cv_in": cv[s]})
        in_maps.append(d)
    res = run_bass_kernel_spmd(nc, in_maps, core_ids=list(range(8)))
    R = res.results
    cat = lambda k: np.concatenate([np.asarray(r[k]) for r in R], axis=0)
    stk = lambda k: np.stack([np.asarray(r[k]) for r in R], axis=0)
    y_prompt = stk("y_p")
    y_sample = cat("y_s").reshape(128, 1, 1024)
    Cp_ = stk("Cp")[None]
    np__ = stk("np_o")[None]
    mp_ = stk("mp_o").reshape(8, 4)[None]
    cvp_ = stk("cvp")[None]
    Cs_ = cat("Cs")[None]
    ns_ = cat("ns_o").reshape(128, 4, 256)[None]
    ms_ = cat("ms_o")[None]
    cvs_ = cat("cvs")[None]
    vs_ = cat("vs_o").reshape(128, 1, 1024)[None]
    return (y_prompt.astype(np.float32), y_sample.astype(np.float32), Cp_.astype(np.float32), np__.astype(np.float32),
            mp_.astype(np.float32), cvp_.astype(np.float32), Cs_.astype(np.float32), ns_.astype(np.float32),
            ms_.astype(np.float32), cvs_.astype(np.float32), vs_.astype(np.float32))
```
